# Optimizing a Trainium2 kernel written in Bass

```python
import math
import jax, jax.numpy as jnp
from jax import lax
import numpy as np

D_MODEL = 1024
BATCH = 8
SEQ = 4096
DEPTH = 2

GRID_W = 64
CTX_LEN = 256
HEAD_DIM = 64
D_FF = 2816
N_MOD = 9
NORM_EPS = 1e-6
ROPE_THETA = 10000.0
Q_BLOCK = 128

HY_CH = 256
HY_EMB = 33
HY_ORDER = 64
HY_FAST_PCT = 0.3
HY_SLOW_PCT = 1.5
HY_TARGET = 1e-2

GQA_HEADS = 4
GQA_KV_HEADS = 2
GQA_GROUP = GQA_HEADS // GQA_KV_HEADS

MLA_HEADS = 4
MLA_NOPE = 64
MLA_ROPE = 32
MLA_QK = MLA_NOPE + MLA_ROPE
MLA_V = 64
MLA_Q_RANK = 256
MLA_KV_RANK = 128

RW_HEADS = 4
RW_N = 64
RW_C = RW_HEADS * RW_N
RW_DECAY_LORA = 64
RW_AAA_LORA = 64
RW_GATE_LORA = 128
RW_GN_EPS = 64e-5

HY_COLS = 3 * HY_CH
GQA_COLS = (GQA_HEADS + 2 * GQA_KV_HEADS) * HEAD_DIM
MLA_COLS = MLA_Q_RANK + MLA_KV_RANK + MLA_ROPE
RW_COLS = 3 * RW_C + 2 * RW_DECAY_LORA + 2 * RW_AAA_LORA + RW_GATE_LORA
D_IN = HY_COLS + GQA_COLS + MLA_COLS + RW_COLS
D_MIX = HY_CH + GQA_HEADS * HEAD_DIM + MLA_HEADS * MLA_V + RW_C
IN_SPLITS = [HY_COLS, HY_COLS + GQA_COLS, HY_COLS + GQA_COLS + MLA_COLS]
RW_SPLITS = [RW_C, 2 * RW_C, 3 * RW_C, 3 * RW_C + RW_DECAY_LORA, 3 * RW_C + 2 * RW_DECAY_LORA,
             3 * RW_C + 2 * RW_DECAY_LORA + RW_AAA_LORA, 3 * RW_C + 2 * RW_DECAY_LORA + 2 * RW_AAA_LORA]

kernel_name = 'hybrid_head_group_flow_block'


def rms_norm(x, g):
    xf = x.astype(jnp.float32)
    y = xf * lax.rsqrt(jnp.mean(xf * xf, axis=-1, keepdims=True) + NORM_EPS)
    return (y * g.astype(jnp.float32)).astype(x.dtype)


def adaln(x, g, mod, i):
    return rms_norm(x, g) * (1.0 + mod[:, 3 * i + 1][:, None]) + mod[:, 3 * i][:, None]


def res_gate(mod, i):
    return mod[:, 3 * i + 2][:, None]


def swiglu(h, w_gate, w_up, w_down):
    return (jax.nn.silu(h @ w_gate) * (h @ w_up)) @ w_down


def centred_conv3(u, w, b):
    up = jnp.pad(u, ((0, 0), (1, 1), (0, 0)))
    return up[:, :-2] * w[0] + up[:, 1:-1] * w[1] + up[:, 2:] * w[2] + b


def token_shift(u, mu):
    up = jnp.pad(u, ((0, 0), (1, 1), (0, 0)))
    return u + mu * (0.5 * (up[:, :-2] + up[:, 2:]) - u)


def rope_tables(rows, d_rot):
    row = jnp.repeat(jnp.arange(rows, dtype=jnp.float32), GRID_W)
    col = jnp.tile(jnp.arange(GRID_W, dtype=jnp.float32), rows)
    n_freq = d_rot // 4
    inv = ROPE_THETA ** (-jnp.arange(n_freq, dtype=jnp.float32) / n_freq)
    ang = jnp.concatenate([row[:, None] * inv, col[:, None] * inv], axis=-1)
    return jnp.cos(ang), jnp.sin(ang)


def apply_rope(x, cos, sin):
    shp = (1, x.shape[1]) + (1,) * (x.ndim - 3) + (cos.shape[-1],)
    cos = cos.reshape(shp)
    sin = sin.reshape(shp)
    xf = x.astype(jnp.float32).reshape(x.shape[:-1] + (-1, 2))
    x0, x1 = xf[..., 0], xf[..., 1]
    out = jnp.stack([x0 * cos - x1 * sin, x0 * sin + x1 * cos], axis=-1)
    return out.reshape(x.shape).astype(x.dtype)


def attention(q, k, v, scale):
    s = jnp.einsum('bqhgd,bkhd->bhgqk', q, k, preferred_element_type=jnp.float32) * scale
    p = jax.nn.softmax(s, axis=-1)
    return jnp.einsum('bhgqk,bkhd->bqhgd', p.astype(v.dtype), v)


def blocked_attention(q, k, v, scale):
    B, L = q.shape[:2]
    nb = L // Q_BLOCK
    qb = jnp.moveaxis(q.reshape((B, nb, Q_BLOCK) + q.shape[2:]), 1, 0)
    ob = lax.map(lambda qi: attention(qi, k, v, scale), qb)
    return jnp.moveaxis(ob, 0, 1).reshape((B, L) + ob.shape[3:])


def hyena_filters(L, w1, b1, w2, b2, w3, b3, w4, freq):
    t01 = jnp.linspace(0.0, 1.0, L, dtype=jnp.float32)[:, None]
    bands = (HY_EMB - 1) // 2
    w_ang = 2.0 * math.pi * jnp.arange(L, dtype=jnp.float32)[:, None] / L
    f = jnp.linspace(1e-4, bands - 1, bands, dtype=jnp.float32)[None]
    z = jnp.concatenate([t01, jnp.cos(f * w_ang), -jnp.sin(f * w_ang)], axis=-1)
    h = jnp.sin(freq * (z @ w1 + b1))
    h = jnp.sin(freq * (h @ w2 + b2))
    h = jnp.sin(freq * (h @ w3 + b3))
    h = (h @ w4).astype(jnp.float32)
    max_decay = math.log(HY_TARGET) / HY_FAST_PCT
    min_decay = math.log(HY_TARGET) / HY_SLOW_PCT
    deltas = jnp.tile(jnp.abs(jnp.linspace(min_decay, max_decay, HY_CH, dtype=jnp.float32)), 2)
    return h * jnp.exp(-t01 * deltas)


def fft_long_conv(u, h_fwd, h_bwd):
    L = u.shape[1]
    n = 2 * L
    hf = jnp.fft.rfft(h_fwd, n=n, axis=0)
    hb = jnp.fft.rfft(h_bwd, n=n, axis=0)
    y_f = jnp.fft.irfft(jnp.fft.rfft(u, n=n, axis=1) * hf, n=n, axis=1)[:, :L]
    y_b = jnp.fft.irfft(jnp.fft.rfft(u[:, ::-1], n=n, axis=1) * hb, n=n, axis=1)[:, :L][:, ::-1]
    return y_f + y_b


def hyena_mixer(p, conv_w, conv_b, f_w1, f_b1, f_w2, f_b2, f_w3, f_b3, f_w4, f_freq, bias):
    L = p.shape[1]
    x1, x2, v = jnp.split(centred_conv3(p, conv_w, conv_b), [HY_CH, 2 * HY_CH], axis=-1)
    h = hyena_filters(L, f_w1, f_b1, f_w2, f_b2, f_w3, f_b3, f_w4, f_freq)
    u = (x1 * v).astype(jnp.float32)
    y = fft_long_conv(u, h[:, :HY_CH], h[:, HY_CH:]) + u * bias
    return (x2 * y).astype(p.dtype)


def gqa_mixer(p_lat, p_ctx, q_norm, k_norm, cos, sin, ctx_out):
    def heads(p):
        B, L = p.shape[:2]
        q, k, v = jnp.split(p, [GQA_HEADS * HEAD_DIM, (GQA_HEADS + GQA_KV_HEADS) * HEAD_DIM], axis=-1)
        q = rms_norm(q.reshape(B, L, GQA_KV_HEADS, GQA_GROUP, HEAD_DIM), q_norm)
        k = rms_norm(k.reshape(B, L, GQA_KV_HEADS, HEAD_DIM), k_norm)
        return q, k, v.reshape(B, L, GQA_KV_HEADS, HEAD_DIM)
    B, L = p_lat.shape[:2]
    q_l, k_l, v_l = heads(p_lat)
    q_c, k_c, v_c = heads(p_ctx)
    q_l = apply_rope(q_l, cos, sin)
    k_l = apply_rope(k_l, cos, sin)
    scale = HEAD_DIM ** -0.5
    y_lat = blocked_attention(q_l, jnp.concatenate([k_c, k_l], axis=1),
                              jnp.concatenate([v_c, v_l], axis=1), scale).reshape(B, L, -1)
    y_ctx = None
    if ctx_out:
        y_ctx = attention(q_c, k_c, v_c, scale).reshape(B, p_ctx.shape[1], -1)
    return y_lat, y_ctx


def mla_mixer(p_lat, p_ctx, cq_norm, ckv_norm, w_uq, w_ukv, q_norm, k_norm, cos, sin, ctx_out):
    def heads(p):
        B, L = p.shape[:2]
        c_q, c_kv, k_rope = jnp.split(p, [MLA_Q_RANK, MLA_Q_RANK + MLA_KV_RANK], axis=-1)
        q = (rms_norm(c_q, cq_norm) @ w_uq).reshape(B, L, MLA_HEADS, 1, MLA_QK)
        kv = (rms_norm(c_kv, ckv_norm) @ w_ukv).reshape(B, L, MLA_HEADS, MLA_NOPE + MLA_V)
        k_nope, v = jnp.split(kv, [MLA_NOPE], axis=-1)
        k_rope = jnp.broadcast_to(k_rope[:, :, None, :], (B, L, MLA_HEADS, MLA_ROPE))
        k = jnp.concatenate([k_nope, k_rope], axis=-1)
        return rms_norm(q, q_norm), rms_norm(k, k_norm), v
    def rot(t):
        return jnp.concatenate([t[..., :MLA_NOPE], apply_rope(t[..., MLA_NOPE:], cos, sin)], axis=-1)
    B, L = p_lat.shape[:2]
    q_l, k_l, v_l = heads(p_lat)
    q_c, k_c, v_c = heads(p_ctx)
    q_l = rot(q_l)
    k_l = rot(k_l)
    scale = MLA_QK ** -0.5
    y_lat = blocked_attention(q_l, jnp.concatenate([k_c, k_l], axis=1),
                              jnp.concatenate([v_c, v_l], axis=1), scale).reshape(B, L, -1)
    y_ctx = None
    if ctx_out:
        y_ctx = attention(q_c, k_c, v_c, scale).reshape(B, p_ctx.shape[1], -1)
    return y_lat, y_ctx


def rwkv_prepare(p, mu, w0, w2, a0, a2, g2, k_k, k_a):
    B, L = p.shape[:2]
    r, k, v, xw_f, xw_b, xa_f, xa_b, xg = jnp.split(token_shift(p, mu), RW_SPLITS, axis=-1)
    def heads(t):
        return t.reshape(B, L, RW_HEADS, RW_N)
    kk = heads(k * k_k).astype(jnp.float32)
    kk = kk / jnp.maximum(jnp.sqrt(jnp.sum(kk * kk, axis=-1, keepdims=True)), 1e-12)
    dirs = []
    for d, (xw, xa) in enumerate(((xw_f, xa_f), (xw_b, xa_b))):
        w_log = -jax.nn.softplus(-(w0[d] + jnp.tanh(xw) @ w2[d])) - 0.5
        decay = jnp.exp(-jnp.exp(w_log.astype(jnp.float32)))
        a = jax.nn.sigmoid(a0[d] + xa @ a2[d])
        k_d = k * (1.0 + (a - 1.0) * k_a)
        dirs.append((heads(decay), heads(k_d), heads(a)))
    g = jax.nn.sigmoid(xg) @ g2
    return heads(r), heads(v), kk, dirs, g


def rwkv_scan(r, w, k, v, kk, a, s0, reverse):
    def step(S, inp):
        r_t, w_t, k_t, v_t, kk_t, a_t = inp
        sa = jnp.einsum('bhvk,bhk->bhv', S, -kk_t)
        S = S * w_t[:, :, None, :] + sa[..., None] * (kk_t * a_t)[:, :, None, :] + v_t[..., None] * k_t[:, :, None, :]
        return S, jnp.einsum('bhvk,bhk->bhv', S, r_t)
    xs = tuple(jnp.moveaxis(t.astype(jnp.float32), 1, 0) for t in (r, w, k, v, kk, a))
    s_fin, ys = lax.scan(step, s0, xs, reverse=reverse)
    return jnp.moveaxis(ys, 0, 1), s_fin


def rwkv_directions(prep, s_init):
    r, v, kk, dirs, g = prep
    ys, finals = [], []
    for d, (decay, k_d, a) in enumerate(dirs):
        y_d, s_d = rwkv_scan(r, decay, k_d, v, kk, a, s_init[d], reverse=(d == 1))
        ys.append(y_d)
        finals.append(s_d)
    return ys, finals


def rwkv_readout(prep, ys, r_k, ln_w, ln_b, dtype):
    r, v, kk, dirs, g = prep
    B, L = r.shape[:2]
    y = ys[0] + ys[1]
    mean = jnp.mean(y, axis=-1, keepdims=True)
    var = jnp.mean(jnp.square(y - mean), axis=-1, keepdims=True)
    y = ((y - mean) * lax.rsqrt(var + RW_GN_EPS)).reshape(B, L, RW_C) * ln_w + ln_b
    bonus = sum(jnp.sum(r * k_d * r_k, axis=-1, keepdims=True) * v for (_, k_d, _) in dirs)
    return ((y + bonus.reshape(B, L, RW_C)) * g).astype(dtype)


def rwkv_mixer(p_lat, p_ctx, mu, w0, w2, a0, a2, g2, k_k, k_a, r_k, ln_w, ln_b, ctx_out):
    prep_c = rwkv_prepare(p_ctx, mu, w0, w2, a0, a2, g2, k_k, k_a)
    prep_l = rwkv_prepare(p_lat, mu, w0, w2, a0, a2, g2, k_k, k_a)
    zeros = jnp.zeros((p_ctx.shape[0], RW_HEADS, RW_N, RW_N), jnp.float32)
    ys_c, s_ctx = rwkv_directions(prep_c, (zeros, zeros))
    ys_l, _ = rwkv_directions(prep_l, s_ctx)
    y_lat = rwkv_readout(prep_l, ys_l, r_k, ln_w, ln_b, p_lat.dtype)
    y_ctx = rwkv_readout(prep_c, ys_c, r_k, ln_w, ln_b, p_ctx.dtype) if ctx_out else None
    return y_lat, y_ctx


def setup_inputs(seed: int = 0) -> dict:
    key = jax.random.key(seed)
    ks = iter(jax.random.split(key, 64))
    def nrm(shape, scale):
        return jax.random.normal(next(ks), shape, jnp.float32) * scale
    def gain(shape):
        return 1.0 + nrm(shape, 0.02)
    L = DEPTH
    return {
        'x': nrm((BATCH, SEQ, D_MODEL), 1.0),
        'c': nrm((BATCH, D_MODEL), 1.0),
        'ctx': nrm((BATCH, CTX_LEN, D_MODEL), 1.0),
        'c_ctx': nrm((D_MODEL,), 1.0),
        'ada_w': nrm((L, D_MODEL, N_MOD * D_MODEL), 0.5 * D_MODEL ** -0.5),
        'ada_b': nrm((L, N_MOD * D_MODEL), 0.02),
        'norm_ffn1': gain((L, D_MODEL)),
        'norm_mix': gain((L, D_MODEL)),
        'norm_ffn2': gain((L, D_MODEL)),
        'ffn1_gate': nrm((L, D_MODEL, D_FF), D_MODEL ** -0.5),
        'ffn1_up': nrm((L, D_MODEL, D_FF), D_MODEL ** -0.5),
        'ffn1_down': nrm((L, D_FF, D_MODEL), D_FF ** -0.5),
        'ffn2_gate': nrm((L, D_MODEL, D_FF), D_MODEL ** -0.5),
        'ffn2_up': nrm((L, D_MODEL, D_FF), D_MODEL ** -0.5),
        'ffn2_down': nrm((L, D_FF, D_MODEL), D_FF ** -0.5),
        'w_in': nrm((L, D_MODEL, D_IN), D_MODEL ** -0.5),
        'w_out': nrm((L, D_MIX, D_MODEL), D_MIX ** -0.5),
        'hy_conv_w': nrm((L, 3, HY_COLS), 3 ** -0.5),
        'hy_conv_b': nrm((L, HY_COLS), 0.02),
        'hy_f_w1': nrm((L, HY_EMB, HY_ORDER), HY_EMB ** -0.5),
        'hy_f_b1': nrm((L, HY_ORDER), 0.02),
        'hy_f_w2': nrm((L, HY_ORDER, HY_ORDER), HY_ORDER ** -0.5),
        'hy_f_b2': nrm((L, HY_ORDER), 0.02),
        'hy_f_w3': nrm((L, HY_ORDER, HY_ORDER), HY_ORDER ** -0.5),
        'hy_f_b3': nrm((L, HY_ORDER), 0.02),
        'hy_f_w4': nrm((L, HY_ORDER, 2 * HY_CH), 0.02),
        'hy_f_freq': gain((L, HY_ORDER)),
        'hy_bias': nrm((L, HY_CH), 0.5),
        'gqa_q_norm': gain((L, HEAD_DIM)),
        'gqa_k_norm': gain((L, HEAD_DIM)),
        'mla_cq_norm': gain((L, MLA_Q_RANK)),
        'mla_ckv_norm': gain((L, MLA_KV_RANK)),
        'mla_w_uq': nrm((L, MLA_Q_RANK, MLA_HEADS * MLA_QK), MLA_Q_RANK ** -0.5),
        'mla_w_ukv': nrm((L, MLA_KV_RANK, MLA_HEADS * (MLA_NOPE + MLA_V)), MLA_KV_RANK ** -0.5),
        'mla_q_norm': gain((L, MLA_QK)),
        'mla_k_norm': gain((L, MLA_QK)),
        'rw_mu': jax.random.uniform(next(ks), (L, RW_COLS), jnp.float32),
        'rw_w0': jnp.broadcast_to(jnp.linspace(-6.0, -1.0, RW_C, dtype=jnp.float32), (L, 2, RW_C)) + nrm((L, 2, RW_C), 0.1),
        'rw_w2': nrm((L, 2, RW_DECAY_LORA, RW_C), 0.1 * RW_DECAY_LORA ** -0.5),
        'rw_a0': nrm((L, 2, RW_C), 0.1),
        'rw_a2': nrm((L, 2, RW_AAA_LORA, RW_C), RW_AAA_LORA ** -0.5),
        'rw_g2': nrm((L, RW_GATE_LORA, RW_C), RW_GATE_LORA ** -0.5),
        'rw_k_k': 0.85 + nrm((L, RW_C), 0.02),
        'rw_k_a': gain((L, RW_C)),
        'rw_r_k': nrm((L, RW_HEADS, RW_N), 0.1),
        'rw_ln_w': gain((L, RW_C)),
        'rw_ln_b': nrm((L, RW_C), 0.02),
    }


def reference(x, c, ctx, c_ctx, ada_w, ada_b, norm_ffn1, norm_mix, norm_ffn2,
              ffn1_gate, ffn1_up, ffn1_down, ffn2_gate, ffn2_up, ffn2_down, w_in, w_out,
              hy_conv_w, hy_conv_b, hy_f_w1, hy_f_b1, hy_f_w2, hy_f_b2, hy_f_w3, hy_f_b3, hy_f_w4,
              hy_f_freq, hy_bias, gqa_q_norm, gqa_k_norm, mla_cq_norm, mla_ckv_norm, mla_w_uq, mla_w_ukv,
              mla_q_norm, mla_k_norm, rw_mu, rw_w0, rw_w2, rw_a0, rw_a2, rw_g2, rw_k_k, rw_k_a, rw_r_k,
              rw_ln_w, rw_ln_b):
    B, L, D = x.shape
    rows = L // GRID_W
    cos_g, sin_g = rope_tables(rows, HEAD_DIM)
    cos_m, sin_m = rope_tables(rows, MLA_ROPE)
    silu_c = jax.nn.silu(c)
    silu_cc = jax.nn.silu(c_ctx)[None]
    for l in range(DEPTH):
        ctx_out = l < DEPTH - 1
        mod_x = (silu_c @ ada_w[l] + ada_b[l]).reshape(B, N_MOD, D)
        mod_c = (silu_cc @ ada_w[l] + ada_b[l]).reshape(1, N_MOD, D)

        x = x + 0.5 * res_gate(mod_x, 0) * swiglu(adaln(x, norm_ffn1[l], mod_x, 0), ffn1_gate[l], ffn1_up[l], ffn1_down[l])
        ctx = ctx + 0.5 * res_gate(mod_c, 0) * swiglu(adaln(ctx, norm_ffn1[l], mod_c, 0), ffn1_gate[l], ffn1_up[l], ffn1_down[l])

        p_x = adaln(x, norm_mix[l], mod_x, 1) @ w_in[l]
        p_c = adaln(ctx, norm_mix[l], mod_c, 1) @ w_in[l]
        hy_x, gq_x, ml_x, rw_x = jnp.split(p_x, IN_SPLITS, axis=-1)
        hy_c, gq_c, ml_c, rw_c = jnp.split(p_c, IN_SPLITS, axis=-1)
        hy_par = (hy_conv_w[l], hy_conv_b[l], hy_f_w1[l], hy_f_b1[l], hy_f_w2[l], hy_f_b2[l],
                  hy_f_w3[l], hy_f_b3[l], hy_f_w4[l], hy_f_freq[l], hy_bias[l])
        y_hy_x = hyena_mixer(hy_x, *hy_par)
        y_gq_x, y_gq_c = gqa_mixer(gq_x, gq_c, gqa_q_norm[l], gqa_k_norm[l], cos_g, sin_g, ctx_out)
        y_ml_x, y_ml_c = mla_mixer(ml_x, ml_c, mla_cq_norm[l], mla_ckv_norm[l], mla_w_uq[l], mla_w_ukv[l],
                                   mla_q_norm[l], mla_k_norm[l], cos_m, sin_m, ctx_out)
        y_rw_x, y_rw_c = rwkv_mixer(rw_x, rw_c, rw_mu[l], rw_w0[l], rw_w2[l], rw_a0[l], rw_a2[l], rw_g2[l],
                                    rw_k_k[l], rw_k_a[l], rw_r_k[l], rw_ln_w[l], rw_ln_b[l], ctx_out)
        y_x = jnp.concatenate([y_hy_x, y_gq_x, y_ml_x, y_rw_x], axis=-1) @ w_out[l]
        x = x + res_gate(mod_x, 1) * y_x
        if ctx_out:
            y_hy_c = hyena_mixer(hy_c, *hy_par)
            y_c = jnp.concatenate([y_hy_c, y_gq_c, y_ml_c, y_rw_c], axis=-1) @ w_out[l]
            ctx = ctx + res_gate(mod_c, 1) * y_c

        x = x + 0.5 * res_gate(mod_x, 2) * swiglu(adaln(x, norm_ffn2[l], mod_x, 2), ffn2_gate[l], ffn2_up[l], ffn2_down[l])
        if ctx_out:
            ctx = ctx + 0.5 * res_gate(mod_c, 2) * swiglu(adaln(ctx, norm_ffn2[l], mod_c, 2), ffn2_gate[l], ffn2_up[l], ffn2_down[l])
    return x
```

```python
import contextlib
import math
import numpy as np
import ml_dtypes
import concourse.bass as bass
import concourse.mybir as mybir
from concourse.bass_utils import run_bass_kernel_spmd

F32 = mybir.dt.float32
BF16 = mybir.dt.bfloat16
ALU = mybir.AluOpType
AF = mybir.ActivationFunctionType
AX = mybir.AxisListType

D = 1024
KC = 8
SEQ = 4096
CTX = 256
T = SEQ + CTX
DEPTH = 2
D_FF = 2816
NF = D_FF // 128
N_MOD = 9
EPS = 1e-6
HY_CH = 256
HY_COLS = 768
GQA_COLS = 512
MLA_COLS = 416
RW_COLS = 1152
D_IN = 2848
NT = 256
TILES = [(t0, NT) for t0 in range(0, T, NT)]

COMPUTE = ("pe", "dve", "act", "pool")
QUEUES = ("sp", "act", "pool")
RING = 16
import os
STQ_RWA = os.environ.get('STQ_RWA', 'pool')
CUT = int(os.environ.get('RWA_CUT', '99'))
RW_BF16 = bool(int(os.environ.get('RW_BF16', '0')))
RW_DBL_BF16 = bool(int(os.environ.get('RW_DBL_BF16', '0')))
DEBUG = False


class Buf:
    __slots__ = ("name", "w", "r")

    def __init__(self, name=""):
        self.name = name
        self.w = None
        self.r = []


class Prog:
    def __init__(self, same_engine_sync=True):
        self.nc = bass.Bass("TRN2", target_bir_lowering=False)
        self.stack = contextlib.ExitStack()
        self.ops = {e: [] for e in ("pe", "dve", "act", "pool", "sp")}
        self.cnt = {e: 0 for e in COMPUTE}
        self.dma_i = {q: 0 for q in QUEUES}
        self.waited = {e: {} for e in self.ops}
        self.same_engine_sync = same_engine_sync
        self.semkeys = set()
        self.AW = 50688
        self.arena = self.stack.enter_context(self.nc.sbuf_tensor("arena", [128, self.AW], F32))
        self.arena_bf = self.arena.bitcast(BF16)
        self.off = 0
        self.marks = []
        self.phase_log = []
        self.banks = [self.stack.enter_context(self.nc.psum_tensor(f"bank{i}", [128, 512], F32)) for i in range(8)]
        self.bank_bufs = [Buf(f"bank{i}") for i in range(8)]

    def alloc(self, shape, dtype=F32):
        p = shape[0]
        n = int(np.prod(shape[1:]))
        words = n if dtype == F32 else (n + 1) // 2
        words = (words + 7) // 8 * 8
        off = self.off
        self.off += words
        assert self.off <= self.AW, f"SBUF arena overflow {self.off}"
        if dtype == F32:
            ap = self.arena[0:p, off:off + n]
        else:
            ap = self.arena_bf[0:p, 2 * off:2 * off + n]
        if len(shape) == 3:
            ap = ap.rearrange("p (a b) -> p a b", a=shape[1])
        elif len(shape) == 4:
            ap = ap.rearrange("p (a b c) -> p a b c", a=shape[1], b=shape[2])
        elif len(shape) == 5:
            ap = ap.rearrange("p (a b c d) -> p a b c d", a=shape[1], b=shape[2], c=shape[3])
        return ap

    def push(self):
        self.marks.append(self.off)

    def pop(self, label=None):
        self.barrier()
        self.off = self.marks.pop()
        if label:
            self.phase_log.append((label, {e: sum(1 for it in self.ops[e] if it[0] == "op" and it[2] == e) for e in COMPUTE}))

    def dram(self, name, shape, dtype=F32, kind="Internal"):
        return self.nc.dram_tensor(name, list(shape), dtype, kind=kind)

    def _need(self, eng, tok):
        if tok is None:
            return
        key, val = tok
        if key == eng and not self.same_engine_sync:
            return
        cur = self.waited[eng].get(key, 0)
        if cur >= val:
            return
        self.waited[eng][key] = val
        self.ops[eng].append(("wait", key, val))

    def _deps(self, eng, reads, writes, pe_accum=False, is_dma=False):
        for b in reads:
            self._need(eng, b.w)
        for b in writes:
            if b.w is not None and (is_dma or b.w[0] != eng) and not (pe_accum and b.w[0] == "pe"):
                self._need(eng, b.w)
            for t in b.r:
                if is_dma or t[0] != eng:
                    self._need(eng, t)

    def _mark(self, tok, reads, writes):
        for b in reads:
            b.r.append(tok)
            if len(b.r) > 16:
                best = {}
                for k, v in b.r:
                    if best.get(k, 0) < v:
                        best[k] = v
                b.r = list(best.items())
        for b in writes:
            b.w = tok
            b.r = []

    def op(self, eng, meth, *args, reads=(), writes=(), **kw):
        self._deps(eng, reads, writes)
        self.cnt[eng] += 1
        tok = (eng, self.cnt[eng])
        self.semkeys.add(eng)
        self.ops[eng].append(("op", (meth, args, kw), eng, 1))
        self._mark(tok, reads, writes)
        return tok

    def mm(self, calls, reads=(), writes=(), meth="matmul"):
        self._deps("pe", reads, writes, pe_accum=True)
        for (a, k) in calls[:-1]:
            self.ops["pe"].append(("op", (meth, a, k), None, 0))
        self.cnt["pe"] += 1
        tok = ("pe", self.cnt["pe"])
        self.semkeys.add("pe")
        a, k = calls[-1]
        self.ops["pe"].append(("op", (meth, a, k), "pe", 1))
        self._mark(tok, reads, writes)
        return tok

    def dma(self, q, out, in_, reads=(), writes=(), **kw):
        eng = q
        self._deps(eng, reads, writes, is_dma=True)
        i = self.dma_i[q]
        self.dma_i[q] += 1
        key = ("dma", q, i % RING)
        val = 16 * (i // RING + 1)
        if i >= RING:
            self._need(eng, (key, val - 16))
        self.semkeys.add(key)
        self.ops[eng].append(("op", ("dma_start", (out, in_), kw), key, 16))
        tok = (key, val)
        self._mark(tok, reads, writes)
        return tok

    def barrier(self):
        toks = [(e, self.cnt[e]) for e in COMPUTE if self.cnt[e] > 0]
        for q in QUEUES:
            n = self.dma_i[q]
            for j in range(max(0, n - RING), n):
                toks.append((("dma", q, j % RING), 16 * (j // RING + 1)))
        for e in self.ops:
            for t in toks:
                self._need(e, t)

    def finish(self):
        self.barrier()
        nc = self.nc
        sems = {}
        for key in sorted(self.semkeys, key=str):
            nm = key if isinstance(key, str) else f"d_{key[1]}_{key[2]}"
            sems[key] = self.stack.enter_context(nc.semaphore("s_" + nm))
        ops = self.ops

        def emit(e, lst):
            for it in lst:
                if it[0] == "wait":
                    e.wait_ge(sems[it[1]], it[2])
                else:
                    meth, a, k = it[1]
                    ins = getattr(e, meth)(*a, **k)
                    if it[2] is not None:
                        ins.then_inc(sems[it[2]], it[3])

        with nc.Block() as block:
            @block.sync
            def _(e):
                emit(e, ops["sp"])

            @block.tensor
            def _(e):
                emit(e, ops["pe"])

            @block.vector
            def _(e):
                emit(e, ops["dve"])

            @block.scalar
            def _(e):
                emit(e, ops["act"])

            @block.gpsimd
            def _(e):
                emit(e, ops["pool"])
        self.stack.close()
        return nc

    def stats(self):
        return ({e: sum(1 for it in l if it[0] == "op") for e, l in self.ops.items()},
                {e: sum(1 for it in l if it[0] == "wait") for e, l in self.ops.items()})


class Rot:
    def __init__(self, items):
        self.items = items
        self.i = 0

    def next(self):
        it = self.items[self.i % len(self.items)]
        self.i += 1
        return it


class K:
    def __init__(self, dbg=False, pT_in=False):
        self.P = P = Prog()
        self.dbg = dbg
        kin = "ExternalInput"
        sk = "ExternalOutput" if dbg else "Internal"
        self.xT_in = P.dram("xT", [D, T], F32, kin)
        self.cvec = P.dram("cvec", [128, KC, 2], F32, kin)
        self.adab = P.dram("adab", [DEPTH, 128, 72], F32, kin)
        self.norms = P.dram("norms", [DEPTH, 3, 128, KC], F32, kin)
        self.ada_w = P.dram("ada_w", [DEPTH, D, N_MOD * D], F32, kin)
        self.w_ffn = {}
        for nm in ("ffn1_gate", "ffn1_up", "ffn2_gate", "ffn2_up"):
            self.w_ffn[nm] = P.dram(nm, [DEPTH, D, D_FF], F32, kin)
        for nm in ("ffn1_down", "ffn2_down"):
            self.w_ffn[nm] = P.dram(nm, [DEPTH, D_FF, D], F32, kin)
        self.w_in = P.dram("w_in", [DEPTH, D, D_IN], F32, kin)
        self.w_out = P.dram("w_out", [DEPTH, D, D], F32, kin)
        self.rope_g = P.dram("rope_g", [2, 128, T], F32, kin)
        self.rope_m = P.dram("rope_m", [2, 96, T], F32, kin)
        self.c_bd64 = P.dram("c_bd64", [128, 128], F32, kin)
        self.c_rotm = P.dram("c_rotm", [128, 128], F32, kin)
        self.c_rot96 = P.dram("c_rot96", [96, 96], F32, kin)
        self.gqa_gain = P.dram("gqa_gain", [DEPTH, 128, 2], F32, kin)
        self.mla_gain = P.dram("mla_gain", [DEPTH, 128, 5], F32, kin)
        self.mla_w_uq = P.dram("mla_w_uq", [DEPTH, 256, 384], F32, kin)
        self.mla_w_ukv = P.dram("mla_w_ukv", [DEPTH, 128, 512], F32, kin)
        self.c_ident = P.dram("c_ident", [128, 128], F32, kin)
        self.hy_cw = P.dram("hy_cw", [DEPTH, 128, 6, 4], F32, kin)
        self.hy_biasT = P.dram("hy_biasT", [DEPTH, 128, 2], F32, kin)
        self.hy_fb = P.dram("hy_fb", [DEPTH, 64, 4], F32, kin)
        self.hy_f_w1 = P.dram("hy_f_w1", [DEPTH, 33, 64], F32, kin)
        self.hy_f_w2 = P.dram("hy_f_w2", [DEPTH, 64, 64], F32, kin)
        self.hy_f_w3 = P.dram("hy_f_w3", [DEPTH, 64, 64], F32, kin)
        self.hy_f_w4 = P.dram("hy_f_w4", [DEPTH, 64, 512], F32, kin)
        self.hy_ndelta = P.dram("hy_ndelta", [128, 512], F32, kin)
        self.hy_tabs = {}
        for L_ in (SEQ, CTX):
            ntt_ = L_ // 128
            self.hy_tabs[L_] = dict(
                z=P.dram(f"hy_z{L_}", [33, L_], F32, kin),
                t01=P.dram(f"hy_t01_{L_}", [128, ntt_], F32, kin),
                F=P.dram(f"dftF{L_}", [2, ntt_, 128, ntt_, 128], BF16, kin),
                I=P.dram(f"dftI{L_}", [2, L_ // 256, 128, ntt_, 256], BF16, kin))
        self.rw_muT = P.dram("rw_muT", [DEPTH, 128, 9], F32, kin)
        self.rw_w0a0 = P.dram("rw_w0a0", [DEPTH, 128, 2, 2, 2], F32, kin)
        self.rw_vecs = P.dram("rw_vecs", [DEPTH, 128, 2, 3], F32, kin)
        self.rw_w2 = P.dram("rw_w2", [DEPTH, 2, 64, 256], F32, kin)
        self.rw_a2 = P.dram("rw_a2", [DEPTH, 2, 64, 256], F32, kin)
        self.rw_g2 = P.dram("rw_g2", [DEPTH, 128, 256], F32, kin)
        self.rw_ln = P.dram("rw_ln", [DEPTH, 2, 256], F32, kin)
        self.rw_masks = P.dram("rw_masks", [128, 2, 2, 128], F32, kin)
        self.c_bdo2 = P.dram("c_bdo2", [128, 2], F32, kin)
        self.rwA = P.dram("rwA", [2, T // 64, 2, 128, 520], F32, sk)
        self.rw_y = P.dram("rw_y", [2, T, 256], F32, sk)
        self.rw_vtm = P.dram("rw_vtm", [T, 256], F32, sk)
        self.rw_gtm = P.dram("rw_gtm", [T, 256], F32, sk)
        self.rw_bon = P.dram("rw_bon", [T, 4], F32, sk)
        self.pT = P.dram("pT", [D_IN, T], F32, "ExternalInput" if pT_in else sk)
        self.vg = P.dram("vg", [T, 128], BF16, sk)
        self.ymixT = P.dram("ymixT", [D, T], BF16, sk)
        self.xs = P.dram("xs", [D, T], F32, sk)
        self.out = P.dram("outT", [D, SEQ], F32, "ExternalOutput")
        self.ones_bf = P.alloc([128, 128], BF16)
        self.b_const = Buf("const")
        P.op("pool", "memset", self.ones_bf, 1.0, writes=[self.b_const])
        self.der = [P.alloc([128, N_MOD, KC, 2], F32) for _ in range(DEPTH)]
        self.b_der = Buf("der")

    def dump(self, name, ap, shape, reads, dtype=F32):
        if not self.dbg:
            return
        d = self.P.dram(name, list(shape), dtype, "ExternalOutput")
        self.P.dma("sp", d.ap(), ap, reads=reads)

    def phase_mod(self):
        P = self.P
        P.push()
        cv = P.alloc([128, KC, 2], F32)
        sc = P.alloc([128, KC, 2], F32)
        b_cv = Buf()
        P.dma("sp", cv, self.cvec.ap(), writes=[b_cv])
        b_sc = Buf()
        P.op("act", "activation", sc, cv, AF.Silu, reads=[b_cv], writes=[b_sc])
        CB = 1152
        wrot = Rot([(P.alloc([128, KC, CB], F32), Buf()) for _ in range(2)])
        mod = P.alloc([128, N_MOD, KC, 2], F32)
        b_mod = Buf()
        adab = P.alloc([128, 72], F32)
        nrm = P.alloc([128, 3, KC], F32)
        b_ld = Buf()
        pm = self.P.banks[0]
        b_pm = self.P.bank_bufs[0]
        for l in range(DEPTH):
            P.dma("sp", adab, self.adab[l], writes=[b_ld])
            P.dma("sp", nrm, self.norms[l].rearrange("i p k -> p i k"), writes=[b_ld])
            for cb in range(N_MOD * D // CB):
                wt, b_wt = wrot.next()
                P.dma("sp", wt, self.ada_w[l].rearrange("(kc p) n -> p kc n", p=128)[:, :, cb * CB:(cb + 1) * CB], writes=[b_wt])
                for jj in range(CB // 128):
                    j = cb * (CB // 128) + jj
                    calls = [((pm[:, 2 * j:2 * j + 2], wt[:, kc, jj * 128:(jj + 1) * 128], sc[:, kc, :]),
                              dict(start=(kc == 0), stop=(kc == KC - 1))) for kc in range(KC)]
                    P.mm(calls, reads=[b_wt, b_sc], writes=[b_pm])
            pmv = pm[:, 0:144].rearrange("p (j s) -> p j s", s=2)
            modv = mod.rearrange("p n k s -> p (n k) s")
            for s in range(2):
                P.op("dve", "tensor_tensor", modv[:, :, s], pmv[:, :, s], adab, ALU.add,
                     reads=[b_pm, b_ld], writes=[b_mod])
            der = self.der[l]
            for i in range(3):
                for s in range(2):
                    P.op("dve", "tensor_copy", der[:, 3 * i, :, s], mod[:, 3 * i, :, s],
                         reads=[b_mod], writes=[self.b_der])
                    P.op("dve", "scalar_tensor_tensor", der[:, 3 * i + 1, :, s], mod[:, 3 * i + 1, :, s], 1.0,
                         nrm[:, i, :], ALU.add, ALU.mult, reads=[b_mod, b_ld], writes=[self.b_der])
                    f = 1.0 if i == 1 else 0.5
                    P.op("dve", "tensor_single_scalar", der[:, 3 * i + 2, :, s], mod[:, 3 * i + 2, :, s], f, ALU.mult,
                         reads=[b_mod], writes=[self.b_der])
            self.dump(f"dbg_der{l}", self.der[l].rearrange("p n k s -> p (n k s)"), [128, 144], [self.b_der])
            self.dump(f"dbg_mod{l}", mod.rearrange("p n k s -> p (n k s)"), [128, 144], [b_mod])
        P.pop("phase_mod")

    def emit_adaln(self, l, i, s, xt, b_xt, n, h, b_h, sq, b_sq, rstd, b_rstd, tmp_rot, ps, b_ps):
        P = self.P
        der = self.der[l]
        P.op("act", "activation", sq[:, :, :n], xt[:, :, :n], AF.Square, reads=[b_xt], writes=[b_sq])
        calls = [((ps[:, :n], self.ones_bf, sq[:, kc, :n]), dict(start=(kc == 0), stop=(kc == KC - 1))) for kc in range(KC)]
        P.mm(calls, reads=[b_sq, self.b_const], writes=[b_ps])
        P.op("act", "activation", rstd[:, :n], ps[:, :n], AF.Ln, bias=EPS, scale=1.0 / D, reads=[b_ps], writes=[b_rstd])
        P.op("act", "activation", rstd[:, :n], rstd[:, :n], AF.Exp, scale=-0.5, reads=[b_rstd], writes=[b_rstd])
        for kc in range(KC):
            tmp, b_tmp = tmp_rot.next()
            P.op("dve", "scalar_tensor_tensor", tmp[:, :n], xt[:, kc, :n], der[:, 3 * i + 1, kc, s:s + 1], rstd[:, :n],
                 ALU.mult, ALU.mult, reads=[b_xt, b_rstd, self.b_der], writes=[b_tmp])
            P.op("act", "activation", h[:, kc, :n], tmp[:, :n], AF.Identity, bias=der[:, 3 * i, kc, s:s + 1], scale=1.0,
                 reads=[b_tmp, self.b_der], writes=[b_h])

    def phase_ffn(self, l, which, src, dst, dst_lat=None, tiles=None):
        P = self.P
        i = 0 if which == 1 else 2
        P.push()
        wg = P.alloc([128, KC, D_FF], BF16)
        wu = P.alloc([128, KC, D_FF], BF16)
        wd = P.alloc([128, NF, D], BF16)
        b_wg, b_wu, b_wd = Buf(), Buf(), Buf()
        gsrc = self.w_ffn[f"ffn{which}_gate"][l].rearrange("(kc p) n -> p kc n", p=128)
        usrc = self.w_ffn[f"ffn{which}_up"][l].rearrange("(kc p) n -> p kc n", p=128)
        dsrc = self.w_ffn[f"ffn{which}_down"][l].rearrange("(f p) n -> p f n", p=128)
        for kc in range(KC):
            P.dma("pool", wg[:, kc, :], gsrc[:, kc, :], writes=[b_wg])
            P.dma("pool", wu[:, kc, :], usrc[:, kc, :], writes=[b_wu])
        for f0 in range(0, NF, 4):
            f1 = min(NF, f0 + 4)
            P.dma("pool", wd[:, f0:f1, :], dsrc[:, f0:f1, :], writes=[b_wd])
        xrot = Rot([(P.alloc([128, KC, NT], F32), Buf()) for _ in range(2)])
        hrot = Rot([(P.alloc([128, KC, NT], BF16), Buf()) for _ in range(2)])
        sq = P.alloc([128, KC, NT], BF16)
        b_sq = Buf()
        rstd = P.alloc([128, NT], F32)
        b_rstd = Buf()
        tmp_rot = Rot([(P.alloc([128, NT], F32), Buf()) for _ in range(2)])
        sg_rot = Rot([(P.alloc([128, NT], F32), Buf()) for _ in range(2)])
        a = P.alloc([128, NF, NT], BF16)
        b_a = [Buf() for _ in range(NF)]
        banks, bb = P.banks, P.bank_bufs
        pg_rot = Rot([(banks[1], bb[1]), (banks[2], bb[2])])
        pu_rot = Rot([(banks[3], bb[3]), (banks[4], bb[4])])
        pd_rot = Rot([(banks[5], bb[5]), (banks[6], bb[6])])
        srcv = src.rearrange("(kc p) t -> p kc t", p=128)
        gate = self.der[l]
        for (t0, n) in (tiles or TILES):
            s = 1 if t0 < CTX else 0
            xt, b_xt = xrot.next()
            h, b_h = hrot.next()
            P.dma("sp", xt[:, :, :n], srcv[:, :, t0:t0 + n], writes=[b_xt])
            self.emit_adaln(l, i, s, xt, b_xt, n, h, b_h, sq, b_sq, rstd, b_rstd, tmp_rot, banks[0], bb[0])
            if t0 == 0 and l == 0 and which == 1:
                self.dump("dbg_h", h.rearrange("p k n -> p (k n)"), [128, KC * NT], [b_h], BF16)
                self.dump("dbg_rstd", rstd, [128, NT], [b_rstd])
            for f in range(NF):
                pg, b_pg = pg_rot.next()
                pu, b_pu = pu_rot.next()
                P.mm([((pg[:, :n], wg[:, kc, f * 128:(f + 1) * 128], h[:, kc, :n]), dict(start=(kc == 0), stop=(kc == KC - 1)))
                      for kc in range(KC)], reads=[b_wg, b_h], writes=[b_pg])
                P.mm([((pu[:, :n], wu[:, kc, f * 128:(f + 1) * 128], h[:, kc, :n]), dict(start=(kc == 0), stop=(kc == KC - 1)))
                      for kc in range(KC)], reads=[b_wu, b_h], writes=[b_pu])
                sg, b_sg = sg_rot.next()
                P.op("act", "activation", sg[:, :n], pg[:, :n], AF.Silu, reads=[b_pg], writes=[b_sg])
                P.op("dve", "tensor_tensor", a[:, f, :n], sg[:, :n], pu[:, :n], ALU.mult, reads=[b_sg, b_pu], writes=[b_a[f]])
            if t0 == 0 and l == 0 and which == 1:
                self.dump("dbg_a", a.rearrange("p k n -> p (k n)"), [128, NF * NT], b_a, BF16)
            for dc in range(KC):
                pd, b_pd = pd_rot.next()
                P.mm([((pd[:, :n], wd[:, f, dc * 128:(dc + 1) * 128], a[:, f, :n]), dict(start=(f == 0), stop=(f == NF - 1)))
                      for f in range(NF)], reads=[b_wd] + b_a, writes=[b_pd])
                P.op("dve", "scalar_tensor_tensor", xt[:, dc, :n], pd[:, :n], gate[:, 3 * i + 2, dc, s:s + 1], xt[:, dc, :n],
                     ALU.mult, ALU.add, reads=[b_pd, b_xt, self.b_der], writes=[b_xt])
            if dst_lat is not None:
                if t0 >= CTX:
                    dv = dst_lat.rearrange("(kc p) t -> p kc t", p=128)
                    P.dma("pool", dv[:, :, t0 - CTX:t0 - CTX + n], xt[:, :, :n], reads=[b_xt])
            else:
                dv = dst.rearrange("(kc p) t -> p kc t", p=128)
                P.dma("pool", dv[:, :, t0:t0 + n], xt[:, :, :n], reads=[b_xt])
        P.pop("phase_ffn")


    def phase_proj(self, l, src):
        P = self.P
        P.push()
        win = P.alloc([128, KC, D_IN], BF16)
        b_win = Buf()
        wsrc = self.w_in[l].rearrange("(kc p) n -> p kc n", p=128)
        for kc in range(KC):
            P.dma("pool", win[:, kc, :], wsrc[:, kc, :], writes=[b_win])
        xrot = Rot([(P.alloc([128, KC, NT], F32), Buf()) for _ in range(2)])
        hrot = Rot([(P.alloc([128, KC, NT], BF16), Buf()) for _ in range(2)])
        sq = P.alloc([128, KC, NT], BF16)
        b_sq = Buf()
        rstd = P.alloc([128, NT], F32)
        b_rstd = Buf()
        tmp_rot = Rot([(P.alloc([128, NT], F32), Buf()) for _ in range(2)])
        NCH = 23
        st_rot = Rot([(P.alloc([128, NCH, NT], F32), Buf()) for _ in range(2)])
        vst_rot = Rot([(P.alloc([128, 2, 128], BF16), Buf()) for _ in range(2)])
        banks, bb = P.banks, P.bank_bufs
        pp_rot = Rot([(banks[i], bb[i]) for i in (1, 2, 3, 4)])
        pv_rot = Rot([(banks[i], bb[i]) for i in (5, 6)])
        srcv = src.rearrange("(kc p) t -> p kc t", p=128)
        pTv = self.pT[0:2816, :].rearrange("(c p) t -> p c t", p=128)
        VG0 = HY_COLS + 384
        for (t0, n) in TILES:
            s = 1 if t0 < CTX else 0
            xt, b_xt = xrot.next()
            h, b_h = hrot.next()
            P.dma("sp", xt[:, :, :n], srcv[:, :, t0:t0 + n], writes=[b_xt])
            self.emit_adaln(l, 1, s, xt, b_xt, n, h, b_h, sq, b_sq, rstd, b_rstd, tmp_rot, banks[0], bb[0])
            st, b_st = st_rot.next()
            for c in range(NCH):
                rows = 128 if c < 22 else 32
                pp, b_pp = pp_rot.next()
                P.mm([((pp[0:rows, :n], win[:, kc, c * 128:c * 128 + rows], h[:, kc, :n]), dict(start=(kc == 0), stop=(kc == KC - 1)))
                      for kc in range(KC)], reads=[b_win, b_h], writes=[b_pp])
                if c % 2 == 0:
                    P.op("act", "copy", st[0:rows, c, :n], pp[0:rows, :n], reads=[b_pp], writes=[b_st])
                else:
                    P.op("dve", "tensor_copy", st[0:rows, c, :n], pp[0:rows, :n], reads=[b_pp], writes=[b_st])
            P.dma("pool", pTv[:, :, t0:t0 + n], st[:, 0:22, :n], reads=[b_st])
            P.dma("pool", self.pT[2816:2848, t0:t0 + n], st[0:32, 22, :n], reads=[b_st])
            vst, b_vst = vst_rot.next()
            for sub in range(n // 128):
                pv, b_pv = pv_rot.next()
                P.mm([((pv[:, 0:128], h[:, kc, sub * 128:(sub + 1) * 128], win[:, kc, VG0:VG0 + 128]), dict(start=(kc == 0), stop=(kc == KC - 1)))
                      for kc in range(KC)], reads=[b_win, b_h], writes=[b_pv])
                P.op("dve", "tensor_copy", vst[:, sub, :], pv[:, 0:128], reads=[b_pv], writes=[b_vst])
            P.dma("pool", self.vg[t0:t0 + n, :].rearrange("(s p) c -> p s c", p=128), vst[:, 0:n // 128, :], reads=[b_vst])
        P.pop("phase_proj")

    def emit_headnorm(self, src, b_src, rows, n, ndim, ones_f, gain, cosv, sinv, rotm, out_bf, b_out, wk, psA, b_psA, psB, b_psB):
        P = self.P
        sq, b_sq, rstd, b_rstd, qn, b_qn, t1, b_t1 = wk
        P.op("act", "activation", sq[0:rows, :n], src, AF.Square, reads=[b_src], writes=[b_sq])
        P.mm([((psA[0:rows, :n], ones_f, sq[0:rows, :n]), dict(start=True, stop=True))], reads=[b_sq, self.b_const], writes=[b_psA])
        P.op("act", "activation", rstd[0:rows, :n], psA[0:rows, :n], AF.Ln, bias=EPS, scale=1.0 / ndim, reads=[b_psA], writes=[b_rstd])
        P.op("act", "activation", rstd[0:rows, :n], rstd[0:rows, :n], AF.Exp, scale=-0.5, reads=[b_rstd], writes=[b_rstd])
        P.op("dve", "scalar_tensor_tensor", qn[0:rows, :n], src, gain, rstd[0:rows, :n], ALU.mult, ALU.mult,
             reads=[b_src, b_rstd, self.b_const], writes=[b_qn])
        P.mm([((psB[0:rows, :n], rotm, qn[0:rows, :n]), dict(start=True, stop=True))], reads=[b_qn, self.b_const], writes=[b_psB])
        P.op("dve", "tensor_tensor", t1[0:rows, :n], qn[0:rows, :n], cosv, ALU.mult, reads=[b_qn, self.b_const], writes=[b_t1])
        P.op("dve", "tensor_tensor", qn[0:rows, :n], psB[0:rows, :n], sinv, ALU.mult, reads=[b_psB, self.b_const], writes=[b_qn])
        if isinstance(out_bf, list):
            for (sl, ap) in out_bf:
                P.op("dve", "tensor_tensor", ap, t1[sl, :n], qn[sl, :n], ALU.add, reads=[b_t1, b_qn], writes=[b_out])
        else:
            P.op("dve", "tensor_tensor", out_bf, t1[0:rows, :n], qn[0:rows, :n], ALU.add, reads=[b_t1, b_qn], writes=[b_out])

    def emit_attn(self, heads, scale, K_rows, n_kt, ebufs, b_stage_rot, ctx_out, co=None, co_every=8):
        P = self.P
        banks, bb = P.banks, P.bank_bufs
        st_rot = Rot([(banks[i], bb[i]) for i in (0, 1, 2, 3)])
        acc_rot = Rot([(banks[i], bb[i]) for i in (4, 5)])
        QN = 512
        LA = 2
        qtiles = [(CTX + i * QN, QN, n_kt) for i in range(SEQ // QN)]
        if ctx_out:
            qtiles = [(0, CTX, CTX // 128)] + qtiles
        for hd in heads:
            for (q0, qn_, nk) in qtiles:
                acc, b_acc = acc_rot.next()
                pend = []

                def pv(item, last):
                    kt_, eb_, b_eb_ = item
                    P.mm([((acc[:, :qn_], hd["v"](kt_), eb_[:, :qn_]), dict(start=(kt_ == 0), stop=last))],
                         reads=[hd["b_v"], b_eb_], writes=[b_acc])
                for kt in range(nk):
                    st, b_st = st_rot.next()
                    P.mm([((st[:, :qn_], hd["k"][:, kt * 128:(kt + 1) * 128], hd["q"][:, q0:q0 + qn_]), dict(start=True, stop=True))],
                         reads=[hd["b_k"], hd["b_q"]], writes=[b_st])
                    eb, b_eb = ebufs.next()
                    P.op("act", "activation", eb[:, :qn_], st[:, :qn_], AF.Exp, scale=scale, reads=[b_st], writes=[b_eb])
                    pend.append((kt, eb, b_eb))
                    if len(pend) > LA:
                        pv(pend.pop(0), False)
                    if co is not None and kt % co_every == co_every - 1:
                        if next(co, "end") == "end":
                            co = None
                while pend:
                    it_ = pend.pop(0)
                    pv(it_, len(pend) == 0)
                (rec, y), b_y = b_stage_rot.next()
                P.op("dve", "reciprocal", rec[0:64, :qn_], acc[64:128, :qn_], reads=[b_acc], writes=[b_y])
                P.op("dve", "tensor_tensor", y[0:64, :qn_], acc[0:64, :qn_], rec[0:64, :qn_], ALU.mult, reads=[b_acc, b_y], writes=[b_y])
                r0 = hd["out_row"]
                P.dma("pool", self.ymixT[r0:r0 + 64, q0:q0 + qn_], y[0:64, :qn_], reads=[b_y])
        if co is not None:
            for _ in co:
                pass

    def phase_gqa(self, l, ctx_out, co_rwkv=False):
        P = self.P
        P.push()
        cosg = P.alloc([128, T], F32)
        sing = P.alloc([128, T], F32)
        bd64 = P.alloc([128, 128], F32)
        rotm = P.alloc([128, 128], F32)
        gain = P.alloc([128, 2], F32)
        bc = self.b_const
        P.dma("sp", cosg, self.rope_g[0], writes=[bc])
        P.dma("sp", sing, self.rope_g[1], writes=[bc])
        P.dma("sp", bd64, self.c_bd64.ap(), writes=[bc])
        P.dma("sp", rotm, self.c_rotm.ap(), writes=[bc])
        P.dma("sp", gain, self.gqa_gain[l], writes=[bc])
        qr = P.alloc([128, 4, T], BF16)
        kr = P.alloc([128, T], BF16)
        b_qr, b_kr = Buf(), Buf()
        P.op("pool", "memset", qr, 0.0, writes=[b_qr])
        vaug = P.alloc([128, T // 128, 2, 128], BF16)
        b_v = Buf()
        P.op("pool", "memset", vaug, 1.0, writes=[b_v])
        vgv = self.vg.rearrange("(kt p) c -> p kt c", p=128)
        for g in range(2):
            P.dma("sp", vaug[:, :, g, 0:64], vgv[:, :, g * 64:(g + 1) * 64], writes=[b_v])
        QN = 512
        prot = Rot([(P.alloc([128, QN], F32), Buf()) for _ in range(2)])
        wk = (P.alloc([128, QN], F32), Buf(), P.alloc([128, QN], F32), Buf(), P.alloc([128, QN], F32), Buf(), P.alloc([128, QN], F32), Buf())
        banks, bb = P.banks, P.bank_bufs
        G0 = HY_COLS
        for r in range(3):
            for t0 in range(0, T, QN):
                n = min(QN, T - t0)
                pt, b_pt = prot.next()
                P.dma("sp", pt[:, :n], self.pT[G0 + r * 128:G0 + (r + 1) * 128, t0:t0 + n], writes=[b_pt])
                outap = [(slice(0, 64), qr[0:64, 2 * r, t0:t0 + n]), (slice(64, 128), qr[64:128, 2 * r + 1, t0:t0 + n])] if r < 2 else kr[:, t0:t0 + n]
                self.emit_headnorm(pt[:, :n], b_pt, 128, n, 64, bd64, gain[:, (0 if r < 2 else 1):(1 if r < 2 else 2)],
                                   cosg[:, t0:t0 + n], sing[:, t0:t0 + n], rotm, outap, (b_qr if r < 2 else b_kr), wk,
                                   banks[6], bb[6], banks[7], bb[7])
        ebufs = Rot([(P.alloc([128, QN], BF16), Buf()) for _ in range(4)])
        stage = Rot([((P.alloc([128, QN], F32), P.alloc([128, QN], BF16)), Buf()) for _ in range(2)])
        heads = []
        for r in range(2):
            for half in range(2):
                base = half * 64
                heads.append(dict(q=qr[:, 2 * r + half, :], b_q=b_qr, k=kr[:, :], b_k=b_kr,
                                  v=(lambda kt, half=half: vaug[:, kt, half, :]), b_v=b_v,
                                  out_row=256 + (half * 2 + r) * 64))
        co = self.rwkv_bc_gen(l, ctx_out, (6, 7)) if co_rwkv else None
        self.emit_attn(heads, 64 ** -0.5, 64, T // 128, ebufs, stage, ctx_out, co=co, co_every=8)
        P.pop("phase_gqa")

    def phase_mla(self, l, ctx_out):
        P = self.P
        P.push()
        bc = self.b_const
        cosm = P.alloc([96, T], F32)
        sinm = P.alloc([96, T], F32)
        ones_f = P.alloc([128, 128], F32)
        rot96 = P.alloc([96, 96], F32)
        gain = P.alloc([128, 5], F32)
        P.dma("sp", cosm, self.rope_m[0], writes=[bc])
        P.dma("sp", sinm, self.rope_m[1], writes=[bc])
        P.op("pool", "memset", ones_f, 1.0, writes=[bc])
        P.dma("sp", rot96, self.c_rot96.ap(), writes=[bc])
        P.dma("sp", gain, self.mla_gain[l], writes=[bc])
        wuq = P.alloc([128, 2, 384], BF16)
        wuk = P.alloc([128, 4, 64], BF16)
        wuv = P.alloc([128, 4, 64], BF16)
        P.dma("pool", wuq, self.mla_w_uq[l].rearrange("(c p) n -> p c n", p=128), writes=[bc])
        ukv = self.mla_w_ukv[l].rearrange("p (h two d) -> p h two d", h=4, two=2)
        P.dma("pool", wuk, ukv[:, :, 0, :], writes=[bc])
        P.dma("pool", wuv, ukv[:, :, 1, :], writes=[bc])
        qr = [P.alloc([96, T], BF16) for _ in range(4)]
        kr = [P.alloc([96, T], BF16) for _ in range(4)]
        b_qr = [Buf() for _ in range(4)]
        b_kr = [Buf() for _ in range(4)]
        vaug = P.alloc([128, T // 128, 4, 128], BF16)
        b_v = Buf()
        P.op("pool", "memset", vaug, 1.0, writes=[b_v])
        QN = 512
        M0 = HY_COLS + GQA_COLS
        cq_rot = Rot([(P.alloc([128, 3, QN], F32), Buf()) for _ in range(1)])
        sqc = P.alloc([128, 3, QN], F32)
        b_sqc = Buf()
        rs2 = P.alloc([128, 2, QN], F32)
        b_rs2 = Buf()
        cqn = P.alloc([128, 3, QN], BF16)
        b_cqn = Buf()
        qs_rot = Rot([(P.alloc([96, QN], F32), Buf()) for _ in range(2)])
        kf_rot = Rot([(P.alloc([96, QN], F32), Buf()) for _ in range(2)])
        wk = (P.alloc([128, QN], F32), Buf(), P.alloc([128, QN], F32), Buf(), P.alloc([128, QN], F32), Buf(), P.alloc([128, QN], F32), Buf())
        banks, bb = P.banks, P.bank_bufs
        pq_rot = Rot([(banks[i], bb[i]) for i in (1, 2)])
        pTc = self.pT[M0:M0 + 384, :].rearrange("(c p) t -> p c t", p=128)
        for t0 in range(0, T, QN):
            n = min(QN, T - t0)
            cq, b_cq = cq_rot.next()
            P.dma("sp", cq[:, :, :n], pTc[:, :, t0:t0 + n], writes=[b_cq])
            P.op("act", "activation", sqc[:, :, :n], cq[:, :, :n], AF.Square, reads=[b_cq], writes=[b_sqc])
            P.mm([((banks[6][:, :n], ones_f, sqc[:, c, :n]), dict(start=(c == 0), stop=(c == 1))) for c in range(2)],
                 reads=[b_sqc, bc], writes=[bb[6]])
            P.mm([((banks[7][:, :n], ones_f, sqc[:, 2, :n]), dict(start=True, stop=True))], reads=[b_sqc, bc], writes=[bb[7]])
            P.op("act", "activation", rs2[:, 0, :n], banks[6][:, :n], AF.Ln, bias=EPS, scale=1.0 / 256, reads=[bb[6]], writes=[b_rs2])
            P.op("act", "activation", rs2[:, 1, :n], banks[7][:, :n], AF.Ln, bias=EPS, scale=1.0 / 128, reads=[bb[7]], writes=[b_rs2])
            P.op("act", "activation", rs2[:, :, :n], rs2[:, :, :n], AF.Exp, scale=-0.5, reads=[b_rs2], writes=[b_rs2])
            for c in range(3):
                P.op("dve", "scalar_tensor_tensor", cqn[:, c, :n], cq[:, c, :n], gain[:, c:c + 1], rs2[:, (0 if c < 2 else 1), :n],
                     ALU.mult, ALU.mult, reads=[b_cq, b_rs2, bc], writes=[b_cqn])
            for h in range(4):
                pq, b_pq = pq_rot.next()
                P.mm([((pq[0:96, :n], wuq[:, c, h * 96:(h + 1) * 96], cqn[:, c, :n]), dict(start=(c == 0), stop=(c == 1))) for c in range(2)],
                     reads=[b_cqn, bc], writes=[b_pq])
                qs, b_qs = qs_rot.next()
                P.op("act", "copy", qs[:, :n], pq[0:96, :n], reads=[b_pq], writes=[b_qs])
                self.emit_headnorm(qs[:, :n], b_qs, 96, n, 96, ones_f[0:96, 0:96], gain[0:96, 3:4], cosm[:, t0:t0 + n], sinm[:, t0:t0 + n],
                                   rot96, qr[h][:, t0:t0 + n], b_qr[h], wk, banks[6], bb[6], banks[7], bb[7])
                pk, b_pk = pq_rot.next()
                P.mm([((pk[0:64, :n], wuk[:, h, :], cqn[:, 2, :n]), dict(start=True, stop=True))], reads=[b_cqn, bc], writes=[b_pk])
                kf, b_kf = kf_rot.next()
                P.dma("sp", kf[64:96, :n], self.pT[M0 + 384:M0 + 416, t0:t0 + n], writes=[b_kf])
                P.op("act", "copy", kf[0:64, :n], pk[0:64, :n], reads=[b_pk], writes=[b_kf])
                self.emit_headnorm(kf[:, :n], b_kf, 96, n, 96, ones_f[0:96, 0:96], gain[0:96, 4:5], cosm[:, t0:t0 + n], sinm[:, t0:t0 + n],
                                   rot96, kr[h][:, t0:t0 + n], b_kr[h], wk, banks[6], bb[6], banks[7], bb[7])
            for sub in range(n // 128):
                kt = t0 // 128 + sub
                pv, b_pv = pq_rot.next()
                P.mm([((pv[:, 0:256], cqn[:, 2, sub * 128:(sub + 1) * 128], wuv.rearrange("p h d -> p (h d)")), dict(start=True, stop=True))],
                     reads=[b_cqn, bc], writes=[b_pv])
                P.op("dve", "tensor_copy", vaug[:, kt, :, 0:64], pv[:, 0:256].rearrange("p (h d) -> p h d", h=4), reads=[b_pv], writes=[b_v])
        ebufs = Rot([(P.alloc([128, QN], BF16), Buf()) for _ in range(4)])
        stage = Rot([((P.alloc([128, QN], F32), P.alloc([128, QN], BF16)), Buf()) for _ in range(2)])
        heads = [dict(q=qr[h], b_q=b_qr[h], k=kr[h], b_k=b_kr[h], v=(lambda kt, h=h: vaug[:, kt, h, :]), b_v=b_v,
                      out_row=512 + h * 64) for h in range(4)]
        self.emit_attn(heads, 96 ** -0.5, 96, T // 128, ebufs, stage, ctx_out)
        P.pop("phase_mla")

    def emit_sin(self, ps, rows, n, bcol, fb, out, b_ps, b_out, wk):
        P = self.P
        pre, b_pre, r, b_r = wk
        MAGIC = 12582912.0
        TWO_PI = 2.0 * math.pi
        P.op("dve", "tensor_scalar", pre[0:rows, :n], ps, fb[0:rows, bcol:bcol + 1], fb[0:rows, 3:4], ALU.add, ALU.mult,
             reads=[b_ps, self.b_const], writes=[b_pre])
        P.op("dve", "tensor_scalar", r[0:rows, :n], pre[0:rows, :n], 1.0 / TWO_PI, MAGIC, ALU.mult, ALU.add, reads=[b_pre], writes=[b_r])
        P.op("dve", "tensor_scalar", r[0:rows, :n], r[0:rows, :n], MAGIC, -TWO_PI, ALU.subtract, ALU.mult, reads=[b_r], writes=[b_r])
        P.op("dve", "tensor_tensor", r[0:rows, :n], r[0:rows, :n], pre[0:rows, :n], ALU.add, reads=[b_r, b_pre], writes=[b_r])
        P.op("dve", "tensor_scalar", r[0:rows, :n], r[0:rows, :n], math.pi, -math.pi, ALU.min, ALU.max, reads=[b_r], writes=[b_r])
        P.op("act", "activation", out, r[0:rows, :n], AF.Sin, reads=[b_r], writes=[b_out])

    def phase_hyena(self, l, L, col0):
        P = self.P
        bc = self.b_const
        banks, bb = P.banks, P.bank_bufs
        ntt = L // 128
        tabs = self.hy_tabs[L]
        P.push()
        u = P.alloc([128, 2, L], F32)
        b_u = Buf()
        Akt = P.alloc([128, ntt, 256], BF16)
        Bkt = P.alloc([128, ntt, 256], BF16)
        b_AB = Buf()
        cw = P.alloc([128, 6, 4], F32)
        hbias = P.alloc([128, 2], F32)
        P.dma("sp", cw, self.hy_cw[l], writes=[bc])
        P.dma("sp", hbias, self.hy_biasT[l], writes=[bc])
        P.push()
        rhsC = P.alloc([128, ntt, 512], BF16)
        rhsS = P.alloc([128, ntt, 512], BF16)
        b_rhs = Buf()
        P.push()
        ident = P.alloc([128, 128], F32)
        P.dma("sp", ident, self.c_ident.ap(), writes=[bc])
        raw = P.alloc([128, L + 2], F32)
        b_raw = Buf()
        x1c = P.alloc([128, L], F32)
        b_x1c = Buf()
        vc = P.alloc([128, L], F32)
        b_vc = Buf()
        P.op("pool", "memset", raw[:, 0:1], 0.0, writes=[b_raw])
        P.op("pool", "memset", raw[:, L + 1:L + 2], 0.0, writes=[b_raw])

        def conv(j6, out, b_out):
            P.dma("sp", raw[:, 1:L + 1], self.pT[j6 * 128:(j6 + 1) * 128, col0:col0 + L], writes=[b_raw])
            P.op("dve", "tensor_scalar", out, raw[:, 1:L + 1], cw[:, j6, 1:2], cw[:, j6, 3:4], ALU.mult, ALU.add, reads=[b_raw, bc], writes=[b_out])
            P.op("dve", "scalar_tensor_tensor", out, raw[:, 0:L], cw[:, j6, 0:1], out, ALU.mult, ALU.add, reads=[b_raw, b_out, bc], writes=[b_out])
            P.op("dve", "scalar_tensor_tensor", out, raw[:, 2:L + 2], cw[:, j6, 2:3], out, ALU.mult, ALU.add, reads=[b_raw, b_out, bc], writes=[b_out])
        tr_rot = Rot([(banks[i], bb[i]) for i in (1, 2, 3, 4)])
        for j in range(2):
            conv(j, x1c, b_x1c)
            conv(4 + j, vc, b_vc)
            P.op("dve", "tensor_tensor", u[:, j, :], x1c, vc, ALU.mult, reads=[b_x1c, b_vc], writes=[b_u])
            for tt in range(ntt):
                pt, b_pt = tr_rot.next()
                P.mm([((pt[:, 0:128], u[:, j, tt * 128:(tt + 1) * 128], ident), {})], reads=[b_u, bc], writes=[b_pt], meth="transpose")
                P.op("act", "copy", rhsC[:, tt, j * 128:(j + 1) * 128], pt[:, 0:128], reads=[b_pt], writes=[b_rhs])
                P.op("dve", "tensor_copy", rhsS[:, tt, j * 128:(j + 1) * 128], pt[:, 0:128], reads=[b_pt], writes=[b_rhs])
        P.pop()
        P.push()
        z = P.alloc([33, L], F32)
        w1 = P.alloc([33, 64], F32)
        w2 = P.alloc([64, 64], F32)
        w3 = P.alloc([64, 64], F32)
        w4 = P.alloc([64, 512], F32)
        fb = P.alloc([64, 4], F32)
        t01 = P.alloc([128, ntt], F32)
        ndel = P.alloc([128, 512], F32)
        P.dma("sp", z, tabs["z"].ap(), writes=[bc])
        P.dma("sp", w1, self.hy_f_w1[l], writes=[bc])
        P.dma("sp", w2, self.hy_f_w2[l], writes=[bc])
        P.dma("sp", w3, self.hy_f_w3[l], writes=[bc])
        P.dma("sp", w4, self.hy_f_w4[l], writes=[bc])
        P.dma("sp", fb, self.hy_fb[l], writes=[bc])
        P.dma("sp", t01, tabs["t01"].ap(), writes=[bc])
        P.dma("sp", ndel, self.hy_ndelta.ap(), writes=[bc])
        hA = P.alloc([64, L], F32)
        hB = P.alloc([64, L], F32)
        b_hA, b_hB = Buf(), Buf()
        wk = (P.alloc([64, 512], F32), Buf(), P.alloc([64, 512], F32), Buf())
        CW = min(512, L)
        ps_rot = Rot([(banks[i], bb[i]) for i in (1, 2, 3, 4)])
        for (lhs, K_, src, b_src, dst, b_dst, bcol) in ((w1, 33, z, bc, hA, b_hA, 0), (w2, 64, hA, b_hA, hB, b_hB, 1), (w3, 64, hB, b_hB, hA, b_hA, 2)):
            for c0 in range(0, L, CW):
                ps, b_ps = ps_rot.next()
                P.mm([((ps[0:64, :CW], lhs, src[0:K_, c0:c0 + CW]), dict(start=True, stop=True))], reads=[b_src, bc], writes=[b_ps])
                self.emit_sin(ps[0:64, :CW], 64, CW, bcol, fb, dst[:, c0:c0 + CW], b_ps, b_dst, wk)
        wt_rot = Rot([(P.alloc([128, 512], F32), Buf()) for _ in range(2)])
        ft_rot = Rot([(P.alloc([128, 512], F32), Buf()) for _ in range(2)])
        for tt in range(ntt):
            ps, b_ps = ps_rot.next()
            P.mm([((ps[:, :], hA[:, tt * 128:(tt + 1) * 128], w4), dict(start=True, stop=True))], reads=[b_hA, bc], writes=[b_ps])
            wt, b_wt = wt_rot.next()
            P.op("act", "activation", wt, ndel, AF.Exp, scale=t01[:, tt:tt + 1], reads=[bc], writes=[b_wt])
            ft, b_ft = ft_rot.next()
            P.op("dve", "tensor_tensor", ft, ps[:, :], wt, ALU.mult, reads=[b_ps, b_wt], writes=[b_ft])
            P.op("dve", "tensor_tensor", rhsC[:, tt, 256:512], ft[:, 0:256], ft[:, 256:512], ALU.add, reads=[b_ft], writes=[b_rhs])
            P.op("dve", "tensor_tensor", rhsS[:, tt, 256:512], ft[:, 256:512], ft[:, 0:256], ALU.subtract, reads=[b_ft], writes=[b_rhs])
        P.pop()
        P.push()
        ck_rot = Rot([((P.alloc([128, ntt, 128], BF16), P.alloc([128, ntt, 128], BF16)), Buf()) for _ in range(2)])
        ec = P.alloc([128, 512], F32)
        es = P.alloc([128, 512], F32)
        b_ec, b_es = Buf(), Buf()
        t1 = P.alloc([128, 256], F32)
        t2 = P.alloc([128, 256], F32)
        b_t1, b_t2 = Buf(), Buf()
        pc_rot = Rot([(banks[i], bb[i]) for i in (1, 2)])
        psn_rot = Rot([(banks[i], bb[i]) for i in (3, 4)])
        for kt in range(ntt):
            (ck, sk), b_ck = ck_rot.next()
            P.dma("sp", ck, tabs["F"][0, kt], writes=[b_ck])
            P.dma("sp", sk, tabs["F"][1, kt], writes=[b_ck])
            pc, b_pc = pc_rot.next()
            psn, b_psn = psn_rot.next()
            P.mm([((pc[:, :], ck[:, tt, :], rhsC[:, tt, :]), dict(start=(tt == 0), stop=(tt == ntt - 1))) for tt in range(ntt)],
                 reads=[b_ck, b_rhs], writes=[b_pc])
            P.mm([((psn[:, :], sk[:, tt, :], rhsS[:, tt, :]), dict(start=(tt == 0), stop=(tt == ntt - 1))) for tt in range(ntt)],
                 reads=[b_ck, b_rhs], writes=[b_psn])
            P.op("act", "copy", ec, pc[:, :], reads=[b_pc], writes=[b_ec])
            P.op("act", "copy", es, psn[:, :], reads=[b_psn], writes=[b_es])
            Uc, Hre, Us, Him = ec[:, 0:256], ec[:, 256:512], es[:, 0:256], es[:, 256:512]
            P.op("dve", "tensor_tensor", t1, Hre, Uc, ALU.mult, reads=[b_ec], writes=[b_t1])
            P.op("dve", "tensor_tensor", t2, Him, Us, ALU.mult, reads=[b_es], writes=[b_t2])
            P.op("dve", "tensor_tensor", Akt[:, kt, :], t1, t2, ALU.add, reads=[b_t1, b_t2], writes=[b_AB])
            P.op("dve", "tensor_tensor", t1, Hre, Us, ALU.mult, reads=[b_ec, b_es], writes=[b_t1])
            P.op("dve", "tensor_tensor", t2, Him, Uc, ALU.mult, reads=[b_es, b_ec], writes=[b_t2])
            P.op("dve", "tensor_tensor", Bkt[:, kt, :], t1, t2, ALU.subtract, reads=[b_t1, b_t2], writes=[b_AB])
        P.pop()
        P.pop()
        P.push()
        TQ = 256
        ct_rot = Rot([((P.alloc([128, ntt, TQ], BF16), P.alloc([128, ntt, TQ], BF16)), Buf()) for _ in range(2)])
        rw_rot = Rot([(P.alloc([128, TQ + 2], F32), Buf()) for _ in range(2)])
        for (rw_, b_rw_) in rw_rot.items:
            P.op("pool", "memset", rw_, 0.0, writes=[b_rw_])
        x2c = P.alloc([128, TQ], F32)
        b_x2c = Buf()
        ta = P.alloc([128, TQ], F32)
        b_ta = Buf()
        yo_rot = Rot([(P.alloc([128, TQ], BF16), Buf()) for _ in range(2)])
        pd_rot = Rot([(banks[i], bb[i]) for i in (5, 6, 7)])
        for tq in range(L // TQ):
            (ct, stt), b_ct = ct_rot.next()
            P.dma("sp", ct, tabs["I"][0, tq], writes=[b_ct])
            P.dma("sp", stt, tabs["I"][1, tq], writes=[b_ct])
            t0 = tq * TQ
            for j in range(2):
                pd, b_pd = pd_rot.next()
                calls = []
                for kt in range(ntt):
                    calls.append(((pd[:, :TQ], Akt[:, kt, j * 128:(j + 1) * 128], ct[:, kt, :]), dict(start=(kt == 0), stop=False)))
                    calls.append(((pd[:, :TQ], Bkt[:, kt, j * 128:(j + 1) * 128], stt[:, kt, :]), dict(start=False, stop=(kt == ntt - 1))))
                P.mm(calls, reads=[b_AB, b_ct], writes=[b_pd])
                rw_, b_rw_ = rw_rot.next()
                lo = max(t0 - 1, 0)
                hi = min(t0 + TQ + 1, L)
                if lo == t0 or hi == t0 + TQ:
                    P.op("pool", "memset", rw_, 0.0, writes=[b_rw_])
                P.dma("sp", rw_[:, lo - (t0 - 1):hi - (t0 - 1)], self.pT[(2 + j) * 128:(3 + j) * 128, col0 + lo:col0 + hi], writes=[b_rw_])
                j6 = 2 + j
                P.op("dve", "tensor_scalar", x2c, rw_[:, 1:TQ + 1], cw[:, j6, 1:2], cw[:, j6, 3:4], ALU.mult, ALU.add, reads=[b_rw_, bc], writes=[b_x2c])
                P.op("dve", "scalar_tensor_tensor", x2c, rw_[:, 0:TQ], cw[:, j6, 0:1], x2c, ALU.mult, ALU.add, reads=[b_rw_, b_x2c, bc], writes=[b_x2c])
                P.op("dve", "scalar_tensor_tensor", x2c, rw_[:, 2:TQ + 2], cw[:, j6, 2:3], x2c, ALU.mult, ALU.add, reads=[b_rw_, b_x2c, bc], writes=[b_x2c])
                P.op("dve", "tensor_single_scalar", ta, u[:, j, t0:t0 + TQ], hbias[:, j:j + 1], ALU.mult, reads=[b_u, bc], writes=[b_ta])
                P.op("dve", "scalar_tensor_tensor", ta, pd[:, :TQ], 1.0 / L, ta, ALU.mult, ALU.add, reads=[b_pd, b_ta], writes=[b_ta])
                yo, b_yo = yo_rot.next()
                P.op("dve", "tensor_tensor", yo, ta, x2c, ALU.mult, reads=[b_ta, b_x2c], writes=[b_yo])
                P.dma("pool", self.ymixT[j * 128:(j + 1) * 128, col0 + t0:col0 + t0 + TQ], yo, reads=[b_yo])
        P.pop()
        P.pop("phase_hyena")

    RW0 = HY_COLS + GQA_COLS + MLA_COLS
    NCHUNK = T // 64

    def phase_rwkv_a(self, l):
        P = self.P
        bc = self.b_const
        banks, bb = P.banks, P.bank_bufs
        TB = 128
        F32R = mybir.dt.float32r

        def R(ap):
            return ap
        CDT = BF16 if RW_BF16 else F32
        DDT = BF16 if RW_DBL_BF16 else F32
        P.push()
        mu = P.alloc([128, 9], F32)
        omm = P.alloc([128, 9], F32)
        hmu = P.alloc([128, 9], F32)
        w0a0 = P.alloc([128, 2, 2, 2], F32)
        vecs = P.alloc([128, 2, 3], F32)
        omka = P.alloc([128, 2], F32)
        w2s = P.alloc([128, 256], F32)
        a2s = P.alloc([128, 256], F32)
        g2 = P.alloc([128, 256], F32)
        masks = P.alloc([128, 2, 2, 128], F32)
        ident = P.alloc([128, 128], F32)
        bd64 = P.alloc([128, 128], F32)
        bdo2 = P.alloc([128, 2], F32)
        ident_bf = P.alloc([128, 128], CDT)
        P.dma("pool", ident_bf, self.c_ident.ap(), writes=[bc])
        P.dma("sp", mu, self.rw_muT[l], writes=[bc])
        P.dma("sp", w0a0, self.rw_w0a0[l], writes=[bc])
        P.dma("sp", vecs, self.rw_vecs[l], writes=[bc])
        P.dma("sp", w2s, self.rw_w2[l].rearrange("d k c -> (d k) c"), writes=[bc])
        P.dma("sp", a2s, self.rw_a2[l].rearrange("d k c -> (d k) c"), writes=[bc])
        P.dma("sp", g2, self.rw_g2[l], writes=[bc])
        P.dma("sp", masks, self.rw_masks.ap(), writes=[bc])
        P.dma("sp", ident, self.c_ident.ap(), writes=[bc])
        P.dma("sp", bd64, self.c_bd64.ap(), writes=[bc])
        P.dma("sp", bdo2, self.c_bdo2.ap(), writes=[bc])
        P.op("dve", "tensor_scalar", omm, mu, -1.0, 1.0, ALU.mult, ALU.add, reads=[bc], writes=[bc])
        P.op("dve", "tensor_single_scalar", hmu, mu, 0.5, ALU.mult, reads=[bc], writes=[bc])
        P.op("dve", "tensor_scalar", omka, vecs[:, :, 1], -1.0, 1.0, ALU.mult, ALU.add, reads=[bc], writes=[bc])
        NBUF = 2
        blk_bufs = []
        for _ in range(NBUF):
            o = dict(raw=P.alloc([128, 9, TB + 2], F32), b_raw=Buf(), tsum=P.alloc([128, 9, TB], F32), b_tsum=Buf(),
                     sh=P.alloc([128, 9, TB], F32), b_sh=Buf(), kk=P.alloc([128, 2, TB], F32), nkk=P.alloc([128, 2, TB], F32), b_kk=Buf(),
                     act_t=P.alloc([128, 2, TB], F32), b_act=Buf(), vT=P.alloc([128, 2, 128], F32), b_vT=Buf(),
                     Vbd=P.alloc([128, 2, 2, 128], CDT), b_Vbd=Buf(), kd=P.alloc([128, 2, 2, TB], F32), b_kd=Buf(),
                     gt=P.alloc([128, 256], F32), b_gt=Buf(), bon=P.alloc([128, 4], F32), b_bon=Buf())
            P.op("pool", "memset", o["Vbd"], 0.0, writes=[o["b_Vbd"]])
            P.op("pool", "memset", o["raw"], 0.0, writes=[o["b_raw"]])
            blk_bufs.append(o)
        tmpA = Rot([(P.alloc([128, TB], F32), Buf()) for _ in range(8)])
        ctxs = {}
        for d in range(2):
            for pr in range(2):
                c = dict(lw=P.alloc([128, TB], F32), b_lw=Buf(), lwT=P.alloc([128, 128], F32), b_lwT=Buf(),
                         GE=P.alloc([128, 3, TB], F32), b_GE=Buf(), aa=P.alloc([128, TB], F32), b_aa=Buf(),
                         t1=P.alloc([128, TB], F32), b_t1=Buf(), be=P.alloc([128, TB], F32), b_be=Buf(),
                         QQ=P.alloc([128, 2, 256], CDT), b1=P.alloc([128, 2, 128], CDT), b2=P.alloc([128, 2, 128], CDT), bset=Buf(),
                         ATm=P.alloc([128, 2, 256], CDT), b_ATm=Buf(), BTm=P.alloc([128, 2, 256], CDT), b_BTm=Buf(),
                         Mt=[P.alloc([128, 2, 128], DDT) for _ in range(2)], b_Mt=[Buf(), Buf()],
                         Nt=[P.alloc([128, 2, 128], DDT) for _ in range(2)], b_Nt=[Buf(), Buf()],
                         Wt=[P.alloc([128, 2, 128], DDT) for _ in range(2)], b_Wt=[Buf(), Buf()],
                         M0b=P.alloc([128, 2, 128], DDT), b_M0b=Buf(), TTf=P.alloc([128, 2, 128], CDT), b_TTf=Buf(),
                         XQ=P.alloc([128, 2, 256], CDT), b_XQ=Buf(), TBVQ=P.alloc([128, 2, 256], CDT), b_TBVQ=Buf(),
                         b1tm=P.alloc([128, 2, 128], CDT), b_b1tm=Buf(), b2tm=P.alloc([128, 2, 128], CDT), b_b2tm=Buf(),
                         OUT=P.alloc([128, 2, 520], F32), b_OUT=Buf())
                for tl in (c["QQ"], c["b1"], c["b2"]):
                    P.op("pool", "memset", tl, 0.0, writes=[c["bset"]])
                ctxs[(d, pr)] = c
        ps_rot = Rot([(banks[i], bb[i]) for i in range(8)])
        R0 = self.RW0
        nblk = T // TB

        def ps():
            return ps_rot.next()

        def gen_dpr(d, pr, B, blk):
            c = ctxs[(d, pr)]
            kb = d * 64
            sh, b_sh, kk, nkk, b_kk, act_t, b_act = B["sh"], B["b_sh"], B["kk"], B["nkk"], B["b_kk"], B["act_t"], B["b_act"]
            Vbd, b_Vbd, kd, b_kd = B["Vbd"], B["b_Vbd"], B["kd"], B["b_kd"]
            lw, b_lw, lwT, b_lwT, GE, b_GE, aa, b_aa = c["lw"], c["b_lw"], c["lwT"], c["b_lwT"], c["GE"], c["b_GE"], c["aa"], c["b_aa"]
            QQ, b1, b2, bset = c["QQ"], c["b1"], c["b2"], c["bset"]
            ATm, b_ATm, BTm, b_BTm = c["ATm"], c["b_ATm"], c["BTm"], c["b_BTm"]
            Mt, b_Mt, Nt, b_Nt, Wt, b_Wt = c["Mt"], c["b_Mt"], c["Nt"], c["b_Nt"], c["Wt"], c["b_Wt"]
            XQ, b_XQ, TBVQ, b_TBVQ = c["XQ"], c["b_XQ"], c["TBVQ"], c["b_TBVQ"]
            b1tm, b_b1tm, b2tm, b_b2tm, OUT, b_OUT = c["b1tm"], c["b_b1tm"], c["b2tm"], c["b_b2tm"], c["OUT"], c["b_OUT"]
            pz, b_pz = ps()
            P.mm([((pz[:, 0:TB], w2s[kb:kb + 64, pr * 128:(pr + 1) * 128], act_t[kb:kb + 64, 0, :]), dict(start=True, stop=True))],
                 reads=[b_act, bc], writes=[b_pz])
            P.op("act", "activation", lw, pz[:, 0:TB], AF.Sigmoid, bias=w0a0[:, 0, d, pr:pr + 1], scale=1.0, reads=[b_pz, bc], writes=[b_lw])
            pz, b_pz = ps()
            P.mm([((pz[:, 0:TB], a2s[kb:kb + 64, pr * 128:(pr + 1) * 128], sh[kb:kb + 64, 7, :]), dict(start=True, stop=True))],
                 reads=[b_sh, bc], writes=[b_pz])
            P.op("act", "activation", aa, pz[:, 0:TB], AF.Sigmoid, bias=w0a0[:, 1, d, pr:pr + 1], scale=1.0, reads=[b_pz, bc], writes=[b_aa])
            yield
            pz, b_pz = ps()
            P.mm([((pz[:, 0:128], lw, ident), {})], reads=[b_lw, bc], writes=[b_pz], meth="transpose")
            P.op("act", "activation", lwT, pz[:, 0:128], AF.Copy, scale=-math.exp(-0.5), reads=[b_pz], writes=[b_lwT])
            P.op("dve", "tensor_scalar", c["t1"], aa, vecs[:, pr, 1:2], omka[:, pr:pr + 1], ALU.mult, ALU.add, reads=[b_aa, bc], writes=[c["b_t1"]])
            P.op("dve", "tensor_tensor", kd[:, d, pr, :], sh[:, 2 + pr, :], c["t1"], ALU.mult, reads=[b_sh, c["b_t1"]], writes=[b_kd])
            P.op("dve", "tensor_tensor", c["be"], kk[:, pr, :], aa, ALU.mult, reads=[b_kk, b_aa], writes=[c["b_be"]])
            yield
            pz, b_pz = ps()
            P.mm([((pz[:, 0:128], lwT, masks[:, d, 1, :]), dict(start=True, stop=True))], reads=[b_lwT, bc], writes=[b_pz])
            P.mm([((pz[:, 128:256], lwT, masks[:, d, 0, :]), dict(start=True, stop=True))], reads=[b_lwT, bc], writes=[b_pz])
            P.op("act", "activation", GE[:, 0:2, :], pz[:, 0:256].rearrange("p (a b) -> p a b", a=2), AF.Exp, reads=[b_pz], writes=[b_GE])
            P.op("act", "activation", GE[:, 2, :], pz[:, 0:128], AF.Exp, scale=-1.0, reads=[b_pz], writes=[b_GE])
            yield
            for hh in range(2):
                sl = slice(hh * 64, (hh + 1) * 64)

                def v3(ap):
                    return ap[sl, :].rearrange("p (c t) -> p c t", c=2)
                eng = "dve"
                P.op(eng, "tensor_tensor", QQ[sl, :, hh * 64:(hh + 1) * 64], v3(nkk[:, pr, :]), v3(GE[:, 1, :]), ALU.mult,
                     reads=[b_kk, b_GE], writes=[bset])
                P.op(eng, "tensor_tensor", QQ[sl, :, 128 + hh * 64:128 + (hh + 1) * 64], v3(sh[:, pr, :]), v3(GE[:, 0, :]), ALU.mult,
                     reads=[b_sh, b_GE], writes=[bset])
                P.op(eng, "tensor_tensor", b1[sl, :, hh * 64:(hh + 1) * 64], v3(c["be"]), v3(GE[:, 2, :]), ALU.mult, reads=[c["b_be"], b_GE], writes=[bset])
                P.op(eng, "tensor_tensor", b2[sl, :, hh * 64:(hh + 1) * 64], v3(kd[:, d, pr, :]), v3(GE[:, 2, :]), ALU.mult,
                     reads=[b_kd, b_GE], writes=[bset])
            yield
            for c2 in range(2):
                pz, b_pz = ps()
                P.mm([((pz[:, 0:256], R(b1[:, c2, :]), R(QQ[:, c2, :])), dict(start=True, stop=True))], reads=[bset], writes=[b_pz])
                P.op("dve", "tensor_tensor", ATm[:, c2, :].rearrange("p (a b) -> p a b", a=2), pz[:, 0:256].rearrange("p (a b) -> p a b", a=2),
                     masks[:, d, :, :], ALU.mult, reads=[b_pz, bc], writes=[b_ATm])
                pz, b_pz = ps()
                P.mm([((pz[:, 0:256], R(b2[:, c2, :]), R(QQ[:, c2, :])), dict(start=True, stop=True))], reads=[bset], writes=[b_pz])
                P.op("dve", "tensor_tensor", BTm[:, c2, :].rearrange("p (a b) -> p a b", a=2), pz[:, 0:256].rearrange("p (a b) -> p a b", a=2),
                     masks[:, d, :, :], ALU.mult, reads=[b_pz, bc], writes=[b_BTm])
            pz, b_pz = ps()
            for c2 in range(2):
                P.mm([((pz[:, c2 * 128:(c2 + 1) * 128], R(QQ[:, c2, 0:128]), R(b1[:, c2, :])), dict(start=True, stop=True))], reads=[bset], writes=[b_pz])
            for c2 in range(2):
                P.op("dve", "tensor_tensor", Nt[0][:, c2, :], pz[:, c2 * 128:(c2 + 1) * 128], masks[:, 1 - d, 0, :], ALU.mult,
                     reads=[b_pz, bc], writes=[b_Nt[0]])
            for (src_fn, dst, b_dst) in ((lambda c2: QQ[:, c2, 0:128], None, None), (lambda c2: b1[:, c2, :], b1tm, b_b1tm), (lambda c2: b2[:, c2, :], b2tm, b_b2tm)):
                pz, b_pz = ps()
                for c2 in range(2):
                    P.mm([((pz[:, c2 * 128:(c2 + 1) * 128], src_fn(c2), ident_bf), {})], reads=[bset, bc], writes=[b_pz], meth="transpose")
                if dst is None:
                    P.op("act", "copy", XQ[:, :, 128:256], pz[:, 0:256].rearrange("p (a b) -> p a b", a=2), reads=[b_pz], writes=[b_XQ])
                else:
                    P.op("act", "copy", dst, pz[:, 0:256].rearrange("p (a b) -> p a b", a=2), reads=[b_pz], writes=[b_dst])
            yield
            for c2 in range(2):
                P.op("dve", "tensor_tensor", Wt[1][:, c2, :], ATm[:, c2, 0:128], ident, ALU.add, reads=[b_ATm, bc], writes=[b_Wt[1]])
            P.op("pool", "tensor_copy", c["M0b"], ATm[:, :, 0:128], reads=[b_ATm], writes=[c["b_M0b"]])
            pz, b_pz = ps()
            for c2 in range(2):
                P.mm([((pz[:, c2 * 128:(c2 + 1) * 128], R(BTm[:, c2, 0:128]), R(Vbd[:, pr, c2, :])), dict(start=True, stop=True))],
                     reads=[b_BTm, b_Vbd], writes=[b_pz])
            P.op("act", "copy", XQ[:, :, 0:128], pz[:, 0:256].rearrange("p (a b) -> p a b", a=2), reads=[b_pz], writes=[b_XQ])

            def Mj(j, c2):
                return (c["M0b"][:, c2, :], c["b_M0b"]) if j == 0 else (Mt[j % 2][:, c2, :], b_Mt[j % 2])
            for jj in range(6):
                if jj >= 1:
                    pz, b_pz = ps()
                    for c2 in range(2):
                        P.mm([((pz[:, c2 * 128:(c2 + 1) * 128], R(Nt[jj % 2][:, c2, :]), R(Wt[jj % 2][:, c2, :])), dict(start=True, stop=True))],
                             reads=[b_Nt[jj % 2], b_Wt[jj % 2]], writes=[b_pz])
                    if jj == 5:
                        P.op("dve", "tensor_tensor", c["TTf"], pz[:, 0:256].rearrange("p (a b) -> p a b", a=2), Wt[jj % 2], ALU.add,
                             reads=[b_pz, b_Wt[jj % 2]], writes=[c["b_TTf"]])
                    else:
                        P.op("dve", "tensor_tensor", Wt[(jj + 1) % 2], pz[:, 0:256].rearrange("p (a b) -> p a b", a=2), Wt[jj % 2], ALU.add,
                             reads=[b_pz, b_Wt[jj % 2]], writes=[b_Wt[(jj + 1) % 2]])
                if jj < 4:
                    pz, b_pz = ps()
                    for c2 in range(2):
                        m_, b_m_ = Mj(jj, c2)
                        P.mm([((pz[:, c2 * 128:(c2 + 1) * 128], R(Nt[jj % 2][:, c2, :]), R(m_)), dict(start=True, stop=True))],
                             reads=[b_Nt[jj % 2], b_m_], writes=[b_pz])
                    P.op("act", "copy", Mt[(jj + 1) % 2], pz[:, 0:256].rearrange("p (a b) -> p a b", a=2), reads=[b_pz], writes=[b_Mt[(jj + 1) % 2]])
                if jj < 5:
                    pz, b_pz = ps()
                    for c2 in range(2):
                        m_, b_m_ = Mj(jj, c2)
                        P.mm([((pz[:, c2 * 128:(c2 + 1) * 128], R(m_), R(Nt[jj % 2][:, c2, :])), dict(start=True, stop=True))],
                             reads=[b_Nt[jj % 2], b_m_], writes=[b_pz])
                    P.op("act", "copy", Nt[(jj + 1) % 2], pz[:, 0:256].rearrange("p (a b) -> p a b", a=2), reads=[b_pz], writes=[b_Nt[(jj + 1) % 2]])
                yield
            TT, b_TT = c["TTf"], c["b_TTf"]
            for c2 in range(2):
                pz, b_pz = ps()
                P.mm([((pz[:, 0:256], R(TT[:, c2, :]), R(XQ[:, c2, :])), dict(start=True, stop=True))], reads=[b_TT, b_XQ], writes=[b_pz])
                P.op("act" if c2 == 0 else "dve", "copy" if c2 == 0 else "tensor_copy", TBVQ[:, c2, :], pz[:, 0:256], reads=[b_pz], writes=[b_TBVQ])
            yield
            pzs = [ps() for _ in range(4)]
            for c2 in range(2):
                TBV, TQT = R(TBVQ[:, c2, 0:128]), R(TBVQ[:, c2, 128:256])
                ApT, BpT = R(ATm[:, c2, 128:256]), R(BTm[:, c2, 128:256])
                V_ = R(Vbd[:, pr, c2, :])
                cs = slice(c2 * 128, (c2 + 1) * 128)
                P.mm([((pzs[0][0][:, cs], TQT, ApT), dict(start=True, stop=True))], reads=[b_TBVQ, b_ATm], writes=[pzs[0][1]])
                P.mm([((pzs[1][0][:, cs], ApT, TBV), dict(start=True, stop=False)), ((pzs[1][0][:, cs], BpT, V_), dict(start=False, stop=True))],
                     reads=[b_ATm, b_BTm, b_TBVQ, b_Vbd], writes=[pzs[1][1]])
                P.mm([((pzs[2][0][:, cs], TQT, R(b1tm[:, c2, :])), dict(start=True, stop=True))], reads=[b_TBVQ, b_b1tm], writes=[pzs[2][1]])
                P.mm([((pzs[3][0][:, cs], R(b1tm[:, c2, :]), TBV), dict(start=True, stop=False)),
                      ((pzs[3][0][:, cs], R(b2tm[:, c2, :]), V_), dict(start=False, stop=True))],
                     reads=[b_b1tm, b_b2tm, b_TBVQ, b_Vbd], writes=[pzs[3][1]])
            v2 = lambda ap: ap.rearrange("p (a b) -> p a b", a=2)
            P.op("dve", "tensor_tensor", OUT[:, :, 0:128], v2(pzs[0][0][:, 0:256]), QQ[:, :, 128:256], ALU.add, reads=[pzs[0][1], bset], writes=[b_OUT])
            P.op("act", "copy", OUT[:, :, 128:256], v2(pzs[1][0][:, 0:256]), reads=[pzs[1][1]], writes=[b_OUT])
            for c2 in range(2):
                cs = slice(c2 * 128, (c2 + 1) * 128)
                P.op("dve", "tensor_tensor", OUT[:, c2, 256:384], pzs[2][0][:, cs], ident, ALU.add, reads=[pzs[2][1], bc], writes=[b_OUT])
                gi = c2 * 64 + (63 if d == 0 else 0)
                gcol = GE[:, 0, gi:gi + 1]
                P.op("act", "activation", OUT[:, c2, 384:512], pzs[3][0][:, cs], AF.Copy, scale=gcol, reads=[pzs[3][1], b_GE], writes=[b_OUT])
                P.op("dve", "tensor_copy", OUT[:, c2, 512:513], gcol, reads=[b_GE], writes=[b_OUT])
            P.dma(STQ_RWA, self.rwA[d, blk * 2:blk * 2 + 2, pr].rearrange("c p w -> p c w"), OUT, reads=[b_OUT])
            yield

        nblk = int(os.environ.get("RWA_NBLK", nblk))
        def prep_blk(blk):
            B = blk_bufs[blk % NBUF]
            raw, b_raw, tsum, b_tsum, sh, b_sh = B["raw"], B["b_raw"], B["tsum"], B["b_tsum"], B["sh"], B["b_sh"]
            kk, nkk, b_kk, act_t, b_act, vT, b_vT, Vbd, b_Vbd = B["kk"], B["nkk"], B["b_kk"], B["act_t"], B["b_act"], B["vT"], B["b_vT"], B["Vbd"], B["b_Vbd"]
            t0 = blk * TB
            seg0, seg1 = (0, CTX) if t0 < CTX else (CTX, T)
            lo, hi = max(t0 - 1, seg0), min(t0 + TB + 1, seg1)
            if lo == t0 or hi == t0 + TB:
                P.op("pool", "memset", raw, 0.0, writes=[b_raw])
            P.dma("sp", raw[:, :, lo - (t0 - 1):hi - (t0 - 1)],
                  self.pT[R0:R0 + 1152, lo:hi].rearrange("(c p) t -> p c t", p=128), writes=[b_raw])
            yield
            P.op("dve", "tensor_tensor", tsum, raw[:, :, 0:TB], raw[:, :, 2:TB + 2], ALU.add, reads=[b_raw], writes=[b_tsum])
            for c in range(9):
                P.op("dve", "tensor_single_scalar", sh[:, c, :], raw[:, c, 1:TB + 1], omm[:, c:c + 1], ALU.mult, reads=[b_raw, bc], writes=[b_sh])
            yield
            for c in range(9):
                P.op("dve", "scalar_tensor_tensor", sh[:, c, :], tsum[:, c, :], hmu[:, c:c + 1], sh[:, c, :], ALU.mult, ALU.add,
                     reads=[b_tsum, b_sh, bc], writes=[b_sh])
            yield
            kqs = []
            for pr in range(2):
                kq, b_kq = tmpA.next()
                sq, b_sq = tmpA.next()
                kqs.append((kq, b_kq, sq, b_sq))
                P.op("dve", "tensor_single_scalar", kq, sh[:, 2 + pr, :], vecs[:, pr, 0:1], ALU.mult, reads=[b_sh, bc], writes=[b_kq])
            P.op("act", "activation", act_t[:, 0, :], sh[:, 6, :], AF.Tanh, reads=[b_sh], writes=[b_act])
            P.op("act", "activation", act_t[:, 1, :], sh[:, 8, :], AF.Sigmoid, reads=[b_sh], writes=[b_act])
            yield
            pzv, b_pzv = ps()
            for pr in range(2):
                P.mm([((pzv[:, pr * 128:(pr + 1) * 128], sh[:, 4 + pr, :], ident), {})], reads=[b_sh, bc], writes=[b_pzv], meth="transpose")
            P.op("act", "copy", vT, pzv[:, 0:256].rearrange("p (a b) -> p a b", a=2), reads=[b_pzv], writes=[b_vT])
            for pr in range(2):
                kq, b_kq, sq, b_sq = kqs[pr]
                P.op("act", "activation", sq, kq, AF.Square, reads=[b_kq], writes=[b_sq])
            yield
            pzg, b_pzg = ps()
            P.mm([((pzg[:, 0:256], act_t[:, 1, :], g2), dict(start=True, stop=True))], reads=[b_act, bc], writes=[b_pzg])
            P.op("act", "copy", B["gt"], pzg[:, 0:256], reads=[b_pzg], writes=[B["b_gt"]])
            for pr in range(2):
                kq, b_kq, sq, b_sq = kqs[pr]
                pz, b_pz = ps()
                P.mm([((pz[:, 0:TB], bd64, sq), dict(start=True, stop=True))], reads=[b_sq, bc], writes=[b_pz])
                P.op("act", "activation", sq, pz[:, 0:TB], AF.Ln, bias=1e-24, scale=1.0, reads=[b_pz], writes=[b_sq])
            P.dma(STQ_RWA, self.rw_gtm[t0:t0 + TB, :], B["gt"], reads=[B["b_gt"]])
            P.dma(STQ_RWA, self.rw_vtm[t0:t0 + TB, :], vT.rearrange("p a b -> p (a b)"), reads=[b_vT])
            yield
            for pr in range(2):
                kq, b_kq, sq, b_sq = kqs[pr]
                P.op("act", "activation", sq, sq, AF.Exp, scale=-0.5, reads=[b_sq], writes=[b_sq])
            for c2 in range(2):
                for hh in range(2):
                    P.op("dve", "tensor_copy", Vbd[hh * 64:(hh + 1) * 64, :, c2, hh * 64:(hh + 1) * 64],
                         vT[c2 * 64:(c2 + 1) * 64, :, hh * 64:(hh + 1) * 64], reads=[b_vT], writes=[b_Vbd])
            yield
            for pr in range(2):
                kq, b_kq, sq, b_sq = kqs[pr]
                P.op("dve", "tensor_tensor", kk[:, pr, :], kq, sq, ALU.mult, reads=[b_kq, b_sq], writes=[b_kk])
                P.op("dve", "scalar_tensor_tensor", nkk[:, pr, :], kq, -1.0, sq, ALU.mult, ALU.mult, reads=[b_kq, b_sq, b_kk], writes=[b_kk])
            yield

        def chains_blk(blk):
            B = blk_bufs[blk % NBUF]
            sh, b_sh = B["sh"], B["b_sh"]
            t0 = blk * TB
            gens = [gen_dpr(d, pr, B, blk) for d in range(2) for pr in range(2)]
            nxt = prep_blk(blk + 1) if blk + 1 < nblk else None
            while gens:
                for g_ in list(gens):
                    try:
                        next(g_)
                    except StopIteration:
                        gens.remove(g_)
                if nxt is not None:
                    try:
                        next(nxt)
                    except StopIteration:
                        nxt = None
            if nxt is not None:
                for _ in nxt:
                    pass
            pz, b_pz = ps()
            for pr in range(2):
                ks, b_ks = tmpA.next()
                P.op("dve", "tensor_tensor", ks, B["kd"][:, 0, pr, :], B["kd"][:, 1, pr, :], ALU.add, reads=[B["b_kd"]], writes=[b_ks])
                P.op("dve", "scalar_tensor_tensor", ks, sh[:, pr, :], vecs[:, pr, 2:3], ks, ALU.mult, ALU.mult, reads=[b_sh, b_ks, bc], writes=[b_ks])
                P.mm([((pz[:, 2 * pr:2 * pr + 2], ks, bdo2), dict(start=True, stop=True))], reads=[b_ks, bc], writes=[b_pz])
            P.op("dve", "tensor_copy", B["bon"], pz[:, 0:4], reads=[b_pz], writes=[B["b_bon"]])
            P.dma(STQ_RWA, self.rw_bon[t0:t0 + TB, :], B["bon"], reads=[B["b_bon"]])

        for _ in prep_blk(0):
            pass
        for blk in range(nblk):
            chains_blk(blk)
        P.pop("phase_rwkv_a")

    def rwkv_bc_gen(self, l, ctx_out, bank_ids):
        P = self.P
        banks, bb = P.banks, P.bank_bufs
        OUTW = 520
        NB = 8
        in_rot = Rot([(P.alloc([128, OUTW], F32), Buf()) for _ in range(NB)])
        st = {}
        for d in range(2):
            for pr in range(2):
                tl = [(P.alloc([128, 128], F32), Buf()) for _ in range(2)]
                P.op("pool", "memset", tl[0][0], 0.0, writes=[tl[0][1]])
                st[(d, pr)] = [tl, 0]
        yo_rot = Rot([(P.alloc([128, 128], F32), Buf()) for _ in range(4)])
        ps_rot = Rot([(banks[i], bb[i]) for i in bank_ids])
        order = {0: list(range(self.NCHUNK)), 1: [3, 2, 1, 0] + list(range(self.NCHUNK - 1, 3, -1))}
        its = [(s_, d, pr, order[d][s_]) for s_ in range(self.NCHUNK) for d in range(2) for pr in range(2)]
        LA = NB - 2
        loaded = []

        def load(i):
            (_, d, pr, j) = its[i]
            it, b_it = in_rot.next()
            P.dma("sp", it, self.rwA[d, j, pr], writes=[b_it])
            loaded.append((it, b_it))
        for i in range(min(LA, len(its))):
            load(i)
        for i, (s_, d, pr, j) in enumerate(its):
            if i + LA < len(its):
                load(i + LA)
            it, b_it = loaded[i]
            tl, cur = st[(d, pr)]
            Pc, b_Pc = tl[cur]
            Pn, b_Pn = tl[1 - cur]
            pz, b_pz = ps_rot.next()
            P.mm([((pz[:, 0:128], it[:, 0:128], Pc), dict(start=True, stop=True))], reads=[b_it, b_Pc], writes=[b_pz])
            yo, b_yo = yo_rot.next()
            P.op("dve", "tensor_tensor", yo, pz[:, 0:128], it[:, 128:256], ALU.add, reads=[b_pz, b_it], writes=[b_yo])
            for hh in range(2):
                hcol = (pr * 2 + hh) * 64
                P.dma("pool", self.rw_y[d, j * 64:(j + 1) * 64, hcol:hcol + 64], yo[hh * 64:(hh + 1) * 64, hh * 64:(hh + 1) * 64], reads=[b_yo])
            pz, b_pz = ps_rot.next()
            P.mm([((pz[:, 0:128], it[:, 256:384], Pc), dict(start=True, stop=True))], reads=[b_it, b_Pc], writes=[b_pz])
            P.op("dve", "scalar_tensor_tensor", Pn, pz[:, 0:128], it[:, 512:513], it[:, 384:512], ALU.mult, ALU.add,
                 reads=[b_pz, b_it], writes=[b_Pn])
            st[(d, pr)][1] = 1 - cur
            if i % 4 == 3:
                yield
        P.barrier()
        bc = self.b_const
        lnw = P.alloc([128, 2, 256], F32)
        ident = P.alloc([128, 128], F32)
        P.dma("sp", lnw, self.rw_ln[l:l + 1].rearrange("o a c -> o (a c)").partition_broadcast(128), writes=[bc])
        P.dma("sp", ident, self.c_ident.ap(), writes=[bc])
        TB = 128
        rot = lambda shape, n=2, dt=F32: Rot([(P.alloc(shape, dt), Buf()) for _ in range(n)])
        yf_r, yb_r, v_r, g_r, bo_r = rot([128, 256], 3), rot([128, 256], 3), rot([128, 256], 3), rot([128, 256], 3), rot([128, 4], 3)
        y_r, sq_r, st_r, o_r = rot([128, 256]), rot([128, 256]), rot([128, 4, 4]), rot([128, 256], 2, BF16)
        ps_rot = Rot([(banks[i], bb[i]) for i in bank_ids])
        blk0 = 0 if ctx_out else CTX // TB
        ldd = {}

        def loadc(blk):
            t0 = blk * TB
            yf, b_yf = yf_r.next(); yb, b_yb = yb_r.next(); vv, b_vv = v_r.next(); gg, b_gg = g_r.next(); bo, b_bo = bo_r.next()
            P.dma("sp", yf, self.rw_y[0, t0:t0 + TB, :], writes=[b_yf])
            P.dma("sp", yb, self.rw_y[1, t0:t0 + TB, :], writes=[b_yb])
            P.dma("sp", vv, self.rw_vtm[t0:t0 + TB, :], writes=[b_vv])
            P.dma("sp", gg, self.rw_gtm[t0:t0 + TB, :], writes=[b_gg])
            P.dma("sp", bo, self.rw_bon[t0:t0 + TB, :], writes=[b_bo])
            ldd[blk] = (yf, b_yf, yb, b_yb, vv, b_vv, gg, b_gg, bo, b_bo)
        loadc(blk0)
        for blk in range(blk0, T // TB):
            t0 = blk * TB
            if blk + 1 < T // TB:
                loadc(blk + 1)
            yf, b_yf, yb, b_yb, vv, b_vv, gg, b_gg, bo, b_bo = ldd.pop(blk)
            y, b_y = y_r.next(); sq, b_sq = sq_r.next(); stt, b_stt = st_r.next()
            P.op("dve", "tensor_tensor", y, yf, yb, ALU.add, reads=[b_yf, b_yb], writes=[b_y])
            y3 = y.rearrange("p (h n) -> p h n", h=4)
            P.op("act", "activation", sq, y, AF.Square, reads=[b_y], writes=[b_sq])
            P.op("dve", "tensor_reduce", stt[:, 0, :], y3, AX.X, ALU.add, reads=[b_y], writes=[b_stt])
            P.op("dve", "tensor_reduce", stt[:, 1, :], sq.rearrange("p (h n) -> p h n", h=4), AX.X, ALU.add, reads=[b_sq, b_stt], writes=[b_stt])
            P.op("dve", "tensor_single_scalar", stt[:, 0, :], stt[:, 0, :], 1.0 / 64, ALU.mult, reads=[b_stt], writes=[b_stt])
            P.op("dve", "tensor_tensor", stt[:, 2, :], stt[:, 0, :], stt[:, 0, :], ALU.mult, reads=[b_stt], writes=[b_stt])
            P.op("dve", "scalar_tensor_tensor", stt[:, 1, :], stt[:, 1, :], 1.0 / 64, stt[:, 2, :], ALU.mult, ALU.subtract, reads=[b_stt], writes=[b_stt])
            P.op("act", "activation", stt[:, 3, :], stt[:, 1, :], AF.Ln, bias=64e-5, scale=1.0, reads=[b_stt], writes=[b_stt])
            P.op("act", "activation", stt[:, 3, :], stt[:, 3, :], AF.Exp, scale=-0.5, reads=[b_stt], writes=[b_stt])
            for h in range(4):
                P.op("dve", "tensor_scalar", y3[:, h, :], y3[:, h, :], stt[:, 0, h:h + 1], stt[:, 3, h:h + 1], ALU.subtract, ALU.mult,
                     reads=[b_y, b_stt], writes=[b_y])
            P.op("dve", "tensor_tensor", y, y, lnw[:, 0, :], ALU.mult, reads=[b_y, bc], writes=[b_y])
            P.op("dve", "tensor_tensor", y, y, lnw[:, 1, :], ALU.add, reads=[b_y, bc], writes=[b_y])
            v3 = vv.rearrange("p (h n) -> p h n", h=4)
            for h in range(4):
                P.op("dve", "scalar_tensor_tensor", y3[:, h, :], v3[:, h, :], bo[:, h:h + 1], y3[:, h, :], ALU.mult, ALU.add,
                     reads=[b_vv, b_bo, b_y], writes=[b_y])
            P.op("dve", "tensor_tensor", y, y, gg, ALU.mult, reads=[b_y, b_gg], writes=[b_y])
            o, b_o = o_r.next()
            for pr in range(2):
                pz, b_pz = ps_rot.next()
                P.mm([((pz[:, 0:128], y[:, pr * 128:(pr + 1) * 128], ident), {})], reads=[b_y, bc], writes=[b_pz], meth="transpose")
                P.op("act", "copy", o[:, pr * 128:(pr + 1) * 128], pz[:, 0:128], reads=[b_pz], writes=[b_o])
                P.dma("pool", self.ymixT[768 + pr * 128:768 + (pr + 1) * 128, t0:t0 + TB], o[:, pr * 128:(pr + 1) * 128], reads=[b_o])
            yield

    def phase_rwkv_bc(self, l, ctx_out):
        self.P.push()
        for _ in self.rwkv_bc_gen(l, ctx_out, tuple(range(8))):
            pass
        self.P.pop("phase_rwkv_bc")

    def phase_out(self, l, src, dst, tiles=None):
        P = self.P
        P.push()
        wo = P.alloc([128, KC, D], BF16)
        b_wo = Buf()
        wsrc = self.w_out[l].rearrange("(kc p) n -> p kc n", p=128)
        for kc in range(KC):
            P.dma("pool", wo[:, kc, :], wsrc[:, kc, :], writes=[b_wo])
        xrot = Rot([(P.alloc([128, KC, NT], F32), Buf()) for _ in range(2)])
        yrot = Rot([(P.alloc([128, KC, NT], BF16), Buf()) for _ in range(2)])
        banks, bb = P.banks, P.bank_bufs
        pd_rot = Rot([(banks[i], bb[i]) for i in (1, 2, 3, 4)])
        srcv = src.rearrange("(kc p) t -> p kc t", p=128)
        dstv = dst.rearrange("(kc p) t -> p kc t", p=128)
        ymv = self.ymixT.rearrange("(kc p) t -> p kc t", p=128)
        gate = self.der[l]
        for (t0, n) in (tiles or TILES):
            s = 1 if t0 < CTX else 0
            xt, b_xt = xrot.next()
            yt, b_yt = yrot.next()
            P.dma("sp", xt[:, :, :n], srcv[:, :, t0:t0 + n], writes=[b_xt])
            P.dma("sp", yt[:, :, :n], ymv[:, :, t0:t0 + n], writes=[b_yt])
            for dc in range(KC):
                pd, b_pd = pd_rot.next()
                P.mm([((pd[:, :n], wo[:, kc, dc * 128:(dc + 1) * 128], yt[:, kc, :n]), dict(start=(kc == 0), stop=(kc == KC - 1)))
                      for kc in range(KC)], reads=[b_wo, b_yt], writes=[b_pd])
                P.op("dve", "scalar_tensor_tensor", xt[:, dc, :n], pd[:, :n], gate[:, 5, dc, s:s + 1], xt[:, dc, :n],
                     ALU.mult, ALU.add, reads=[b_pd, b_xt, self.b_der], writes=[b_xt])
            P.dma("pool", dstv[:, :, t0:t0 + n], xt[:, :, :n], reads=[b_xt])
        P.pop("phase_out")


def host_layout(inputs, b):
    m = {}
    x = inputs["x"][b]
    ctx = inputs["ctx"][b]
    m["xT"] = np.ascontiguousarray(np.concatenate([ctx, x], axis=0).T)
    cv = np.stack([inputs["c"][b], inputs["c_ctx"]], axis=-1)
    m["cvec"] = np.ascontiguousarray(cv.reshape(KC, 128, 2).transpose(1, 0, 2))
    return m


def rope_consts():
    tl = np.arange(SEQ)
    row = (tl // 64).astype(np.float64)
    col = (tl % 64).astype(np.float64)

    def table(n_freq, dims, lead):
        inv = 10000.0 ** (-np.arange(n_freq, dtype=np.float64) / n_freq)
        cos = np.ones((lead + dims, T), np.float64)
        sin = np.zeros((lead + dims, T), np.float64)
        for d in range(dims):
            m = d // 2
            ang = row * inv[m] if m < n_freq else col * inv[m - n_freq]
            cos[lead + d, CTX:] = np.cos(ang)
            sin[lead + d, CTX:] = np.sin(ang)
        return cos, sin
    cg, sg = table(16, 64, 0)
    rope_g = np.stack([np.tile(cg, (2, 1)), np.tile(sg, (2, 1))]).astype(np.float32)
    cm, sm = table(8, 32, 64)
    rope_m = np.stack([cm, sm]).astype(np.float32)
    rotm = np.zeros((128, 128), np.float32)
    for m in range(64):
        rotm[2 * m + 1, 2 * m] = -1.0
        rotm[2 * m, 2 * m + 1] = 1.0
    rot96 = np.zeros((96, 96), np.float32)
    for m in range(16):
        j0 = 64 + 2 * m
        rot96[j0 + 1, j0] = -1.0
        rot96[j0, j0 + 1] = 1.0
    bd64 = np.zeros((128, 128), np.float32)
    bd64[:64, :64] = 1.0
    bd64[64:, 64:] = 1.0
    return dict(rope_g=rope_g, rope_m=rope_m, c_rotm=rotm, c_rot96=rot96, c_bd64=bd64)


_HY_CACHE = {}


def hyena_consts():
    if _HY_CACHE:
        return _HY_CACHE
    m = {}
    m["c_ident"] = np.eye(128, dtype=np.float32)
    max_decay = math.log(1e-2) / 0.3
    min_decay = math.log(1e-2) / 1.5
    deltas = np.abs(np.linspace(min_decay, max_decay, HY_CH, dtype=np.float32))
    m["hy_ndelta"] = np.ascontiguousarray(np.broadcast_to(-np.tile(deltas, 2)[None, :], (128, 512))).astype(np.float32)
    for L in (SEQ, CTX):
        ntt = L // 128
        t01 = np.linspace(0.0, 1.0, L, dtype=np.float32)
        bands = 16
        w_ang = (np.float32(2.0 * math.pi) * np.arange(L, dtype=np.float32) / np.float32(L)).astype(np.float32)
        f = np.linspace(1e-4, bands - 1, bands, dtype=np.float32)
        arg = (f[None, :] * w_ang[:, None]).astype(np.float32)
        z = np.concatenate([t01[:, None], np.cos(arg), -np.sin(arg)], axis=-1).astype(np.float32)
        m[f"hy_z{L}"] = np.ascontiguousarray(z.T)
        m[f"hy_t01_{L}"] = np.ascontiguousarray(t01.reshape(ntt, 128).T)
        N = 2 * L
        t = np.arange(L, dtype=np.int64)
        kk = np.arange(L, dtype=np.int64)
        ph = ((2 * kk[None, :] + 1) * t[:, None]) % (2 * N)
        ang = ph.astype(np.float64) * (math.pi / N)
        mats = [np.cos(ang).astype(ml_dtypes.bfloat16), np.sin(ang).astype(ml_dtypes.bfloat16)]
        del ang, ph
        F = np.stack([M.reshape(ntt, 128, ntt, 128).transpose(2, 1, 0, 3) for M in mats])
        m[f"dftF{L}"] = np.ascontiguousarray(F)
        I = np.stack([M.reshape(L // 256, 256, ntt, 128).transpose(0, 3, 2, 1) for M in mats])
        m[f"dftI{L}"] = np.ascontiguousarray(I)
    _HY_CACHE.update(m)
    return _HY_CACHE


def host_shared(inputs):
    m = {}
    m.update(hyena_consts())
    cwv = np.concatenate([inputs["hy_conv_w"], inputs["hy_conv_b"][:, None, :]], axis=1)
    m["hy_cw"] = np.ascontiguousarray(cwv.reshape(DEPTH, 4, 6, 128).transpose(0, 3, 2, 1))
    m["hy_biasT"] = np.ascontiguousarray(inputs["hy_bias"].reshape(DEPTH, 2, 128).transpose(0, 2, 1))
    m["hy_fb"] = np.ascontiguousarray(np.stack([inputs["hy_f_b1"], inputs["hy_f_b2"], inputs["hy_f_b3"], inputs["hy_f_freq"]], axis=-1))
    for nm in ("hy_f_w1", "hy_f_w2", "hy_f_w3", "hy_f_w4", "rw_w2", "rw_a2", "rw_g2"):
        m[nm] = inputs[nm]
    m["rw_muT"] = np.ascontiguousarray(inputs["rw_mu"].reshape(DEPTH, 9, 128).transpose(0, 2, 1))
    wa = np.stack([inputs["rw_w0"], inputs["rw_a0"]], axis=1)
    m["rw_w0a0"] = np.ascontiguousarray(wa.reshape(DEPTH, 2, 2, 2, 128).transpose(0, 4, 1, 2, 3))
    vv = np.stack([inputs["rw_k_k"], inputs["rw_k_a"], inputs["rw_r_k"].reshape(DEPTH, 256)], axis=-1)
    m["rw_vecs"] = np.ascontiguousarray(vv.reshape(DEPTH, 2, 128, 3).transpose(0, 2, 1, 3))
    m["rw_ln"] = np.ascontiguousarray(np.stack([inputs["rw_ln_w"], inputs["rw_ln_b"]], axis=1))
    i_ = np.arange(64)
    S_f = (i_[:, None] < i_[None, :]).astype(np.float32)
    I_f = (i_[:, None] <= i_[None, :]).astype(np.float32)
    mk = np.zeros((128, 2, 2, 128), np.float32)
    for dd, (S_, I_) in enumerate(((S_f, I_f), (S_f.T, I_f.T))):
        for hb in range(2):
            mk[hb * 64:(hb + 1) * 64, dd, 0, hb * 64:(hb + 1) * 64] = S_
            mk[hb * 64:(hb + 1) * 64, dd, 1, hb * 64:(hb + 1) * 64] = I_
    m["rw_masks"] = mk
    bo2 = np.zeros((128, 2), np.float32)
    bo2[:64, 0] = 1.0
    bo2[64:, 1] = 1.0
    m["c_bdo2"] = bo2
    m["adab"] = np.ascontiguousarray(inputs["ada_b"].reshape(DEPTH, 72, 128).transpose(0, 2, 1))
    nr = np.stack([inputs["norm_ffn1"], inputs["norm_mix"], inputs["norm_ffn2"]], axis=1)
    m["norms"] = np.ascontiguousarray(nr.reshape(DEPTH, 3, KC, 128).transpose(0, 1, 3, 2))
    m["ada_w"] = inputs["ada_w"]
    for nm in ("ffn1_gate", "ffn1_up", "ffn1_down", "ffn2_gate", "ffn2_up", "ffn2_down", "w_out", "mla_w_uq", "mla_w_ukv"):
        m[nm] = inputs[nm]
    w_in = inputs["w_in"].copy()
    q0 = HY_COLS
    qc = inputs["w_in"][:, :, q0:q0 + 256].reshape(DEPTH, D, 4, 64)
    w_in[:, :, q0:q0 + 256] = qc[:, :, [0, 2, 1, 3], :].reshape(DEPTH, D, 256)
    m["w_in"] = w_in
    m.update(rope_consts())
    m["gqa_gain"] = np.ascontiguousarray(np.stack([np.tile(inputs["gqa_q_norm"], (1, 2)), np.tile(inputs["gqa_k_norm"], (1, 2))], axis=-1))
    mg = np.zeros((DEPTH, 128, 5), np.float32)
    mg[:, :, 0] = inputs["mla_cq_norm"][:, 0:128]
    mg[:, :, 1] = inputs["mla_cq_norm"][:, 128:256]
    mg[:, :, 2] = inputs["mla_ckv_norm"]
    mg[:, 0:96, 3] = inputs["mla_q_norm"]
    mg[:, 0:96, 4] = inputs["mla_k_norm"]
    m["mla_gain"] = mg
    return m


def build(dbg=False, stop=None):
    if stop == "rwa_only":
        k = K(dbg=dbg, pT_in=True)
        k.phase_rwkv_a(0)
        return k, k.P.finish()
    k = K(dbg=dbg)
    k.phase_mod()
    if stop in ("proj", "rw", "hy", "attn"):
        k.phase_ffn(0, 1, k.xT_in, k.xs)
        k.phase_proj(0, k.xs)
        if stop == "rw":
            k.phase_rwkv_a(0)
            if not os.environ.get("RWA_ONLY"):
                k.phase_rwkv_bc(0, True)
        if stop == "hy":
            k.phase_hyena(0, SEQ, CTX)
            k.phase_hyena(0, CTX, 0)
        if stop == "attn":
            k.phase_gqa(0, True)
            k.phase_mla(0, True)
        return k, k.P.finish()
    lat_tiles = [tl for tl in TILES if tl[0] >= CTX]
    for l in range(DEPTH):
        last = (l == DEPTH - 1)
        ctx_out = not last
        k.phase_ffn(l, 1, k.xT_in if l == 0 else k.xs, k.xs)
        k.phase_proj(l, k.xs)
        k.phase_hyena(l, SEQ, CTX)
        if ctx_out:
            k.phase_hyena(l, CTX, 0)
        k.phase_rwkv_a(l)
        k.phase_gqa(l, ctx_out, co_rwkv=True)
        k.phase_mla(l, ctx_out)
        k.phase_out(l, k.xs, k.xs, tiles=None if ctx_out else lat_tiles)
        if last:
            k.phase_ffn(l, 2, k.xs, k.xs, dst_lat=k.out, tiles=lat_tiles)
        else:
            k.phase_ffn(l, 2, k.xs, k.xs)
    nc = k.P.finish()
    return k, nc


def kernel(**inputs):
    inputs = {k_: np.asarray(v) for k_, v in inputs.items()}
    k, nc = build()
    shared = host_shared(inputs)
    in_maps = []
    for b in range(8):
        m = dict(shared)
        m.update(host_layout(inputs, b))
        in_maps.append(m)
    res = run_bass_kernel_spmd(nc, in_maps, core_ids=list(range(8)))
    out = np.stack([np.ascontiguousarray(r["outT"].T) for r in res.results], axis=0)
    return out.astype(np.float32)
```

```python
import contextlib
import math
import numpy as np
import ml_dtypes
import concourse.bass as bass
import concourse.mybir as mybir
from concourse.bass_utils import run_bass_kernel_spmd

F32 = mybir.dt.float32
BF16 = mybir.dt.bfloat16
ALU = mybir.AluOpType
AF = mybir.ActivationFunctionType
AX = mybir.AxisListType

D = 1024
KC = 8
SEQ = 4096
CTX = 256
T = SEQ + CTX
DEPTH = 2
D_FF = 2816
NF = D_FF // 128
N_MOD = 9
EPS = 1e-6
HY_CH = 256
HY_COLS = 768
GQA_COLS = 512
MLA_COLS = 416
RW_COLS = 1152
D_IN = 2848
NT = 256
TILES = [(t0, NT) for t0 in range(0, T, NT)]

COMPUTE = ("pe", "dve", "act", "pool")
QUEUES = ("sp", "act", "pool")
RING = 16
import os
STQ_RWA = os.environ.get('STQ_RWA', 'pool')
CUT = int(os.environ.get('RWA_CUT', '99'))
RW_BF16 = bool(int(os.environ.get('RW_BF16', '0')))
RW_DBL_BF16 = bool(int(os.environ.get('RW_DBL_BF16', '0')))
DEBUG = False


class Buf:
    __slots__ = ("name", "w", "r")

    def __init__(self, name=""):
        self.name = name
        self.w = None
        self.r = []


class Prog:
    def __init__(self, same_engine_sync=True):
        self.nc = bass.Bass("TRN2", target_bir_lowering=False)
        self.stack = contextlib.ExitStack()
        self.ops = {e: [] for e in ("pe", "dve", "act", "pool", "sp")}
        self.cnt = {e: 0 for e in COMPUTE}
        self.dma_i = {q: 0 for q in QUEUES}
        self.waited = {e: {} for e in self.ops}
        self.same_engine_sync = same_engine_sync
        self.semkeys = set()
        self.AW = 50688
        self.arena = self.stack.enter_context(self.nc.sbuf_tensor("arena", [128, self.AW], F32))
        self.arena_bf = self.arena.bitcast(BF16)
        self.off = 0
        self.marks = []
        self.phase_log = []
        self.banks = [self.stack.enter_context(self.nc.psum_tensor(f"bank{i}", [128, 512], F32)) for i in range(8)]
        self.bank_bufs = [Buf(f"bank{i}") for i in range(8)]

    def alloc(self, shape, dtype=F32):
        p = shape[0]
        n = int(np.prod(shape[1:]))
        words = n if dtype == F32 else (n + 1) // 2
        words = (words + 7) // 8 * 8
        off = self.off
        self.off += words
        assert self.off <= self.AW, f"SBUF arena overflow {self.off}"
        if dtype == F32:
            ap = self.arena[0:p, off:off + n]
        else:
            ap = self.arena_bf[0:p, 2 * off:2 * off + n]
        if len(shape) == 3:
            ap = ap.rearrange("p (a b) -> p a b", a=shape[1])
        elif len(shape) == 4:
            ap = ap.rearrange("p (a b c) -> p a b c", a=shape[1], b=shape[2])
        elif len(shape) == 5:
            ap = ap.rearrange("p (a b c d) -> p a b c d", a=shape[1], b=shape[2], c=shape[3])
        return ap

    def push(self):
        self.marks.append(self.off)

    def pop(self, label=None):
        self.barrier()
        self.off = self.marks.pop()
        if label:
            self.phase_log.append((label, {e: sum(1 for it in self.ops[e] if it[0] == "op" and it[2] == e) for e in COMPUTE}))

    def dram(self, name, shape, dtype=F32, kind="Internal"):
        return self.nc.dram_tensor(name, list(shape), dtype, kind=kind)

    def _need(self, eng, tok):
        if tok is None:
            return
        key, val = tok
        if key == eng and not self.same_engine_sync:
            return
        cur = self.waited[eng].get(key, 0)
        if cur >= val:
            return
        self.waited[eng][key] = val
        self.ops[eng].append(("wait", key, val))

    def _deps(self, eng, reads, writes, pe_accum=False, is_dma=False):
        for b in reads:
            self._need(eng, b.w)
        for b in writes:
            if b.w is not None and (is_dma or b.w[0] != eng) and not (pe_accum and b.w[0] == "pe"):
                self._need(eng, b.w)
            for t in b.r:
                if is_dma or t[0] != eng:
                    self._need(eng, t)

    def _mark(self, tok, reads, writes):
        for b in reads:
            b.r.append(tok)
            if len(b.r) > 16:
                best = {}
                for k, v in b.r:
                    if best.get(k, 0) < v:
                        best[k] = v
                b.r = list(best.items())
        for b in writes:
            b.w = tok
            b.r = []

    def op(self, eng, meth, *args, reads=(), writes=(), **kw):
        self._deps(eng, reads, writes)
        self.cnt[eng] += 1
        tok = (eng, self.cnt[eng])
        self.semkeys.add(eng)
        self.ops[eng].append(("op", (meth, args, kw), eng, 1))
        self._mark(tok, reads, writes)
        return tok

    def mm(self, calls, reads=(), writes=(), meth="matmul"):
        self._deps("pe", reads, writes, pe_accum=True)
        for (a, k) in calls[:-1]:
            self.ops["pe"].append(("op", (meth, a, k), None, 0))
        self.cnt["pe"] += 1
        tok = ("pe", self.cnt["pe"])
        self.semkeys.add("pe")
        a, k = calls[-1]
        self.ops["pe"].append(("op", (meth, a, k), "pe", 1))
        self._mark(tok, reads, writes)
        return tok

    def dma(self, q, out, in_, reads=(), writes=(), **kw):
        eng = q
        self._deps(eng, reads, writes, is_dma=True)
        i = self.dma_i[q]
        self.dma_i[q] += 1
        key = ("dma", q, i % RING)
        val = 16 * (i // RING + 1)
        if i >= RING:
            self._need(eng, (key, val - 16))
        self.semkeys.add(key)
        self.ops[eng].append(("op", ("dma_start", (out, in_), kw), key, 16))
        tok = (key, val)
        self._mark(tok, reads, writes)
        return tok

    def barrier(self):
        toks = [(e, self.cnt[e]) for e in COMPUTE if self.cnt[e] > 0]
        for q in QUEUES:
            n = self.dma_i[q]
            for j in range(max(0, n - RING), n):
                toks.append((("dma", q, j % RING), 16 * (j // RING + 1)))
        for e in self.ops:
            for t in toks:
                self._need(e, t)

    def finish(self):
        self.barrier()
        nc = self.nc
        sems = {}
        for key in sorted(self.semkeys, key=str):
            nm = key if isinstance(key, str) else f"d_{key[1]}_{key[2]}"
            sems[key] = self.stack.enter_context(nc.semaphore("s_" + nm))
        ops = self.ops

        def emit(e, lst):
            for it in lst:
                if it[0] == "wait":
                    e.wait_ge(sems[it[1]], it[2])
                else:
                    meth, a, k = it[1]
                    ins = getattr(e, meth)(*a, **k)
                    if it[2] is not None:
                        ins.then_inc(sems[it[2]], it[3])

        with nc.Block() as block:
            @block.sync
            def _(e):
                emit(e, ops["sp"])

            @block.tensor
            def _(e):
                emit(e, ops["pe"])

            @block.vector
            def _(e):
                emit(e, ops["dve"])

            @block.scalar
            def _(e):
                emit(e, ops["act"])

            @block.gpsimd
            def _(e):
                emit(e, ops["pool"])
        self.stack.close()
        return nc

    def stats(self):
        return ({e: sum(1 for it in l if it[0] == "op") for e, l in self.ops.items()},
                {e: sum(1 for it in l if it[0] == "wait") for e, l in self.ops.items()})


class Rot:
    def __init__(self, items):
        self.items = items
        self.i = 0

    def next(self):
        it = self.items[self.i % len(self.items)]
        self.i += 1
        return it


class K:
    def __init__(self, dbg=False, pT_in=False):
        self.P = P = Prog()
        self.dbg = dbg
        kin = "ExternalInput"
        sk = "ExternalOutput" if dbg else "Internal"
        self.xT_in = P.dram("xT", [D, T], F32, kin)
        self.cvec = P.dram("cvec", [128, KC, 2], F32, kin)
        self.adab = P.dram("adab", [DEPTH, 128, 72], F32, kin)
        self.norms = P.dram("norms", [DEPTH, 3, 128, KC], F32, kin)
        self.ada_w = P.dram("ada_w", [DEPTH, D, N_MOD * D], F32, kin)
        self.w_ffn = {}
        for nm in ("ffn1_gate", "ffn1_up", "ffn2_gate", "ffn2_up"):
            self.w_ffn[nm] = P.dram(nm, [DEPTH, D, D_FF], F32, kin)
        for nm in ("ffn1_down", "ffn2_down"):
            self.w_ffn[nm] = P.dram(nm, [DEPTH, D_FF, D], F32, kin)
        self.w_in = P.dram("w_in", [DEPTH, D, D_IN], F32, kin)
        self.w_out = P.dram("w_out", [DEPTH, D, D], F32, kin)
        self.rope_g = P.dram("rope_g", [2, 128, T], F32, kin)
        self.rope_m = P.dram("rope_m", [2, 96, T], F32, kin)
        self.c_bd64 = P.dram("c_bd64", [128, 128], F32, kin)
        self.c_rotm = P.dram("c_rotm", [128, 128], F32, kin)
        self.c_rot96 = P.dram("c_rot96", [96, 96], F32, kin)
        self.gqa_gain = P.dram("gqa_gain", [DEPTH, 128, 2], F32, kin)
        self.mla_gain = P.dram("mla_gain", [DEPTH, 128, 5], F32, kin)
        self.mla_w_uq = P.dram("mla_w_uq", [DEPTH, 256, 384], F32, kin)
        self.mla_w_ukv = P.dram("mla_w_ukv", [DEPTH, 128, 512], F32, kin)
        self.c_ident = P.dram("c_ident", [128, 128], F32, kin)
        self.hy_cw = P.dram("hy_cw", [DEPTH, 128, 6, 4], F32, kin)
        self.hy_biasT = P.dram("hy_biasT", [DEPTH, 128, 2], F32, kin)
        self.hy_fb = P.dram("hy_fb", [DEPTH, 64, 4], F32, kin)
        self.hy_f_w1 = P.dram("hy_f_w1", [DEPTH, 33, 64], F32, kin)
        self.hy_f_w2 = P.dram("hy_f_w2", [DEPTH, 64, 64], F32, kin)
        self.hy_f_w3 = P.dram("hy_f_w3", [DEPTH, 64, 64], F32, kin)
        self.hy_f_w4 = P.dram("hy_f_w4", [DEPTH, 64, 512], F32, kin)
        self.hy_ndelta = P.dram("hy_ndelta", [128, 512], F32, kin)
        self.hy_tabs = {}
        for L_ in (SEQ, CTX):
            ntt_ = L_ // 128
            self.hy_tabs[L_] = dict(
                z=P.dram(f"hy_z{L_}", [33, L_], F32, kin),
                t01=P.dram(f"hy_t01_{L_}", [128, ntt_], F32, kin),
                F=P.dram(f"dftF{L_}", [2, ntt_, 128, ntt_, 128], BF16, kin),
                I=P.dram(f"dftI{L_}", [2, L_ // 256, 128, ntt_, 256], BF16, kin))
        self.rw_muT = P.dram("rw_muT", [DEPTH, 128, 9], F32, kin)
        self.rw_w0a0 = P.dram("rw_w0a0", [DEPTH, 128, 2, 2, 2], F32, kin)
        self.rw_vecs = P.dram("rw_vecs", [DEPTH, 128, 2, 3], F32, kin)
        self.rw_w2 = P.dram("rw_w2", [DEPTH, 2, 64, 256], F32, kin)
        self.rw_a2 = P.dram("rw_a2", [DEPTH, 2, 64, 256], F32, kin)
        self.rw_g2 = P.dram("rw_g2", [DEPTH, 128, 256], F32, kin)
        self.rw_ln = P.dram("rw_ln", [DEPTH, 2, 256], F32, kin)
        self.rw_masks = P.dram("rw_masks", [128, 2, 2, 128], F32, kin)
        self.c_bdo2 = P.dram("c_bdo2", [128, 2], F32, kin)
        self.rwA = P.dram("rwA", [2, T // 64, 2, 128, 520], F32, sk)
        self.rw_y = P.dram("rw_y", [2, T, 256], F32, sk)
        self.rw_vtm = P.dram("rw_vtm", [T, 256], F32, sk)
        self.rw_gtm = P.dram("rw_gtm", [T, 256], F32, sk)
        self.rw_bon = P.dram("rw_bon", [T, 4], F32, sk)
        self.pT = P.dram("pT", [D_IN, T], F32, "ExternalInput" if pT_in else sk)
        self.vg = P.dram("vg", [T, 128], BF16, sk)
        self.ymixT = P.dram("ymixT", [D, T], BF16, sk)
        self.xs = P.dram("xs", [D, T], F32, sk)
        self.out = P.dram("outT", [D, SEQ], F32, "ExternalOutput")
        self.ones_bf = P.alloc([128, 128], BF16)
        self.b_const = Buf("const")
        P.op("pool", "memset", self.ones_bf, 1.0, writes=[self.b_const])
        self.der = [P.alloc([128, N_MOD, KC, 2], F32) for _ in range(DEPTH)]
        self.b_der = Buf("der")

    def dump(self, name, ap, shape, reads, dtype=F32):
        if not self.dbg:
            return
        d = self.P.dram(name, list(shape), dtype, "ExternalOutput")
        self.P.dma("sp", d.ap(), ap, reads=reads)

    def phase_mod(self):
        P = self.P
        P.push()
        cv = P.alloc([128, KC, 2], F32)
        sc = P.alloc([128, KC, 2], F32)
        b_cv = Buf()
        P.dma("sp", cv, self.cvec.ap(), writes=[b_cv])
        b_sc = Buf()
        P.op("act", "activation", sc, cv, AF.Silu, reads=[b_cv], writes=[b_sc])
        CB = 1152
        wrot = Rot([(P.alloc([128, KC, CB], F32), Buf()) for _ in range(2)])
        mod = P.alloc([128, N_MOD, KC, 2], F32)
        b_mod = Buf()
        adab = P.alloc([128, 72], F32)
        nrm = P.alloc([128, 3, KC], F32)
        b_ld = Buf()
        pm = self.P.banks[0]
        b_pm = self.P.bank_bufs[0]
        for l in range(DEPTH):
            P.dma("sp", adab, self.adab[l], writes=[b_ld])
            P.dma("sp", nrm, self.norms[l].rearrange("i p k -> p i k"), writes=[b_ld])
            for cb in range(N_MOD * D // CB):
                wt, b_wt = wrot.next()
                P.dma("sp", wt, self.ada_w[l].rearrange("(kc p) n -> p kc n", p=128)[:, :, cb * CB:(cb + 1) * CB], writes=[b_wt])
                for jj in range(CB // 128):
                    j = cb * (CB // 128) + jj
                    calls = [((pm[:, 2 * j:2 * j + 2], wt[:, kc, jj * 128:(jj + 1) * 128], sc[:, kc, :]),
                              dict(start=(kc == 0), stop=(kc == KC - 1))) for kc in range(KC)]
                    P.mm(calls, reads=[b_wt, b_sc], writes=[b_pm])
            pmv = pm[:, 0:144].rearrange("p (j s) -> p j s", s=2)
            modv = mod.rearrange("p n k s -> p (n k) s")
            for s in range(2):
                P.op("dve", "tensor_tensor", modv[:, :, s], pmv[:, :, s], adab, ALU.add,
                     reads=[b_pm, b_ld], writes=[b_mod])
            der = self.der[l]
            for i in range(3):
                for s in range(2):
                    P.op("dve", "tensor_copy", der[:, 3 * i, :, s], mod[:, 3 * i, :, s],
                         reads=[b_mod], writes=[self.b_der])
                    P.op("dve", "scalar_tensor_tensor", der[:, 3 * i + 1, :, s], mod[:, 3 * i + 1, :, s], 1.0,
                         nrm[:, i, :], ALU.add, ALU.mult, reads=[b_mod, b_ld], writes=[self.b_der])
                    f = 1.0 if i == 1 else 0.5
                    P.op("dve", "tensor_single_scalar", der[:, 3 * i + 2, :, s], mod[:, 3 * i + 2, :, s], f, ALU.mult,
                         reads=[b_mod], writes=[self.b_der])
            self.dump(f"dbg_der{l}", self.der[l].rearrange("p n k s -> p (n k s)"), [128, 144], [self.b_der])
            self.dump(f"dbg_mod{l}", mod.rearrange("p n k s -> p (n k s)"), [128, 144], [b_mod])
        P.pop("phase_mod")

    def emit_adaln(self, l, i, s, xt, b_xt, n, h, b_h, sq, b_sq, rstd, b_rstd, tmp_rot, ps, b_ps):
        P = self.P
        der = self.der[l]
        P.op("act", "activation", sq[:, :, :n], xt[:, :, :n], AF.Square, reads=[b_xt], writes=[b_sq])
        calls = [((ps[:, :n], self.ones_bf, sq[:, kc, :n]), dict(start=(kc == 0), stop=(kc == KC - 1))) for kc in range(KC)]
        P.mm(calls, reads=[b_sq, self.b_const], writes=[b_ps])
        P.op("act", "activation", rstd[:, :n], ps[:, :n], AF.Ln, bias=EPS, scale=1.0 / D, reads=[b_ps], writes=[b_rstd])
        P.op("act", "activation", rstd[:, :n], rstd[:, :n], AF.Exp, scale=-0.5, reads=[b_rstd], writes=[b_rstd])
        for kc in range(KC):
            tmp, b_tmp = tmp_rot.next()
            P.op("dve", "scalar_tensor_tensor", tmp[:, :n], xt[:, kc, :n], der[:, 3 * i + 1, kc, s:s + 1], rstd[:, :n],
                 ALU.mult, ALU.mult, reads=[b_xt, b_rstd, self.b_der], writes=[b_tmp])
            P.op("act", "activation", h[:, kc, :n], tmp[:, :n], AF.Identity, bias=der[:, 3 * i, kc, s:s + 1], scale=1.0,
                 reads=[b_tmp, self.b_der], writes=[b_h])

    def phase_ffn(self, l, which, src, dst, dst_lat=None, tiles=None):
        P = self.P
        i = 0 if which == 1 else 2
        P.push()
        wg = P.alloc([128, KC, D_FF], BF16)
        wu = P.alloc([128, KC, D_FF], BF16)
        wd = P.alloc([128, NF, D], BF16)
        b_wg, b_wu, b_wd = Buf(), Buf(), Buf()
        gsrc = self.w_ffn[f"ffn{which}_gate"][l].rearrange("(kc p) n -> p kc n", p=128)
        usrc = self.w_ffn[f"ffn{which}_up"][l].rearrange("(kc p) n -> p kc n", p=128)
        dsrc = self.w_ffn[f"ffn{which}_down"][l].rearrange("(f p) n -> p f n", p=128)
        for kc in range(KC):
            P.dma("pool", wg[:, kc, :], gsrc[:, kc, :], writes=[b_wg])
            P.dma("pool", wu[:, kc, :], usrc[:, kc, :], writes=[b_wu])
        for f0 in range(0, NF, 4):
            f1 = min(NF, f0 + 4)
            P.dma("pool", wd[:, f0:f1, :], dsrc[:, f0:f1, :], writes=[b_wd])
        xrot = Rot([(P.alloc([128, KC, NT], F32), Buf()) for _ in range(2)])
        hrot = Rot([(P.alloc([128, KC, NT], BF16), Buf()) for _ in range(2)])
        sq = P.alloc([128, KC, NT], BF16)
        b_sq = Buf()
        rstd = P.alloc([128, NT], F32)
        b_rstd = Buf()
        tmp_rot = Rot([(P.alloc([128, NT], F32), Buf()) for _ in range(2)])
        sg_rot = Rot([(P.alloc([128, NT], F32), Buf()) for _ in range(2)])
        a = P.alloc([128, NF, NT], BF16)
        b_a = [Buf() for _ in range(NF)]
        banks, bb = P.banks, P.bank_bufs
        pg_rot = Rot([(banks[1], bb[1]), (banks[2], bb[2])])
        pu_rot = Rot([(banks[3], bb[3]), (banks[4], bb[4])])
        pd_rot = Rot([(banks[5], bb[5]), (banks[6], bb[6])])
        srcv = src.rearrange("(kc p) t -> p kc t", p=128)
        gate = self.der[l]
        for (t0, n) in (tiles or TILES):
            s = 1 if t0 < CTX else 0
            xt, b_xt = xrot.next()
            h, b_h = hrot.next()
            P.dma("sp", xt[:, :, :n], srcv[:, :, t0:t0 + n], writes=[b_xt])
            self.emit_adaln(l, i, s, xt, b_xt, n, h, b_h, sq, b_sq, rstd, b_rstd, tmp_rot, banks[0], bb[0])
            if t0 == 0 and l == 0 and which == 1:
                self.dump("dbg_h", h.rearrange("p k n -> p (k n)"), [128, KC * NT], [b_h], BF16)
                self.dump("dbg_rstd", rstd, [128, NT], [b_rstd])
            for f in range(NF):
                pg, b_pg = pg_rot.next()
                pu, b_pu = pu_rot.next()
                P.mm([((pg[:, :n], wg[:, kc, f * 128:(f + 1) * 128], h[:, kc, :n]), dict(start=(kc == 0), stop=(kc == KC - 1)))
                      for kc in range(KC)], reads=[b_wg, b_h], writes=[b_pg])
                P.mm([((pu[:, :n], wu[:, kc, f * 128:(f + 1) * 128], h[:, kc, :n]), dict(start=(kc == 0), stop=(kc == KC - 1)))
                      for kc in range(KC)], reads=[b_wu, b_h], writes=[b_pu])
                sg, b_sg = sg_rot.next()
                P.op("act", "activation", sg[:, :n], pg[:, :n], AF.Silu, reads=[b_pg], writes=[b_sg])
                P.op("dve", "tensor_tensor", a[:, f, :n], sg[:, :n], pu[:, :n], ALU.mult, reads=[b_sg, b_pu], writes=[b_a[f]])
            if t0 == 0 and l == 0 and which == 1:
                self.dump("dbg_a", a.rearrange("p k n -> p (k n)"), [128, NF * NT], b_a, BF16)
            for dc in range(KC):
                pd, b_pd = pd_rot.next()
                P.mm([((pd[:, :n], wd[:, f, dc * 128:(dc + 1) * 128], a[:, f, :n]), dict(start=(f == 0), stop=(f == NF - 1)))
                      for f in range(NF)], reads=[b_wd] + b_a, writes=[b_pd])
                P.op("dve", "scalar_tensor_tensor", xt[:, dc, :n], pd[:, :n], gate[:, 3 * i + 2, dc, s:s + 1], xt[:, dc, :n],
                     ALU.mult, ALU.add, reads=[b_pd, b_xt, self.b_der], writes=[b_xt])
            if dst_lat is not None:
                if t0 >= CTX:
                    dv = dst_lat.rearrange("(kc p) t -> p kc t", p=128)
                    P.dma("pool", dv[:, :, t0 - CTX:t0 - CTX + n], xt[:, :, :n], reads=[b_xt])
            else:
                dv = dst.rearrange("(kc p) t -> p kc t", p=128)
                P.dma("pool", dv[:, :, t0:t0 + n], xt[:, :, :n], reads=[b_xt])
        P.pop("phase_ffn")


    def phase_proj(self, l, src):
        P = self.P
        P.push()
        win = P.alloc([128, KC, D_IN], BF16)
        b_win = Buf()
        wsrc = self.w_in[l].rearrange("(kc p) n -> p kc n", p=128)
        for kc in range(KC):
            P.dma("pool", win[:, kc, :], wsrc[:, kc, :], writes=[b_win])
        xrot = Rot([(P.alloc([128, KC, NT], F32), Buf()) for _ in range(2)])
        hrot = Rot([(P.alloc([128, KC, NT], BF16), Buf()) for _ in range(2)])
        sq = P.alloc([128, KC, NT], BF16)
        b_sq = Buf()
        rstd = P.alloc([128, NT], F32)
        b_rstd = Buf()
        tmp_rot = Rot([(P.alloc([128, NT], F32), Buf()) for _ in range(2)])
        NCH = 23
        st_rot = Rot([(P.alloc([128, NCH, NT], F32), Buf()) for _ in range(2)])
        vst_rot = Rot([(P.alloc([128, 2, 128], BF16), Buf()) for _ in range(2)])
        banks, bb = P.banks, P.bank_bufs
        pp_rot = Rot([(banks[i], bb[i]) for i in (1, 2, 3, 4)])
        pv_rot = Rot([(banks[i], bb[i]) for i in (5, 6)])
        srcv = src.rearrange("(kc p) t -> p kc t", p=128)
        pTv = self.pT[0:2816, :].rearrange("(c p) t -> p c t", p=128)
        VG0 = HY_COLS + 384
        for (t0, n) in TILES:
            s = 1 if t0 < CTX else 0
            xt, b_xt = xrot.next()
            h, b_h = hrot.next()
            P.dma("sp", xt[:, :, :n], srcv[:, :, t0:t0 + n], writes=[b_xt])
            self.emit_adaln(l, 1, s, xt, b_xt, n, h, b_h, sq, b_sq, rstd, b_rstd, tmp_rot, banks[0], bb[0])
            st, b_st = st_rot.next()
            for c in range(NCH):
                rows = 128 if c < 22 else 32
                pp, b_pp = pp_rot.next()
                P.mm([((pp[0:rows, :n], win[:, kc, c * 128:c * 128 + rows], h[:, kc, :n]), dict(start=(kc == 0), stop=(kc == KC - 1)))
                      for kc in range(KC)], reads=[b_win, b_h], writes=[b_pp])
                if c % 2 == 0:
                    P.op("act", "copy", st[0:rows, c, :n], pp[0:rows, :n], reads=[b_pp], writes=[b_st])
                else:
                    P.op("dve", "tensor_copy", st[0:rows, c, :n], pp[0:rows, :n], reads=[b_pp], writes=[b_st])
            P.dma("pool", pTv[:, :, t0:t0 + n], st[:, 0:22, :n], reads=[b_st])
            P.dma("pool", self.pT[2816:2848, t0:t0 + n], st[0:32, 22, :n], reads=[b_st])
            vst, b_vst = vst_rot.next()
            for sub in range(n // 128):
                pv, b_pv = pv_rot.next()
                P.mm([((pv[:, 0:128], h[:, kc, sub * 128:(sub + 1) * 128], win[:, kc, VG0:VG0 + 128]), dict(start=(kc == 0), stop=(kc == KC - 1)))
                      for kc in range(KC)], reads=[b_win, b_h], writes=[b_pv])
                P.op("dve", "tensor_copy", vst[:, sub, :], pv[:, 0:128], reads=[b_pv], writes=[b_vst])
            P.dma("pool", self.vg[t0:t0 + n, :].rearrange("(s p) c -> p s c", p=128), vst[:, 0:n // 128, :], reads=[b_vst])
        P.pop("phase_proj")

    def emit_headnorm(self, src, b_src, rows, n, ndim, ones_f, gain, cosv, sinv, rotm, out_bf, b_out, wk, psA, b_psA, psB, b_psB):
        P = self.P
        sq, b_sq, rstd, b_rstd, qn, b_qn, t1, b_t1 = wk
        P.op("act", "activation", sq[0:rows, :n], src, AF.Square, reads=[b_src], writes=[b_sq])
        P.mm([((psA[0:rows, :n], ones_f, sq[0:rows, :n]), dict(start=True, stop=True))], reads=[b_sq, self.b_const], writes=[b_psA])
        P.op("act", "activation", rstd[0:rows, :n], psA[0:rows, :n], AF.Ln, bias=EPS, scale=1.0 / ndim, reads=[b_psA], writes=[b_rstd])
        P.op("act", "activation", rstd[0:rows, :n], rstd[0:rows, :n], AF.Exp, scale=-0.5, reads=[b_rstd], writes=[b_rstd])
        P.op("dve", "scalar_tensor_tensor", qn[0:rows, :n], src, gain, rstd[0:rows, :n], ALU.mult, ALU.mult,
             reads=[b_src, b_rstd, self.b_const], writes=[b_qn])
        P.mm([((psB[0:rows, :n], rotm, qn[0:rows, :n]), dict(start=True, stop=True))], reads=[b_qn, self.b_const], writes=[b_psB])
        P.op("dve", "tensor_tensor", t1[0:rows, :n], qn[0:rows, :n], cosv, ALU.mult, reads=[b_qn, self.b_const], writes=[b_t1])
        P.op("dve", "tensor_tensor", qn[0:rows, :n], psB[0:rows, :n], sinv, ALU.mult, reads=[b_psB, self.b_const], writes=[b_qn])
        if isinstance(out_bf, list):
            for (sl, ap) in out_bf:
                P.op("dve", "tensor_tensor", ap, t1[sl, :n], qn[sl, :n], ALU.add, reads=[b_t1, b_qn], writes=[b_out])
        else:
            P.op("dve", "tensor_tensor", out_bf, t1[0:rows, :n], qn[0:rows, :n], ALU.add, reads=[b_t1, b_qn], writes=[b_out])

    def _wkargs(self, wk_sets):
        wk, (ia, ib) = wk_sets.next()
        bk, bb_ = self.P.banks, self.P.bank_bufs
        return wk, bk[ia], bb_[ia], bk[ib], bb_[ib]

    def emit_attn(self, heads, scale, K_rows, n_kt, ebufs, b_stage_rot, ctx_out, co=None, co_every=8):
        P = self.P
        banks, bb = P.banks, P.bank_bufs
        st_rot = Rot([(banks[i], bb[i]) for i in (0, 1, 2, 3)])
        acc_rot = Rot([(banks[i], bb[i]) for i in (4, 5)])
        QN = 512
        LA = 2
        qtiles = [(CTX + i * QN, QN, n_kt) for i in range(SEQ // QN)]
        if ctx_out:
            qtiles = [(0, CTX, CTX // 128)] + qtiles
        for hd in heads:
            for (q0, qn_, nk) in qtiles:
                acc, b_acc = acc_rot.next()
                pend = []

                def pv(item, last):
                    kt_, eb_, b_eb_ = item
                    P.mm([((acc[:, :qn_], hd["v"](kt_), eb_[:, :qn_]), dict(start=(kt_ == 0), stop=last))],
                         reads=[hd["b_v"], b_eb_], writes=[b_acc])
                for kt in range(nk):
                    st, b_st = st_rot.next()
                    P.mm([((st[:, :qn_], hd["k"][:, kt * 128:(kt + 1) * 128], hd["q"][:, q0:q0 + qn_]), dict(start=True, stop=True))],
                         reads=[hd["b_k"], hd["b_q"]], writes=[b_st])
                    eb, b_eb = ebufs.next()
                    P.op("act", "activation", eb[:, :qn_], st[:, :qn_], AF.Exp, scale=scale, reads=[b_st], writes=[b_eb])
                    pend.append((kt, eb, b_eb))
                    if len(pend) > LA:
                        pv(pend.pop(0), False)
                    if co is not None and kt % co_every == co_every - 1:
                        if next(co, "end") == "end":
                            co = None
                while pend:
                    it_ = pend.pop(0)
                    pv(it_, len(pend) == 0)
                (rec, y), b_y = b_stage_rot.next()
                P.op("dve", "reciprocal", rec[0:64, :qn_], acc[64:128, :qn_], reads=[b_acc], writes=[b_y])
                P.op("dve", "tensor_tensor", y[0:64, :qn_], acc[0:64, :qn_], rec[0:64, :qn_], ALU.mult, reads=[b_acc, b_y], writes=[b_y])
                r0 = hd["out_row"]
                P.dma("pool", self.ymixT[r0:r0 + 64, q0:q0 + qn_], y[0:64, :qn_], reads=[b_y])
        if co is not None:
            for _ in co:
                pass

    def phase_gqa(self, l, ctx_out, co_rwkv=False):
        P = self.P
        P.push()
        cosg = P.alloc([128, T], F32)
        sing = P.alloc([128, T], F32)
        bd64 = P.alloc([128, 128], F32)
        rotm = P.alloc([128, 128], F32)
        gain = P.alloc([128, 2], F32)
        bc = self.b_const
        P.dma("sp", cosg, self.rope_g[0], writes=[bc])
        P.dma("sp", sing, self.rope_g[1], writes=[bc])
        P.dma("sp", bd64, self.c_bd64.ap(), writes=[bc])
        P.dma("sp", rotm, self.c_rotm.ap(), writes=[bc])
        P.dma("sp", gain, self.gqa_gain[l], writes=[bc])
        qr = P.alloc([128, 4, T], BF16)
        kr = P.alloc([128, T], BF16)
        b_qr, b_kr = Buf(), Buf()
        P.op("pool", "memset", qr, 0.0, writes=[b_qr])
        vaug = P.alloc([128, T // 128, 2, 128], BF16)
        b_v = Buf()
        P.op("pool", "memset", vaug, 1.0, writes=[b_v])
        vgv = self.vg.rearrange("(kt p) c -> p kt c", p=128)
        for g in range(2):
            P.dma("sp", vaug[:, :, g, 0:64], vgv[:, :, g * 64:(g + 1) * 64], writes=[b_v])
        QN = 512
        prot = Rot([(P.alloc([128, QN], F32), Buf()) for _ in range(2)])
        wk_sets = Rot([((P.alloc([128, QN], F32), Buf(), P.alloc([128, QN], F32), Buf(), P.alloc([128, QN], F32), Buf(), P.alloc([128, QN], F32), Buf()), pb) for pb in ((6, 7), (4, 5))])
        banks, bb = P.banks, P.bank_bufs
        G0 = HY_COLS
        for r in range(3):
            for t0 in range(0, T, QN):
                n = min(QN, T - t0)
                pt, b_pt = prot.next()
                P.dma("sp", pt[:, :n], self.pT[G0 + r * 128:G0 + (r + 1) * 128, t0:t0 + n], writes=[b_pt])
                outap = [(slice(0, 64), qr[0:64, 2 * r, t0:t0 + n]), (slice(64, 128), qr[64:128, 2 * r + 1, t0:t0 + n])] if r < 2 else kr[:, t0:t0 + n]
                self.emit_headnorm(pt[:, :n], b_pt, 128, n, 64, bd64, gain[:, (0 if r < 2 else 1):(1 if r < 2 else 2)],
                                   cosg[:, t0:t0 + n], sing[:, t0:t0 + n], rotm, outap, (b_qr if r < 2 else b_kr), *self._wkargs(wk_sets))
        ebufs = Rot([(P.alloc([128, QN], BF16), Buf()) for _ in range(4)])
        stage = Rot([((P.alloc([128, QN], F32), P.alloc([128, QN], BF16)), Buf()) for _ in range(2)])
        heads = []
        for r in range(2):
            for half in range(2):
                base = half * 64
                heads.append(dict(q=qr[:, 2 * r + half, :], b_q=b_qr, k=kr[:, :], b_k=b_kr,
                                  v=(lambda kt, half=half: vaug[:, kt, half, :]), b_v=b_v,
                                  out_row=256 + (half * 2 + r) * 64))
        co = self.rwkv_bc_gen(l, ctx_out, (6, 7)) if co_rwkv else None
        self.emit_attn(heads, 64 ** -0.5, 64, T // 128, ebufs, stage, ctx_out, co=co, co_every=8)
        P.pop("phase_gqa")

    def phase_mla(self, l, ctx_out):
        P = self.P
        P.push()
        bc = self.b_const
        cosm = P.alloc([96, T], F32)
        sinm = P.alloc([96, T], F32)
        ones_f = P.alloc([128, 128], F32)
        rot96 = P.alloc([96, 96], F32)
        gain = P.alloc([128, 5], F32)
        P.dma("sp", cosm, self.rope_m[0], writes=[bc])
        P.dma("sp", sinm, self.rope_m[1], writes=[bc])
        P.op("pool", "memset", ones_f, 1.0, writes=[bc])
        P.dma("sp", rot96, self.c_rot96.ap(), writes=[bc])
        P.dma("sp", gain, self.mla_gain[l], writes=[bc])
        wuq = P.alloc([128, 2, 384], BF16)
        wuk = P.alloc([128, 4, 64], BF16)
        wuv = P.alloc([128, 4, 64], BF16)
        P.dma("pool", wuq, self.mla_w_uq[l].rearrange("(c p) n -> p c n", p=128), writes=[bc])
        ukv = self.mla_w_ukv[l].rearrange("p (h two d) -> p h two d", h=4, two=2)
        P.dma("pool", wuk, ukv[:, :, 0, :], writes=[bc])
        P.dma("pool", wuv, ukv[:, :, 1, :], writes=[bc])
        qr = [P.alloc([96, T], BF16) for _ in range(4)]
        kr = [P.alloc([96, T], BF16) for _ in range(4)]
        b_qr = [Buf() for _ in range(4)]
        b_kr = [Buf() for _ in range(4)]
        vaug = P.alloc([128, T // 128, 4, 128], BF16)
        b_v = Buf()
        P.op("pool", "memset", vaug, 1.0, writes=[b_v])
        QN = 512
        M0 = HY_COLS + GQA_COLS
        cq_rot = Rot([(P.alloc([128, 3, QN], F32), Buf()) for _ in range(1)])
        sqc = P.alloc([128, 3, QN], F32)
        b_sqc = Buf()
        rs2 = P.alloc([128, 2, QN], F32)
        b_rs2 = Buf()
        cqn = P.alloc([128, 3, QN], BF16)
        b_cqn = Buf()
        qs_rot = Rot([(P.alloc([96, QN], F32), Buf()) for _ in range(2)])
        kf_rot = Rot([(P.alloc([96, QN], F32), Buf()) for _ in range(2)])
        wk_sets = Rot([((P.alloc([128, QN], F32), Buf(), P.alloc([128, QN], F32), Buf(), P.alloc([128, QN], F32), Buf(), P.alloc([128, QN], F32), Buf()), pb) for pb in ((6, 7), (4, 5))])
        banks, bb = P.banks, P.bank_bufs
        pq_rot = Rot([(banks[i], bb[i]) for i in (1, 2)])
        pTc = self.pT[M0:M0 + 384, :].rearrange("(c p) t -> p c t", p=128)
        for t0 in range(0, T, QN):
            n = min(QN, T - t0)
            cq, b_cq = cq_rot.next()
            P.dma("sp", cq[:, :, :n], pTc[:, :, t0:t0 + n], writes=[b_cq])
            P.op("act", "activation", sqc[:, :, :n], cq[:, :, :n], AF.Square, reads=[b_cq], writes=[b_sqc])
            P.mm([((banks[6][:, :n], ones_f, sqc[:, c, :n]), dict(start=(c == 0), stop=(c == 1))) for c in range(2)],
                 reads=[b_sqc, bc], writes=[bb[6]])
            P.mm([((banks[7][:, :n], ones_f, sqc[:, 2, :n]), dict(start=True, stop=True))], reads=[b_sqc, bc], writes=[bb[7]])
            P.op("act", "activation", rs2[:, 0, :n], banks[6][:, :n], AF.Ln, bias=EPS, scale=1.0 / 256, reads=[bb[6]], writes=[b_rs2])
            P.op("act", "activation", rs2[:, 1, :n], banks[7][:, :n], AF.Ln, bias=EPS, scale=1.0 / 128, reads=[bb[7]], writes=[b_rs2])
            P.op("act", "activation", rs2[:, :, :n], rs2[:, :, :n], AF.Exp, scale=-0.5, reads=[b_rs2], writes=[b_rs2])
            for c in range(3):
                P.op("dve", "scalar_tensor_tensor", cqn[:, c, :n], cq[:, c, :n], gain[:, c:c + 1], rs2[:, (0 if c < 2 else 1), :n],
                     ALU.mult, ALU.mult, reads=[b_cq, b_rs2, bc], writes=[b_cqn])
            for h in range(4):
                pq, b_pq = pq_rot.next()
                P.mm([((pq[0:96, :n], wuq[:, c, h * 96:(h + 1) * 96], cqn[:, c, :n]), dict(start=(c == 0), stop=(c == 1))) for c in range(2)],
                     reads=[b_cqn, bc], writes=[b_pq])
                qs, b_qs = qs_rot.next()
                P.op("act", "copy", qs[:, :n], pq[0:96, :n], reads=[b_pq], writes=[b_qs])
                self.emit_headnorm(qs[:, :n], b_qs, 96, n, 96, ones_f[0:96, 0:96], gain[0:96, 3:4], cosm[:, t0:t0 + n], sinm[:, t0:t0 + n],
                                   rot96, qr[h][:, t0:t0 + n], b_qr[h], *self._wkargs(wk_sets))
                pk, b_pk = pq_rot.next()
                P.mm([((pk[0:64, :n], wuk[:, h, :], cqn[:, 2, :n]), dict(start=True, stop=True))], reads=[b_cqn, bc], writes=[b_pk])
                kf, b_kf = kf_rot.next()
                P.dma("sp", kf[64:96, :n], self.pT[M0 + 384:M0 + 416, t0:t0 + n], writes=[b_kf])
                P.op("act", "copy", kf[0:64, :n], pk[0:64, :n], reads=[b_pk], writes=[b_kf])
                self.emit_headnorm(kf[:, :n], b_kf, 96, n, 96, ones_f[0:96, 0:96], gain[0:96, 4:5], cosm[:, t0:t0 + n], sinm[:, t0:t0 + n],
                                   rot96, kr[h][:, t0:t0 + n], b_kr[h], *self._wkargs(wk_sets))
            for sub in range(n // 128):
                kt = t0 // 128 + sub
                pv, b_pv = pq_rot.next()
                P.mm([((pv[:, 0:256], cqn[:, 2, sub * 128:(sub + 1) * 128], wuv.rearrange("p h d -> p (h d)")), dict(start=True, stop=True))],
                     reads=[b_cqn, bc], writes=[b_pv])
                P.op("dve", "tensor_copy", vaug[:, kt, :, 0:64], pv[:, 0:256].rearrange("p (h d) -> p h d", h=4), reads=[b_pv], writes=[b_v])
        ebufs = Rot([(P.alloc([128, QN], BF16), Buf()) for _ in range(4)])
        stage = Rot([((P.alloc([128, QN], F32), P.alloc([128, QN], BF16)), Buf()) for _ in range(2)])
        heads = [dict(q=qr[h], b_q=b_qr[h], k=kr[h], b_k=b_kr[h], v=(lambda kt, h=h: vaug[:, kt, h, :]), b_v=b_v,
                      out_row=512 + h * 64) for h in range(4)]
        self.emit_attn(heads, 96 ** -0.5, 96, T // 128, ebufs, stage, ctx_out)
        P.pop("phase_mla")

    def emit_sin(self, ps, rows, n, bcol, fb, out, b_ps, b_out, wk):
        P = self.P
        pre, b_pre, r, b_r = wk
        MAGIC = 12582912.0
        TWO_PI = 2.0 * math.pi
        P.op("dve", "tensor_scalar", pre[0:rows, :n], ps, fb[0:rows, bcol:bcol + 1], fb[0:rows, 3:4], ALU.add, ALU.mult,
             reads=[b_ps, self.b_const], writes=[b_pre])
        P.op("dve", "tensor_scalar", r[0:rows, :n], pre[0:rows, :n], 1.0 / TWO_PI, MAGIC, ALU.mult, ALU.add, reads=[b_pre], writes=[b_r])
        P.op("dve", "tensor_scalar", r[0:rows, :n], r[0:rows, :n], MAGIC, -TWO_PI, ALU.subtract, ALU.mult, reads=[b_r], writes=[b_r])
        P.op("dve", "tensor_tensor", r[0:rows, :n], r[0:rows, :n], pre[0:rows, :n], ALU.add, reads=[b_r, b_pre], writes=[b_r])
        P.op("dve", "tensor_scalar", r[0:rows, :n], r[0:rows, :n], math.pi, -math.pi, ALU.min, ALU.max, reads=[b_r], writes=[b_r])
        P.op("act", "activation", out, r[0:rows, :n], AF.Sin, reads=[b_r], writes=[b_out])

    def phase_hyena(self, l, L, col0):
        P = self.P
        bc = self.b_const
        banks, bb = P.banks, P.bank_bufs
        ntt = L // 128
        tabs = self.hy_tabs[L]
        P.push()
        u = P.alloc([128, 2, L], F32)
        b_u = Buf()
        Akt = P.alloc([128, ntt, 256], BF16)
        Bkt = P.alloc([128, ntt, 256], BF16)
        b_AB = Buf()
        cw = P.alloc([128, 6, 4], F32)
        hbias = P.alloc([128, 2], F32)
        P.dma("sp", cw, self.hy_cw[l], writes=[bc])
        P.dma("sp", hbias, self.hy_biasT[l], writes=[bc])
        P.push()
        rhsC = P.alloc([128, ntt, 512], BF16)
        rhsS = P.alloc([128, ntt, 512], BF16)
        b_rhs = Buf()
        P.push()
        ident = P.alloc([128, 128], F32)
        P.dma("sp", ident, self.c_ident.ap(), writes=[bc])
        raw = P.alloc([128, L + 2], F32)
        b_raw = Buf()
        x1c = P.alloc([128, L], F32)
        b_x1c = Buf()
        vc = P.alloc([128, L], F32)
        b_vc = Buf()
        P.op("pool", "memset", raw[:, 0:1], 0.0, writes=[b_raw])
        P.op("pool", "memset", raw[:, L + 1:L + 2], 0.0, writes=[b_raw])

        def conv(j6, out, b_out):
            P.dma("sp", raw[:, 1:L + 1], self.pT[j6 * 128:(j6 + 1) * 128, col0:col0 + L], writes=[b_raw])
            P.op("dve", "tensor_scalar", out, raw[:, 1:L + 1], cw[:, j6, 1:2], cw[:, j6, 3:4], ALU.mult, ALU.add, reads=[b_raw, bc], writes=[b_out])
            P.op("dve", "scalar_tensor_tensor", out, raw[:, 0:L], cw[:, j6, 0:1], out, ALU.mult, ALU.add, reads=[b_raw, b_out, bc], writes=[b_out])
            P.op("dve", "scalar_tensor_tensor", out, raw[:, 2:L + 2], cw[:, j6, 2:3], out, ALU.mult, ALU.add, reads=[b_raw, b_out, bc], writes=[b_out])
        tr_rot = Rot([(banks[i], bb[i]) for i in (1, 2, 3, 4)])
        for j in range(2):
            conv(j, x1c, b_x1c)
            conv(4 + j, vc, b_vc)
            P.op("dve", "tensor_tensor", u[:, j, :], x1c, vc, ALU.mult, reads=[b_x1c, b_vc], writes=[b_u])
            for tt in range(ntt):
                pt, b_pt = tr_rot.next()
                P.mm([((pt[:, 0:128], u[:, j, tt * 128:(tt + 1) * 128], ident), {})], reads=[b_u, bc], writes=[b_pt], meth="transpose")
                P.op("act", "copy", rhsC[:, tt, j * 128:(j + 1) * 128], pt[:, 0:128], reads=[b_pt], writes=[b_rhs])
                P.op("dve", "tensor_copy", rhsS[:, tt, j * 128:(j + 1) * 128], pt[:, 0:128], reads=[b_pt], writes=[b_rhs])
        P.pop()
        P.push()
        z = P.alloc([33, L], F32)
        w1 = P.alloc([33, 64], F32)
        w2 = P.alloc([64, 64], F32)
        w3 = P.alloc([64, 64], F32)
        w4 = P.alloc([64, 512], F32)
        fb = P.alloc([64, 4], F32)
        t01 = P.alloc([128, ntt], F32)
        ndel = P.alloc([128, 512], F32)
        P.dma("sp", z, tabs["z"].ap(), writes=[bc])
        P.dma("sp", w1, self.hy_f_w1[l], writes=[bc])
        P.dma("sp", w2, self.hy_f_w2[l], writes=[bc])
        P.dma("sp", w3, self.hy_f_w3[l], writes=[bc])
        P.dma("sp", w4, self.hy_f_w4[l], writes=[bc])
        P.dma("sp", fb, self.hy_fb[l], writes=[bc])
        P.dma("sp", t01, tabs["t01"].ap(), writes=[bc])
        P.dma("sp", ndel, self.hy_ndelta.ap(), writes=[bc])
        hA = P.alloc([64, L], F32)
        hB = P.alloc([64, L], F32)
        b_hA, b_hB = Buf(), Buf()
        wk = (P.alloc([64, 512], F32), Buf(), P.alloc([64, 512], F32), Buf())
        CW = min(512, L)
        ps_rot = Rot([(banks[i], bb[i]) for i in (1, 2, 3, 4)])
        for (lhs, K_, src, b_src, dst, b_dst, bcol) in ((w1, 33, z, bc, hA, b_hA, 0), (w2, 64, hA, b_hA, hB, b_hB, 1), (w3, 64, hB, b_hB, hA, b_hA, 2)):
            for c0 in range(0, L, CW):
                ps, b_ps = ps_rot.next()
                P.mm([((ps[0:64, :CW], lhs, src[0:K_, c0:c0 + CW]), dict(start=True, stop=True))], reads=[b_src, bc], writes=[b_ps])
                self.emit_sin(ps[0:64, :CW], 64, CW, bcol, fb, dst[:, c0:c0 + CW], b_ps, b_dst, wk)
        wt_rot = Rot([(P.alloc([128, 512], F32), Buf()) for _ in range(2)])
        ft_rot = Rot([(P.alloc([128, 512], F32), Buf()) for _ in range(2)])
        for tt in range(ntt):
            ps, b_ps = ps_rot.next()
            P.mm([((ps[:, :], hA[:, tt * 128:(tt + 1) * 128], w4), dict(start=True, stop=True))], reads=[b_hA, bc], writes=[b_ps])
            wt, b_wt = wt_rot.next()
            P.op("act", "activation", wt, ndel, AF.Exp, scale=t01[:, tt:tt + 1], reads=[bc], writes=[b_wt])
            ft, b_ft = ft_rot.next()
            P.op("dve", "tensor_tensor", ft, ps[:, :], wt, ALU.mult, reads=[b_ps, b_wt], writes=[b_ft])
            P.op("dve", "tensor_tensor", rhsC[:, tt, 256:512], ft[:, 0:256], ft[:, 256:512], ALU.add, reads=[b_ft], writes=[b_rhs])
            P.op("dve", "tensor_tensor", rhsS[:, tt, 256:512], ft[:, 256:512], ft[:, 0:256], ALU.subtract, reads=[b_ft], writes=[b_rhs])
        P.pop()
        P.push()
        ck_rot = Rot([((P.alloc([128, ntt, 128], BF16), P.alloc([128, ntt, 128], BF16)), Buf()) for _ in range(2)])
        ec = P.alloc([128, 512], F32)
        es = P.alloc([128, 512], F32)
        b_ec, b_es = Buf(), Buf()
        t1 = P.alloc([128, 256], F32)
        t2 = P.alloc([128, 256], F32)
        b_t1, b_t2 = Buf(), Buf()
        pc_rot = Rot([(banks[i], bb[i]) for i in (1, 2)])
        psn_rot = Rot([(banks[i], bb[i]) for i in (3, 4)])
        for kt in range(ntt):
            (ck, sk), b_ck = ck_rot.next()
            P.dma("sp", ck, tabs["F"][0, kt], writes=[b_ck])
            P.dma("sp", sk, tabs["F"][1, kt], writes=[b_ck])
            pc, b_pc = pc_rot.next()
            psn, b_psn = psn_rot.next()
            P.mm([((pc[:, :], ck[:, tt, :], rhsC[:, tt, :]), dict(start=(tt == 0), stop=(tt == ntt - 1))) for tt in range(ntt)],
                 reads=[b_ck, b_rhs], writes=[b_pc])
            P.mm([((psn[:, :], sk[:, tt, :], rhsS[:, tt, :]), dict(start=(tt == 0), stop=(tt == ntt - 1))) for tt in range(ntt)],
                 reads=[b_ck, b_rhs], writes=[b_psn])
            P.op("act", "copy", ec, pc[:, :], reads=[b_pc], writes=[b_ec])
            P.op("act", "copy", es, psn[:, :], reads=[b_psn], writes=[b_es])
            Uc, Hre, Us, Him = ec[:, 0:256], ec[:, 256:512], es[:, 0:256], es[:, 256:512]
            P.op("dve", "tensor_tensor", t1, Hre, Uc, ALU.mult, reads=[b_ec], writes=[b_t1])
            P.op("dve", "tensor_tensor", t2, Him, Us, ALU.mult, reads=[b_es], writes=[b_t2])
            P.op("dve", "tensor_tensor", Akt[:, kt, :], t1, t2, ALU.add, reads=[b_t1, b_t2], writes=[b_AB])
            P.op("dve", "tensor_tensor", t1, Hre, Us, ALU.mult, reads=[b_ec, b_es], writes=[b_t1])
            P.op("dve", "tensor_tensor", t2, Him, Uc, ALU.mult, reads=[b_es, b_ec], writes=[b_t2])
            P.op("dve", "tensor_tensor", Bkt[:, kt, :], t1, t2, ALU.subtract, reads=[b_t1, b_t2], writes=[b_AB])
        P.pop()
        P.pop()
        P.push()
        TQ = 256
        ct_rot = Rot([((P.alloc([128, ntt, TQ], BF16), P.alloc([128, ntt, TQ], BF16)), Buf()) for _ in range(2)])
        rw_rot = Rot([(P.alloc([128, TQ + 2], F32), Buf()) for _ in range(2)])
        for (rw_, b_rw_) in rw_rot.items:
            P.op("pool", "memset", rw_, 0.0, writes=[b_rw_])
        x2c = P.alloc([128, TQ], F32)
        b_x2c = Buf()
        ta = P.alloc([128, TQ], F32)
        b_ta = Buf()
        yo_rot = Rot([(P.alloc([128, TQ], BF16), Buf()) for _ in range(2)])
        pd_rot = Rot([(banks[i], bb[i]) for i in (5, 6, 7)])
        for tq in range(L // TQ):
            (ct, stt), b_ct = ct_rot.next()
            P.dma("sp", ct, tabs["I"][0, tq], writes=[b_ct])
            P.dma("sp", stt, tabs["I"][1, tq], writes=[b_ct])
            t0 = tq * TQ
            for j in range(2):
                pd, b_pd = pd_rot.next()
                calls = []
                for kt in range(ntt):
                    calls.append(((pd[:, :TQ], Akt[:, kt, j * 128:(j + 1) * 128], ct[:, kt, :]), dict(start=(kt == 0), stop=False)))
                    calls.append(((pd[:, :TQ], Bkt[:, kt, j * 128:(j + 1) * 128], stt[:, kt, :]), dict(start=False, stop=(kt == ntt - 1))))
                P.mm(calls, reads=[b_AB, b_ct], writes=[b_pd])
                rw_, b_rw_ = rw_rot.next()
                lo = max(t0 - 1, 0)
                hi = min(t0 + TQ + 1, L)
                if lo == t0 or hi == t0 + TQ:
                    P.op("pool", "memset", rw_, 0.0, writes=[b_rw_])
                P.dma("sp", rw_[:, lo - (t0 - 1):hi - (t0 - 1)], self.pT[(2 + j) * 128:(3 + j) * 128, col0 + lo:col0 + hi], writes=[b_rw_])
                j6 = 2 + j
                P.op("dve", "tensor_scalar", x2c, rw_[:, 1:TQ + 1], cw[:, j6, 1:2], cw[:, j6, 3:4], ALU.mult, ALU.add, reads=[b_rw_, bc], writes=[b_x2c])
                P.op("dve", "scalar_tensor_tensor", x2c, rw_[:, 0:TQ], cw[:, j6, 0:1], x2c, ALU.mult, ALU.add, reads=[b_rw_, b_x2c, bc], writes=[b_x2c])
                P.op("dve", "scalar_tensor_tensor", x2c, rw_[:, 2:TQ + 2], cw[:, j6, 2:3], x2c, ALU.mult, ALU.add, reads=[b_rw_, b_x2c, bc], writes=[b_x2c])
                P.op("dve", "tensor_single_scalar", ta, u[:, j, t0:t0 + TQ], hbias[:, j:j + 1], ALU.mult, reads=[b_u, bc], writes=[b_ta])
                P.op("dve", "scalar_tensor_tensor", ta, pd[:, :TQ], 1.0 / L, ta, ALU.mult, ALU.add, reads=[b_pd, b_ta], writes=[b_ta])
                yo, b_yo = yo_rot.next()
                P.op("dve", "tensor_tensor", yo, ta, x2c, ALU.mult, reads=[b_ta, b_x2c], writes=[b_yo])
                P.dma("pool", self.ymixT[j * 128:(j + 1) * 128, col0 + t0:col0 + t0 + TQ], yo, reads=[b_yo])
        P.pop()
        P.pop("phase_hyena")

    RW0 = HY_COLS + GQA_COLS + MLA_COLS
    NCHUNK = T // 64

    def phase_rwkv_a(self, l):
        P = self.P
        bc = self.b_const
        banks, bb = P.banks, P.bank_bufs
        TB = 128
        F32R = mybir.dt.float32r

        def R(ap):
            return ap
        CDT = BF16 if RW_BF16 else F32
        DDT = BF16 if RW_DBL_BF16 else F32
        P.push()
        mu = P.alloc([128, 9], F32)
        omm = P.alloc([128, 9], F32)
        hmu = P.alloc([128, 9], F32)
        w0a0 = P.alloc([128, 2, 2, 2], F32)
        vecs = P.alloc([128, 2, 3], F32)
        omka = P.alloc([128, 2], F32)
        w2s = P.alloc([128, 256], F32)
        a2s = P.alloc([128, 256], F32)
        g2 = P.alloc([128, 256], F32)
        masks = P.alloc([128, 2, 2, 128], F32)
        ident = P.alloc([128, 128], F32)
        bd64 = P.alloc([128, 128], F32)
        bdo2 = P.alloc([128, 2], F32)
        ident_bf = P.alloc([128, 128], CDT)
        P.dma("pool", ident_bf, self.c_ident.ap(), writes=[bc])
        P.dma("sp", mu, self.rw_muT[l], writes=[bc])
        P.dma("sp", w0a0, self.rw_w0a0[l], writes=[bc])
        P.dma("sp", vecs, self.rw_vecs[l], writes=[bc])
        P.dma("sp", w2s, self.rw_w2[l].rearrange("d k c -> (d k) c"), writes=[bc])
        P.dma("sp", a2s, self.rw_a2[l].rearrange("d k c -> (d k) c"), writes=[bc])
        P.dma("sp", g2, self.rw_g2[l], writes=[bc])
        P.dma("sp", masks, self.rw_masks.ap(), writes=[bc])
        P.dma("sp", ident, self.c_ident.ap(), writes=[bc])
        P.dma("sp", bd64, self.c_bd64.ap(), writes=[bc])
        P.dma("sp", bdo2, self.c_bdo2.ap(), writes=[bc])
        P.op("dve", "tensor_scalar", omm, mu, -1.0, 1.0, ALU.mult, ALU.add, reads=[bc], writes=[bc])
        P.op("dve", "tensor_single_scalar", hmu, mu, 0.5, ALU.mult, reads=[bc], writes=[bc])
        P.op("dve", "tensor_scalar", omka, vecs[:, :, 1], -1.0, 1.0, ALU.mult, ALU.add, reads=[bc], writes=[bc])
        NBUF = 2
        blk_bufs = []
        for _ in range(NBUF):
            o = dict(raw=P.alloc([128, 9, TB + 2], F32), b_raw=Buf(), tsum=P.alloc([128, 9, TB], F32), b_tsum=Buf(),
                     sh=P.alloc([128, 9, TB], F32), b_sh=Buf(), kk=P.alloc([128, 2, TB], F32), nkk=P.alloc([128, 2, TB], F32), b_kk=Buf(),
                     act_t=P.alloc([128, 2, TB], F32), b_act=Buf(), vT=P.alloc([128, 2, 128], F32), b_vT=Buf(),
                     Vbd=P.alloc([128, 2, 2, 128], CDT), b_Vbd=Buf(), kd=P.alloc([128, 2, 2, TB], F32), b_kd=Buf(),
                     gt=P.alloc([128, 256], F32), b_gt=Buf(), bon=P.alloc([128, 4], F32), b_bon=Buf())
            P.op("pool", "memset", o["Vbd"], 0.0, writes=[o["b_Vbd"]])
            P.op("pool", "memset", o["raw"], 0.0, writes=[o["b_raw"]])
            blk_bufs.append(o)
        tmpA = Rot([(P.alloc([128, TB], F32), Buf()) for _ in range(8)])
        ctxs = {}
        for d in range(2):
            for pr in range(2):
                c = dict(lw=P.alloc([128, TB], F32), b_lw=Buf(), lwT=P.alloc([128, 128], F32), b_lwT=Buf(),
                         GE=P.alloc([128, 3, TB], F32), b_GE=Buf(), aa=P.alloc([128, TB], F32), b_aa=Buf(),
                         t1=P.alloc([128, TB], F32), b_t1=Buf(), be=P.alloc([128, TB], F32), b_be=Buf(),
                         QQ=P.alloc([128, 2, 256], CDT), b1=P.alloc([128, 2, 128], CDT), b2=P.alloc([128, 2, 128], CDT), bset=Buf(),
                         ATm=P.alloc([128, 2, 256], CDT), b_ATm=Buf(), BTm=P.alloc([128, 2, 256], CDT), b_BTm=Buf(),
                         Mt=[P.alloc([128, 2, 128], DDT) for _ in range(2)], b_Mt=[Buf(), Buf()],
                         Nt=[P.alloc([128, 2, 128], DDT) for _ in range(2)], b_Nt=[Buf(), Buf()],
                         Wt=[P.alloc([128, 2, 128], DDT) for _ in range(2)], b_Wt=[Buf(), Buf()],
                         M0b=P.alloc([128, 2, 128], DDT), b_M0b=Buf(), TTf=P.alloc([128, 2, 128], CDT), b_TTf=Buf(),
                         XQ=P.alloc([128, 2, 256], CDT), b_XQ=Buf(), TBVQ=P.alloc([128, 2, 256], CDT), b_TBVQ=Buf(),
                         b1tm=P.alloc([128, 2, 128], CDT), b_b1tm=Buf(), b2tm=P.alloc([128, 2, 128], CDT), b_b2tm=Buf(),
                         OUT=P.alloc([128, 2, 520], F32), b_OUT=Buf())
                for tl in (c["QQ"], c["b1"], c["b2"]):
                    P.op("pool", "memset", tl, 0.0, writes=[c["bset"]])
                ctxs[(d, pr)] = c
        ps_rot = Rot([(banks[i], bb[i]) for i in range(8)])
        R0 = self.RW0
        nblk = T // TB

        def ps():
            return ps_rot.next()

        def gen_dpr(d, pr, B, blk):
            c = ctxs[(d, pr)]
            kb = d * 64
            sh, b_sh, kk, nkk, b_kk, act_t, b_act = B["sh"], B["b_sh"], B["kk"], B["nkk"], B["b_kk"], B["act_t"], B["b_act"]
            Vbd, b_Vbd, kd, b_kd = B["Vbd"], B["b_Vbd"], B["kd"], B["b_kd"]
            lw, b_lw, lwT, b_lwT, GE, b_GE, aa, b_aa = c["lw"], c["b_lw"], c["lwT"], c["b_lwT"], c["GE"], c["b_GE"], c["aa"], c["b_aa"]
            QQ, b1, b2, bset = c["QQ"], c["b1"], c["b2"], c["bset"]
            ATm, b_ATm, BTm, b_BTm = c["ATm"], c["b_ATm"], c["BTm"], c["b_BTm"]
            Mt, b_Mt, Nt, b_Nt, Wt, b_Wt = c["Mt"], c["b_Mt"], c["Nt"], c["b_Nt"], c["Wt"], c["b_Wt"]
            XQ, b_XQ, TBVQ, b_TBVQ = c["XQ"], c["b_XQ"], c["TBVQ"], c["b_TBVQ"]
            b1tm, b_b1tm, b2tm, b_b2tm, OUT, b_OUT = c["b1tm"], c["b_b1tm"], c["b2tm"], c["b_b2tm"], c["OUT"], c["b_OUT"]
            pz, b_pz = ps()
            P.mm([((pz[:, 0:TB], w2s[kb:kb + 64, pr * 128:(pr + 1) * 128], act_t[kb:kb + 64, 0, :]), dict(start=True, stop=True))],
                 reads=[b_act, bc], writes=[b_pz])
            P.op("act", "activation", lw, pz[:, 0:TB], AF.Sigmoid, bias=w0a0[:, 0, d, pr:pr + 1], scale=1.0, reads=[b_pz, bc], writes=[b_lw])
            pz, b_pz = ps()
            P.mm([((pz[:, 0:TB], a2s[kb:kb + 64, pr * 128:(pr + 1) * 128], sh[kb:kb + 64, 7, :]), dict(start=True, stop=True))],
                 reads=[b_sh, bc], writes=[b_pz])
            P.op("act", "activation", aa, pz[:, 0:TB], AF.Sigmoid, bias=w0a0[:, 1, d, pr:pr + 1], scale=1.0, reads=[b_pz, bc], writes=[b_aa])
            yield
            pz, b_pz = ps()
            P.mm([((pz[:, 0:128], lw, ident), dict(start=True, stop=True))], reads=[b_lw, bc], writes=[b_pz])
            P.op("act", "activation", lwT, pz[:, 0:128], AF.Copy, scale=-math.exp(-0.5), reads=[b_pz], writes=[b_lwT])
            P.op("dve", "tensor_scalar", c["t1"], aa, vecs[:, pr, 1:2], omka[:, pr:pr + 1], ALU.mult, ALU.add, reads=[b_aa, bc], writes=[c["b_t1"]])
            P.op("dve", "tensor_tensor", kd[:, d, pr, :], sh[:, 2 + pr, :], c["t1"], ALU.mult, reads=[b_sh, c["b_t1"]], writes=[b_kd])
            P.op("dve", "tensor_tensor", c["be"], kk[:, pr, :], aa, ALU.mult, reads=[b_kk, b_aa], writes=[c["b_be"]])
            yield
            pz, b_pz = ps()
            P.mm([((pz[:, 0:128], lwT, masks[:, d, 1, :]), dict(start=True, stop=True))], reads=[b_lwT, bc], writes=[b_pz])
            P.mm([((pz[:, 128:256], lwT, masks[:, d, 0, :]), dict(start=True, stop=True))], reads=[b_lwT, bc], writes=[b_pz])
            P.op("act", "activation", GE[:, 0:2, :], pz[:, 0:256].rearrange("p (a b) -> p a b", a=2), AF.Exp, reads=[b_pz], writes=[b_GE])
            P.op("act", "activation", GE[:, 2, :], pz[:, 0:128], AF.Exp, scale=-1.0, reads=[b_pz], writes=[b_GE])
            yield
            for hh in range(2):
                sl = slice(hh * 64, (hh + 1) * 64)

                def v3(ap):
                    return ap[sl, :].rearrange("p (c t) -> p c t", c=2)
                eng = "dve"
                P.op(eng, "tensor_tensor", QQ[sl, :, hh * 64:(hh + 1) * 64], v3(nkk[:, pr, :]), v3(GE[:, 1, :]), ALU.mult,
                     reads=[b_kk, b_GE], writes=[bset])
                P.op(eng, "tensor_tensor", QQ[sl, :, 128 + hh * 64:128 + (hh + 1) * 64], v3(sh[:, pr, :]), v3(GE[:, 0, :]), ALU.mult,
                     reads=[b_sh, b_GE], writes=[bset])
                P.op(eng, "tensor_tensor", b1[sl, :, hh * 64:(hh + 1) * 64], v3(c["be"]), v3(GE[:, 2, :]), ALU.mult, reads=[c["b_be"], b_GE], writes=[bset])
                P.op(eng, "tensor_tensor", b2[sl, :, hh * 64:(hh + 1) * 64], v3(kd[:, d, pr, :]), v3(GE[:, 2, :]), ALU.mult,
                     reads=[b_kd, b_GE], writes=[bset])
            yield
            for c2 in range(2):
                pz, b_pz = ps()
                P.mm([((pz[:, 0:256], R(b1[:, c2, :]), R(QQ[:, c2, :])), dict(start=True, stop=True))], reads=[bset], writes=[b_pz])
                P.op("dve", "tensor_tensor", ATm[:, c2, :].rearrange("p (a b) -> p a b", a=2), pz[:, 0:256].rearrange("p (a b) -> p a b", a=2),
                     masks[:, d, :, :], ALU.mult, reads=[b_pz, bc], writes=[b_ATm])
                pz, b_pz = ps()
                P.mm([((pz[:, 0:256], R(b2[:, c2, :]), R(QQ[:, c2, :])), dict(start=True, stop=True))], reads=[bset], writes=[b_pz])
                P.op("dve", "tensor_tensor", BTm[:, c2, :].rearrange("p (a b) -> p a b", a=2), pz[:, 0:256].rearrange("p (a b) -> p a b", a=2),
                     masks[:, d, :, :], ALU.mult, reads=[b_pz, bc], writes=[b_BTm])
            pz, b_pz = ps()
            for c2 in range(2):
                P.mm([((pz[:, c2 * 128:(c2 + 1) * 128], R(QQ[:, c2, 0:128]), R(b1[:, c2, :])), dict(start=True, stop=True))], reads=[bset], writes=[b_pz])
            for c2 in range(2):
                P.op("dve", "tensor_tensor", Nt[0][:, c2, :], pz[:, c2 * 128:(c2 + 1) * 128], masks[:, 1 - d, 0, :], ALU.mult,
                     reads=[b_pz, bc], writes=[b_Nt[0]])
            for (src_fn, dst, b_dst) in ((lambda c2: QQ[:, c2, 0:128], None, None), (lambda c2: b1[:, c2, :], b1tm, b_b1tm), (lambda c2: b2[:, c2, :], b2tm, b_b2tm)):
                pz, b_pz = ps()
                for c2 in range(2):
                    P.mm([((pz[:, c2 * 128:(c2 + 1) * 128], src_fn(c2), ident_bf), dict(start=True, stop=True))], reads=[bset, bc], writes=[b_pz])
                if dst is None:
                    P.op("act", "copy", XQ[:, :, 128:256], pz[:, 0:256].rearrange("p (a b) -> p a b", a=2), reads=[b_pz], writes=[b_XQ])
                else:
                    P.op("act", "copy", dst, pz[:, 0:256].rearrange("p (a b) -> p a b", a=2), reads=[b_pz], writes=[b_dst])
            yield
            for c2 in range(2):
                P.op("dve", "tensor_tensor", Wt[1][:, c2, :], ATm[:, c2, 0:128], ident, ALU.add, reads=[b_ATm, bc], writes=[b_Wt[1]])
            P.op("pool", "tensor_copy", c["M0b"], ATm[:, :, 0:128], reads=[b_ATm], writes=[c["b_M0b"]])
            pz, b_pz = ps()
            for c2 in range(2):
                P.mm([((pz[:, c2 * 128:(c2 + 1) * 128], R(BTm[:, c2, 0:128]), R(Vbd[:, pr, c2, :])), dict(start=True, stop=True))],
                     reads=[b_BTm, b_Vbd], writes=[b_pz])
            P.op("act", "copy", XQ[:, :, 0:128], pz[:, 0:256].rearrange("p (a b) -> p a b", a=2), reads=[b_pz], writes=[b_XQ])

            def Mj(j, c2):
                return (c["M0b"][:, c2, :], c["b_M0b"]) if j == 0 else (Mt[j % 2][:, c2, :], b_Mt[j % 2])
            for jj in range(6):
                if jj >= 1:
                    pz, b_pz = ps()
                    for c2 in range(2):
                        P.mm([((pz[:, c2 * 128:(c2 + 1) * 128], R(Nt[jj % 2][:, c2, :]), R(Wt[jj % 2][:, c2, :])), dict(start=True, stop=True))],
                             reads=[b_Nt[jj % 2], b_Wt[jj % 2]], writes=[b_pz])
                    if jj == 5:
                        P.op("dve", "tensor_tensor", c["TTf"], pz[:, 0:256].rearrange("p (a b) -> p a b", a=2), Wt[jj % 2], ALU.add,
                             reads=[b_pz, b_Wt[jj % 2]], writes=[c["b_TTf"]])
                    else:
                        P.op("dve", "tensor_tensor", Wt[(jj + 1) % 2], pz[:, 0:256].rearrange("p (a b) -> p a b", a=2), Wt[jj % 2], ALU.add,
                             reads=[b_pz, b_Wt[jj % 2]], writes=[b_Wt[(jj + 1) % 2]])
                if jj < 4:
                    pz, b_pz = ps()
                    for c2 in range(2):
                        m_, b_m_ = Mj(jj, c2)
                        P.mm([((pz[:, c2 * 128:(c2 + 1) * 128], R(Nt[jj % 2][:, c2, :]), R(m_)), dict(start=True, stop=True))],
                             reads=[b_Nt[jj % 2], b_m_], writes=[b_pz])
                    P.op("act", "copy", Mt[(jj + 1) % 2], pz[:, 0:256].rearrange("p (a b) -> p a b", a=2), reads=[b_pz], writes=[b_Mt[(jj + 1) % 2]])
                if jj < 5:
                    pz, b_pz = ps()
                    for c2 in range(2):
                        m_, b_m_ = Mj(jj, c2)
                        P.mm([((pz[:, c2 * 128:(c2 + 1) * 128], R(m_), R(Nt[jj % 2][:, c2, :])), dict(start=True, stop=True))],
                             reads=[b_Nt[jj % 2], b_m_], writes=[b_pz])
                    P.op("act", "copy", Nt[(jj + 1) % 2], pz[:, 0:256].rearrange("p (a b) -> p a b", a=2), reads=[b_pz], writes=[b_Nt[(jj + 1) % 2]])
                yield
            TT, b_TT = c["TTf"], c["b_TTf"]
            for c2 in range(2):
                pz, b_pz = ps()
                P.mm([((pz[:, 0:256], R(TT[:, c2, :]), R(XQ[:, c2, :])), dict(start=True, stop=True))], reads=[b_TT, b_XQ], writes=[b_pz])
                P.op("act" if c2 == 0 else "dve", "copy" if c2 == 0 else "tensor_copy", TBVQ[:, c2, :], pz[:, 0:256], reads=[b_pz], writes=[b_TBVQ])
            yield
            pzs = [ps() for _ in range(4)]
            for c2 in range(2):
                TBV, TQT = R(TBVQ[:, c2, 0:128]), R(TBVQ[:, c2, 128:256])
                ApT, BpT = R(ATm[:, c2, 128:256]), R(BTm[:, c2, 128:256])
                V_ = R(Vbd[:, pr, c2, :])
                cs = slice(c2 * 128, (c2 + 1) * 128)
                P.mm([((pzs[0][0][:, cs], TQT, ApT), dict(start=True, stop=True))], reads=[b_TBVQ, b_ATm], writes=[pzs[0][1]])
                P.mm([((pzs[1][0][:, cs], ApT, TBV), dict(start=True, stop=False)), ((pzs[1][0][:, cs], BpT, V_), dict(start=False, stop=True))],
                     reads=[b_ATm, b_BTm, b_TBVQ, b_Vbd], writes=[pzs[1][1]])
                P.mm([((pzs[2][0][:, cs], TQT, R(b1tm[:, c2, :])), dict(start=True, stop=True))], reads=[b_TBVQ, b_b1tm], writes=[pzs[2][1]])
                P.mm([((pzs[3][0][:, cs], R(b1tm[:, c2, :]), TBV), dict(start=True, stop=False)),
                      ((pzs[3][0][:, cs], R(b2tm[:, c2, :]), V_), dict(start=False, stop=True))],
                     reads=[b_b1tm, b_b2tm, b_TBVQ, b_Vbd], writes=[pzs[3][1]])
            v2 = lambda ap: ap.rearrange("p (a b) -> p a b", a=2)
            P.op("dve", "tensor_tensor", OUT[:, :, 0:128], v2(pzs[0][0][:, 0:256]), QQ[:, :, 128:256], ALU.add, reads=[pzs[0][1], bset], writes=[b_OUT])
            P.op("act", "copy", OUT[:, :, 128:256], v2(pzs[1][0][:, 0:256]), reads=[pzs[1][1]], writes=[b_OUT])
            for c2 in range(2):
                cs = slice(c2 * 128, (c2 + 1) * 128)
                P.op("dve", "tensor_tensor", OUT[:, c2, 256:384], pzs[2][0][:, cs], ident, ALU.add, reads=[pzs[2][1], bc], writes=[b_OUT])
                gi = c2 * 64 + (63 if d == 0 else 0)
                gcol = GE[:, 0, gi:gi + 1]
                P.op("act", "activation", OUT[:, c2, 384:512], pzs[3][0][:, cs], AF.Copy, scale=gcol, reads=[pzs[3][1], b_GE], writes=[b_OUT])
                P.op("dve", "tensor_copy", OUT[:, c2, 512:513], gcol, reads=[b_GE], writes=[b_OUT])
            P.dma(STQ_RWA, self.rwA[d, blk * 2:blk * 2 + 2, pr].rearrange("c p w -> p c w"), OUT, reads=[b_OUT])
            yield

        nblk = int(os.environ.get("RWA_NBLK", nblk))
        def prep_blk(blk):
            B = blk_bufs[blk % NBUF]
            raw, b_raw, tsum, b_tsum, sh, b_sh = B["raw"], B["b_raw"], B["tsum"], B["b_tsum"], B["sh"], B["b_sh"]
            kk, nkk, b_kk, act_t, b_act, vT, b_vT, Vbd, b_Vbd = B["kk"], B["nkk"], B["b_kk"], B["act_t"], B["b_act"], B["vT"], B["b_vT"], B["Vbd"], B["b_Vbd"]
            t0 = blk * TB
            seg0, seg1 = (0, CTX) if t0 < CTX else (CTX, T)
            lo, hi = max(t0 - 1, seg0), min(t0 + TB + 1, seg1)
            if lo == t0 or hi == t0 + TB:
                P.op("pool", "memset", raw, 0.0, writes=[b_raw])
            P.dma("sp", raw[:, :, lo - (t0 - 1):hi - (t0 - 1)],
                  self.pT[R0:R0 + 1152, lo:hi].rearrange("(c p) t -> p c t", p=128), writes=[b_raw])
            yield
            P.op("dve", "tensor_tensor", tsum, raw[:, :, 0:TB], raw[:, :, 2:TB + 2], ALU.add, reads=[b_raw], writes=[b_tsum])
            for c in range(9):
                P.op("dve", "tensor_single_scalar", sh[:, c, :], raw[:, c, 1:TB + 1], omm[:, c:c + 1], ALU.mult, reads=[b_raw, bc], writes=[b_sh])
            yield
            for c in range(9):
                P.op("dve", "scalar_tensor_tensor", sh[:, c, :], tsum[:, c, :], hmu[:, c:c + 1], sh[:, c, :], ALU.mult, ALU.add,
                     reads=[b_tsum, b_sh, bc], writes=[b_sh])
            yield
            kqs = []
            for pr in range(2):
                kq, b_kq = tmpA.next()
                sq, b_sq = tmpA.next()
                kqs.append((kq, b_kq, sq, b_sq))
                P.op("dve", "tensor_single_scalar", kq, sh[:, 2 + pr, :], vecs[:, pr, 0:1], ALU.mult, reads=[b_sh, bc], writes=[b_kq])
            P.op("act", "activation", act_t[:, 0, :], sh[:, 6, :], AF.Tanh, reads=[b_sh], writes=[b_act])
            P.op("act", "activation", act_t[:, 1, :], sh[:, 8, :], AF.Sigmoid, reads=[b_sh], writes=[b_act])
            yield
            pzv, b_pzv = ps()
            for pr in range(2):
                P.mm([((pzv[:, pr * 128:(pr + 1) * 128], sh[:, 4 + pr, :], ident), dict(start=True, stop=True))], reads=[b_sh, bc], writes=[b_pzv])
            P.op("act", "copy", vT, pzv[:, 0:256].rearrange("p (a b) -> p a b", a=2), reads=[b_pzv], writes=[b_vT])
            for pr in range(2):
                kq, b_kq, sq, b_sq = kqs[pr]
                P.op("act", "activation", sq, kq, AF.Square, reads=[b_kq], writes=[b_sq])
            yield
            pzg, b_pzg = ps()
            P.mm([((pzg[:, 0:256], act_t[:, 1, :], g2), dict(start=True, stop=True))], reads=[b_act, bc], writes=[b_pzg])
            P.op("act", "copy", B["gt"], pzg[:, 0:256], reads=[b_pzg], writes=[B["b_gt"]])
            for pr in range(2):
                kq, b_kq, sq, b_sq = kqs[pr]
                pz, b_pz = ps()
                P.mm([((pz[:, 0:TB], bd64, sq), dict(start=True, stop=True))], reads=[b_sq, bc], writes=[b_pz])
                P.op("act", "activation", sq, pz[:, 0:TB], AF.Ln, bias=1e-24, scale=1.0, reads=[b_pz], writes=[b_sq])
            P.dma(STQ_RWA, self.rw_gtm[t0:t0 + TB, :], B["gt"], reads=[B["b_gt"]])
            P.dma(STQ_RWA, self.rw_vtm[t0:t0 + TB, :], vT.rearrange("p a b -> p (a b)"), reads=[b_vT])
            yield
            for pr in range(2):
                kq, b_kq, sq, b_sq = kqs[pr]
                P.op("act", "activation", sq, sq, AF.Exp, scale=-0.5, reads=[b_sq], writes=[b_sq])
            for c2 in range(2):
                for hh in range(2):
                    P.op("dve", "tensor_copy", Vbd[hh * 64:(hh + 1) * 64, :, c2, hh * 64:(hh + 1) * 64],
                         vT[c2 * 64:(c2 + 1) * 64, :, hh * 64:(hh + 1) * 64], reads=[b_vT], writes=[b_Vbd])
            yield
            for pr in range(2):
                kq, b_kq, sq, b_sq = kqs[pr]
                P.op("dve", "tensor_tensor", kk[:, pr, :], kq, sq, ALU.mult, reads=[b_kq, b_sq], writes=[b_kk])
                P.op("dve", "scalar_tensor_tensor", nkk[:, pr, :], kq, -1.0, sq, ALU.mult, ALU.mult, reads=[b_kq, b_sq, b_kk], writes=[b_kk])
            yield

        def chains_blk(blk):
            B = blk_bufs[blk % NBUF]
            sh, b_sh = B["sh"], B["b_sh"]
            t0 = blk * TB
            gens = [gen_dpr(d, pr, B, blk) for d in range(2) for pr in range(2)]
            nxt = prep_blk(blk + 1) if blk + 1 < nblk else None
            while gens:
                for g_ in list(gens):
                    try:
                        next(g_)
                    except StopIteration:
                        gens.remove(g_)
                if nxt is not None:
                    try:
                        next(nxt)
                    except StopIteration:
                        nxt = None
            if nxt is not None:
                for _ in nxt:
                    pass
            pz, b_pz = ps()
            for pr in range(2):
                ks, b_ks = tmpA.next()
                P.op("dve", "tensor_tensor", ks, B["kd"][:, 0, pr, :], B["kd"][:, 1, pr, :], ALU.add, reads=[B["b_kd"]], writes=[b_ks])
                P.op("dve", "scalar_tensor_tensor", ks, sh[:, pr, :], vecs[:, pr, 2:3], ks, ALU.mult, ALU.mult, reads=[b_sh, b_ks, bc], writes=[b_ks])
                P.mm([((pz[:, 2 * pr:2 * pr + 2], ks, bdo2), dict(start=True, stop=True))], reads=[b_ks, bc], writes=[b_pz])
            P.op("dve", "tensor_copy", B["bon"], pz[:, 0:4], reads=[b_pz], writes=[B["b_bon"]])
            P.dma(STQ_RWA, self.rw_bon[t0:t0 + TB, :], B["bon"], reads=[B["b_bon"]])

        for _ in prep_blk(0):
            pass
        for blk in range(nblk):
            chains_blk(blk)
        P.pop("phase_rwkv_a")

    def rwkv_bc_gen(self, l, ctx_out, bank_ids):
        P = self.P
        banks, bb = P.banks, P.bank_bufs
        OUTW = 520
        NB = 8
        in_rot = Rot([(P.alloc([128, OUTW], F32), Buf()) for _ in range(NB)])
        st = {}
        for d in range(2):
            for pr in range(2):
                tl = [(P.alloc([128, 128], F32), Buf()) for _ in range(2)]
                P.op("pool", "memset", tl[0][0], 0.0, writes=[tl[0][1]])
                st[(d, pr)] = [tl, 0]
        yo_rot = Rot([(P.alloc([128, 128], F32), Buf()) for _ in range(4)])
        ps_rot = Rot([(banks[i], bb[i]) for i in bank_ids])
        order = {0: list(range(self.NCHUNK)), 1: [3, 2, 1, 0] + list(range(self.NCHUNK - 1, 3, -1))}
        its = [(s_, d, pr, order[d][s_]) for s_ in range(self.NCHUNK) for d in range(2) for pr in range(2)]
        LA = NB - 2
        loaded = []

        def load(i):
            (_, d, pr, j) = its[i]
            it, b_it = in_rot.next()
            P.dma("sp", it, self.rwA[d, j, pr], writes=[b_it])
            loaded.append((it, b_it))
        for i in range(min(LA, len(its))):
            load(i)
        for i, (s_, d, pr, j) in enumerate(its):
            if i + LA < len(its):
                load(i + LA)
            it, b_it = loaded[i]
            tl, cur = st[(d, pr)]
            Pc, b_Pc = tl[cur]
            Pn, b_Pn = tl[1 - cur]
            pz, b_pz = ps_rot.next()
            P.mm([((pz[:, 0:128], it[:, 0:128], Pc), dict(start=True, stop=True))], reads=[b_it, b_Pc], writes=[b_pz])
            yo, b_yo = yo_rot.next()
            P.op("dve", "tensor_tensor", yo, pz[:, 0:128], it[:, 128:256], ALU.add, reads=[b_pz, b_it], writes=[b_yo])
            for hh in range(2):
                hcol = (pr * 2 + hh) * 64
                P.dma("pool", self.rw_y[d, j * 64:(j + 1) * 64, hcol:hcol + 64], yo[hh * 64:(hh + 1) * 64, hh * 64:(hh + 1) * 64], reads=[b_yo])
            pz, b_pz = ps_rot.next()
            P.mm([((pz[:, 0:128], it[:, 256:384], Pc), dict(start=True, stop=True))], reads=[b_it, b_Pc], writes=[b_pz])
            P.op("dve", "scalar_tensor_tensor", Pn, pz[:, 0:128], it[:, 512:513], it[:, 384:512], ALU.mult, ALU.add,
                 reads=[b_pz, b_it], writes=[b_Pn])
            st[(d, pr)][1] = 1 - cur
            if i % 4 == 3:
                yield
        P.barrier()
        bc = self.b_const
        lnw = P.alloc([128, 2, 256], F32)
        ident = P.alloc([128, 128], F32)
        P.dma("sp", lnw, self.rw_ln[l:l + 1].rearrange("o a c -> o (a c)").partition_broadcast(128), writes=[bc])
        P.dma("sp", ident, self.c_ident.ap(), writes=[bc])
        TB = 128
        rot = lambda shape, n=2, dt=F32: Rot([(P.alloc(shape, dt), Buf()) for _ in range(n)])
        yf_r, yb_r, v_r, g_r, bo_r = rot([128, 256], 3), rot([128, 256], 3), rot([128, 256], 3), rot([128, 256], 3), rot([128, 4], 3)
        y_r, sq_r, st_r, o_r = rot([128, 256]), rot([128, 256]), rot([128, 4, 4]), rot([128, 256], 2, BF16)
        ps_rot = Rot([(banks[i], bb[i]) for i in bank_ids])
        blk0 = 0 if ctx_out else CTX // TB
        ldd = {}

        def loadc(blk):
            t0 = blk * TB
            yf, b_yf = yf_r.next(); yb, b_yb = yb_r.next(); vv, b_vv = v_r.next(); gg, b_gg = g_r.next(); bo, b_bo = bo_r.next()
            P.dma("sp", yf, self.rw_y[0, t0:t0 + TB, :], writes=[b_yf])
            P.dma("sp", yb, self.rw_y[1, t0:t0 + TB, :], writes=[b_yb])
            P.dma("sp", vv, self.rw_vtm[t0:t0 + TB, :], writes=[b_vv])
            P.dma("sp", gg, self.rw_gtm[t0:t0 + TB, :], writes=[b_gg])
            P.dma("sp", bo, self.rw_bon[t0:t0 + TB, :], writes=[b_bo])
            ldd[blk] = (yf, b_yf, yb, b_yb, vv, b_vv, gg, b_gg, bo, b_bo)
        loadc(blk0)
        for blk in range(blk0, T // TB):
            t0 = blk * TB
            if blk + 1 < T // TB:
                loadc(blk + 1)
            yf, b_yf, yb, b_yb, vv, b_vv, gg, b_gg, bo, b_bo = ldd.pop(blk)
            y, b_y = y_r.next(); sq, b_sq = sq_r.next(); stt, b_stt = st_r.next()
            P.op("dve", "tensor_tensor", y, yf, yb, ALU.add, reads=[b_yf, b_yb], writes=[b_y])
            y3 = y.rearrange("p (h n) -> p h n", h=4)
            P.op("act", "activation", sq, y, AF.Square, reads=[b_y], writes=[b_sq])
            P.op("dve", "tensor_reduce", stt[:, 0, :], y3, AX.X, ALU.add, reads=[b_y], writes=[b_stt])
            P.op("dve", "tensor_reduce", stt[:, 1, :], sq.rearrange("p (h n) -> p h n", h=4), AX.X, ALU.add, reads=[b_sq, b_stt], writes=[b_stt])
            P.op("dve", "tensor_single_scalar", stt[:, 0, :], stt[:, 0, :], 1.0 / 64, ALU.mult, reads=[b_stt], writes=[b_stt])
            P.op("dve", "tensor_tensor", stt[:, 2, :], stt[:, 0, :], stt[:, 0, :], ALU.mult, reads=[b_stt], writes=[b_stt])
            P.op("dve", "scalar_tensor_tensor", stt[:, 1, :], stt[:, 1, :], 1.0 / 64, stt[:, 2, :], ALU.mult, ALU.subtract, reads=[b_stt], writes=[b_stt])
            P.op("act", "activation", stt[:, 3, :], stt[:, 1, :], AF.Ln, bias=64e-5, scale=1.0, reads=[b_stt], writes=[b_stt])
            P.op("act", "activation", stt[:, 3, :], stt[:, 3, :], AF.Exp, scale=-0.5, reads=[b_stt], writes=[b_stt])
            for h in range(4):
                P.op("dve", "tensor_scalar", y3[:, h, :], y3[:, h, :], stt[:, 0, h:h + 1], stt[:, 3, h:h + 1], ALU.subtract, ALU.mult,
                     reads=[b_y, b_stt], writes=[b_y])
            P.op("dve", "tensor_tensor", y, y, lnw[:, 0, :], ALU.mult, reads=[b_y, bc], writes=[b_y])
            P.op("dve", "tensor_tensor", y, y, lnw[:, 1, :], ALU.add, reads=[b_y, bc], writes=[b_y])
            v3 = vv.rearrange("p (h n) -> p h n", h=4)
            for h in range(4):
                P.op("dve", "scalar_tensor_tensor", y3[:, h, :], v3[:, h, :], bo[:, h:h + 1], y3[:, h, :], ALU.mult, ALU.add,
                     reads=[b_vv, b_bo, b_y], writes=[b_y])
            P.op("dve", "tensor_tensor", y, y, gg, ALU.mult, reads=[b_y, b_gg], writes=[b_y])
            o, b_o = o_r.next()
            for pr in range(2):
                pz, b_pz = ps_rot.next()
                P.mm([((pz[:, 0:128], y[:, pr * 128:(pr + 1) * 128], ident), {})], reads=[b_y, bc], writes=[b_pz], meth="transpose")
                P.op("act", "copy", o[:, pr * 128:(pr + 1) * 128], pz[:, 0:128], reads=[b_pz], writes=[b_o])
                P.dma("pool", self.ymixT[768 + pr * 128:768 + (pr + 1) * 128, t0:t0 + TB], o[:, pr * 128:(pr + 1) * 128], reads=[b_o])
            yield

    def phase_rwkv_bc(self, l, ctx_out):
        self.P.push()
        for _ in self.rwkv_bc_gen(l, ctx_out, tuple(range(8))):
            pass
        self.P.pop("phase_rwkv_bc")

    def phase_out(self, l, src, dst, tiles=None):
        P = self.P
        P.push()
        wo = P.alloc([128, KC, D], BF16)
        b_wo = Buf()
        wsrc = self.w_out[l].rearrange("(kc p) n -> p kc n", p=128)
        for kc in range(KC):
            P.dma("pool", wo[:, kc, :], wsrc[:, kc, :], writes=[b_wo])
        xrot = Rot([(P.alloc([128, KC, NT], F32), Buf()) for _ in range(2)])
        yrot = Rot([(P.alloc([128, KC, NT], BF16), Buf()) for _ in range(2)])
        banks, bb = P.banks, P.bank_bufs
        pd_rot = Rot([(banks[i], bb[i]) for i in (1, 2, 3, 4)])
        srcv = src.rearrange("(kc p) t -> p kc t", p=128)
        dstv = dst.rearrange("(kc p) t -> p kc t", p=128)
        ymv = self.ymixT.rearrange("(kc p) t -> p kc t", p=128)
        gate = self.der[l]
        for (t0, n) in (tiles or TILES):
            s = 1 if t0 < CTX else 0
            xt, b_xt = xrot.next()
            yt, b_yt = yrot.next()
            P.dma("sp", xt[:, :, :n], srcv[:, :, t0:t0 + n], writes=[b_xt])
            P.dma("sp", yt[:, :, :n], ymv[:, :, t0:t0 + n], writes=[b_yt])
            for dc in range(KC):
                pd, b_pd = pd_rot.next()
                P.mm([((pd[:, :n], wo[:, kc, dc * 128:(dc + 1) * 128], yt[:, kc, :n]), dict(start=(kc == 0), stop=(kc == KC - 1)))
                      for kc in range(KC)], reads=[b_wo, b_yt], writes=[b_pd])
                P.op("dve", "scalar_tensor_tensor", xt[:, dc, :n], pd[:, :n], gate[:, 5, dc, s:s + 1], xt[:, dc, :n],
                     ALU.mult, ALU.add, reads=[b_pd, b_xt, self.b_der], writes=[b_xt])
            P.dma("pool", dstv[:, :, t0:t0 + n], xt[:, :, :n], reads=[b_xt])
        P.pop("phase_out")


def host_layout(inputs, b):
    m = {}
    x = inputs["x"][b]
    ctx = inputs["ctx"][b]
    m["xT"] = np.ascontiguousarray(np.concatenate([ctx, x], axis=0).T)
    cv = np.stack([inputs["c"][b], inputs["c_ctx"]], axis=-1)
    m["cvec"] = np.ascontiguousarray(cv.reshape(KC, 128, 2).transpose(1, 0, 2))
    return m


def rope_consts():
    tl = np.arange(SEQ)
    row = (tl // 64).astype(np.float64)
    col = (tl % 64).astype(np.float64)

    def table(n_freq, dims, lead):
        inv = 10000.0 ** (-np.arange(n_freq, dtype=np.float64) / n_freq)
        cos = np.ones((lead + dims, T), np.float64)
        sin = np.zeros((lead + dims, T), np.float64)
        for d in range(dims):
            m = d // 2
            ang = row * inv[m] if m < n_freq else col * inv[m - n_freq]
            cos[lead + d, CTX:] = np.cos(ang)
            sin[lead + d, CTX:] = np.sin(ang)
        return cos, sin
    cg, sg = table(16, 64, 0)
    rope_g = np.stack([np.tile(cg, (2, 1)), np.tile(sg, (2, 1))]).astype(np.float32)
    cm, sm = table(8, 32, 64)
    rope_m = np.stack([cm, sm]).astype(np.float32)
    rotm = np.zeros((128, 128), np.float32)
    for m in range(64):
        rotm[2 * m + 1, 2 * m] = -1.0
        rotm[2 * m, 2 * m + 1] = 1.0
    rot96 = np.zeros((96, 96), np.float32)
    for m in range(16):
        j0 = 64 + 2 * m
        rot96[j0 + 1, j0] = -1.0
        rot96[j0, j0 + 1] = 1.0
    bd64 = np.zeros((128, 128), np.float32)
    bd64[:64, :64] = 1.0
    bd64[64:, 64:] = 1.0
    return dict(rope_g=rope_g, rope_m=rope_m, c_rotm=rotm, c_rot96=rot96, c_bd64=bd64)


_HY_CACHE = {}


def hyena_consts():
    if _HY_CACHE:
        return _HY_CACHE
    m = {}
    m["c_ident"] = np.eye(128, dtype=np.float32)
    max_decay = math.log(1e-2) / 0.3
    min_decay = math.log(1e-2) / 1.5
    deltas = np.abs(np.linspace(min_decay, max_decay, HY_CH, dtype=np.float32))
    m["hy_ndelta"] = np.ascontiguousarray(np.broadcast_to(-np.tile(deltas, 2)[None, :], (128, 512))).astype(np.float32)
    for L in (SEQ, CTX):
        ntt = L // 128
        t01 = np.linspace(0.0, 1.0, L, dtype=np.float32)
        bands = 16
        w_ang = (np.float32(2.0 * math.pi) * np.arange(L, dtype=np.float32) / np.float32(L)).astype(np.float32)
        f = np.linspace(1e-4, bands - 1, bands, dtype=np.float32)
        arg = (f[None, :] * w_ang[:, None]).astype(np.float32)
        z = np.concatenate([t01[:, None], np.cos(arg), -np.sin(arg)], axis=-1).astype(np.float32)
        m[f"hy_z{L}"] = np.ascontiguousarray(z.T)
        m[f"hy_t01_{L}"] = np.ascontiguousarray(t01.reshape(ntt, 128).T)
        N = 2 * L
        t = np.arange(L, dtype=np.int64)
        kk = np.arange(L, dtype=np.int64)
        ph = ((2 * kk[None, :] + 1) * t[:, None]) % (2 * N)
        ang = ph.astype(np.float64) * (math.pi / N)
        mats = [np.cos(ang).astype(ml_dtypes.bfloat16), np.sin(ang).astype(ml_dtypes.bfloat16)]
        del ang, ph
        F = np.stack([M.reshape(ntt, 128, ntt, 128).transpose(2, 1, 0, 3) for M in mats])
        m[f"dftF{L}"] = np.ascontiguousarray(F)
        I = np.stack([M.reshape(L // 256, 256, ntt, 128).transpose(0, 3, 2, 1) for M in mats])
        m[f"dftI{L}"] = np.ascontiguousarray(I)
    _HY_CACHE.update(m)
    return _HY_CACHE


def host_shared(inputs):
    m = {}
    m.update(hyena_consts())
    cwv = np.concatenate([inputs["hy_conv_w"], inputs["hy_conv_b"][:, None, :]], axis=1)
    m["hy_cw"] = np.ascontiguousarray(cwv.reshape(DEPTH, 4, 6, 128).transpose(0, 3, 2, 1))
    m["hy_biasT"] = np.ascontiguousarray(inputs["hy_bias"].reshape(DEPTH, 2, 128).transpose(0, 2, 1))
    m["hy_fb"] = np.ascontiguousarray(np.stack([inputs["hy_f_b1"], inputs["hy_f_b2"], inputs["hy_f_b3"], inputs["hy_f_freq"]], axis=-1))
    for nm in ("hy_f_w1", "hy_f_w2", "hy_f_w3", "hy_f_w4", "rw_w2", "rw_a2", "rw_g2"):
        m[nm] = inputs[nm]
    m["rw_muT"] = np.ascontiguousarray(inputs["rw_mu"].reshape(DEPTH, 9, 128).transpose(0, 2, 1))
    wa = np.stack([inputs["rw_w0"], inputs["rw_a0"]], axis=1)
    m["rw_w0a0"] = np.ascontiguousarray(wa.reshape(DEPTH, 2, 2, 2, 128).transpose(0, 4, 1, 2, 3))
    vv = np.stack([inputs["rw_k_k"], inputs["rw_k_a"], inputs["rw_r_k"].reshape(DEPTH, 256)], axis=-1)
    m["rw_vecs"] = np.ascontiguousarray(vv.reshape(DEPTH, 2, 128, 3).transpose(0, 2, 1, 3))
    m["rw_ln"] = np.ascontiguousarray(np.stack([inputs["rw_ln_w"], inputs["rw_ln_b"]], axis=1))
    i_ = np.arange(64)
    S_f = (i_[:, None] < i_[None, :]).astype(np.float32)
    I_f = (i_[:, None] <= i_[None, :]).astype(np.float32)
    mk = np.zeros((128, 2, 2, 128), np.float32)
    for dd, (S_, I_) in enumerate(((S_f, I_f), (S_f.T, I_f.T))):
        for hb in range(2):
            mk[hb * 64:(hb + 1) * 64, dd, 0, hb * 64:(hb + 1) * 64] = S_
            mk[hb * 64:(hb + 1) * 64, dd, 1, hb * 64:(hb + 1) * 64] = I_
    m["rw_masks"] = mk
    bo2 = np.zeros((128, 2), np.float32)
    bo2[:64, 0] = 1.0
    bo2[64:, 1] = 1.0
    m["c_bdo2"] = bo2
    m["adab"] = np.ascontiguousarray(inputs["ada_b"].reshape(DEPTH, 72, 128).transpose(0, 2, 1))
    nr = np.stack([inputs["norm_ffn1"], inputs["norm_mix"], inputs["norm_ffn2"]], axis=1)
    m["norms"] = np.ascontiguousarray(nr.reshape(DEPTH, 3, KC, 128).transpose(0, 1, 3, 2))
    m["ada_w"] = inputs["ada_w"]
    for nm in ("ffn1_gate", "ffn1_up", "ffn1_down", "ffn2_gate", "ffn2_up", "ffn2_down", "w_out", "mla_w_uq", "mla_w_ukv"):
        m[nm] = inputs[nm]
    w_in = inputs["w_in"].copy()
    q0 = HY_COLS
    qc = inputs["w_in"][:, :, q0:q0 + 256].reshape(DEPTH, D, 4, 64)
    w_in[:, :, q0:q0 + 256] = qc[:, :, [0, 2, 1, 3], :].reshape(DEPTH, D, 256)
    m["w_in"] = w_in
    m.update(rope_consts())
    m["gqa_gain"] = np.ascontiguousarray(np.stack([np.tile(inputs["gqa_q_norm"], (1, 2)), np.tile(inputs["gqa_k_norm"], (1, 2))], axis=-1))
    mg = np.zeros((DEPTH, 128, 5), np.float32)
    mg[:, :, 0] = inputs["mla_cq_norm"][:, 0:128]
    mg[:, :, 1] = inputs["mla_cq_norm"][:, 128:256]
    mg[:, :, 2] = inputs["mla_ckv_norm"]
    mg[:, 0:96, 3] = inputs["mla_q_norm"]
    mg[:, 0:96, 4] = inputs["mla_k_norm"]
    m["mla_gain"] = mg
    return m


def build(dbg=False, stop=None):
    if stop == "rwa_only":
        k = K(dbg=dbg, pT_in=True)
        k.phase_rwkv_a(0)
        return k, k.P.finish()
    k = K(dbg=dbg)
    k.phase_mod()
    if stop in ("proj", "rw", "hy", "attn"):
        k.phase_ffn(0, 1, k.xT_in, k.xs)
        k.phase_proj(0, k.xs)
        if stop == "rw":
            k.phase_rwkv_a(0)
            if not os.environ.get("RWA_ONLY"):
                k.phase_rwkv_bc(0, True)
        if stop == "hy":
            k.phase_hyena(0, SEQ, CTX)
            k.phase_hyena(0, CTX, 0)
        if stop == "attn":
            k.phase_gqa(0, True)
            k.phase_mla(0, True)
        return k, k.P.finish()
    lat_tiles = [tl for tl in TILES if tl[0] >= CTX]
    for l in range(DEPTH):
        last = (l == DEPTH - 1)
        ctx_out = not last
        k.phase_ffn(l, 1, k.xT_in if l == 0 else k.xs, k.xs)
        k.phase_proj(l, k.xs)
        k.phase_hyena(l, SEQ, CTX)
        if ctx_out:
            k.phase_hyena(l, CTX, 0)
        k.phase_rwkv_a(l)
        k.phase_gqa(l, ctx_out, co_rwkv=True)
        k.phase_mla(l, ctx_out)
        k.phase_out(l, k.xs, k.xs, tiles=None if ctx_out else lat_tiles)
        if last:
            k.phase_ffn(l, 2, k.xs, k.xs, dst_lat=k.out, tiles=lat_tiles)
        else:
            k.phase_ffn(l, 2, k.xs, k.xs)
    nc = k.P.finish()
    return k, nc


def kernel(**inputs):
    inputs = {k_: np.asarray(v) for k_, v in inputs.items()}
    k, nc = build()
    shared = host_shared(inputs)
    in_maps = []
    for b in range(8):
        m = dict(shared)
        m.update(host_layout(inputs, b))
        in_maps.append(m)
    res = run_bass_kernel_spmd(nc, in_maps, core_ids=list(range(8)))
    out = np.stack([np.ascontiguousarray(r["outT"].T) for r in res.results], axis=0)
    return out.astype(np.float32)
```

```python
import contextlib
import math
import numpy as np
import ml_dtypes
import concourse.bass as bass
import concourse.mybir as mybir
from concourse.bass_utils import run_bass_kernel_spmd

F32 = mybir.dt.float32
BF16 = mybir.dt.bfloat16
ALU = mybir.AluOpType
AF = mybir.ActivationFunctionType
AX = mybir.AxisListType

D = 1024
KC = 8
SEQ = 4096
CTX = 256
T = SEQ + CTX
DEPTH = 2
D_FF = 2816
NF = D_FF // 128
N_MOD = 9
EPS = 1e-6
HY_CH = 256
HY_COLS = 768
GQA_COLS = 512
MLA_COLS = 416
RW_COLS = 1152
D_IN = 2848
NT = 256
TILES = [(t0, NT) for t0 in range(0, T, NT)]

COMPUTE = ("pe", "dve", "act", "pool")
QUEUES = ("sp", "act", "pool")
RING = 16
import os
STQ_RWA = os.environ.get('STQ_RWA', 'pool')
CUT = int(os.environ.get('RWA_CUT', '99'))
RW_BF16 = bool(int(os.environ.get('RW_BF16', '0')))
RW_DBL_BF16 = bool(int(os.environ.get('RW_DBL_BF16', '0')))
DEBUG = False


class Buf:
    __slots__ = ("name", "w", "r")

    def __init__(self, name=""):
        self.name = name
        self.w = None
        self.r = []


class Prog:
    def __init__(self, same_engine_sync=True):
        self.nc = bass.Bass("TRN2", target_bir_lowering=False)
        self.stack = contextlib.ExitStack()
        self.ops = {e: [] for e in ("pe", "dve", "act", "pool", "sp")}
        self.cnt = {e: 0 for e in COMPUTE}
        self.dma_i = {q: 0 for q in QUEUES}
        self.waited = {e: {} for e in self.ops}
        self.same_engine_sync = same_engine_sync
        self.semkeys = set()
        self.AW = 50688
        self.arena = self.stack.enter_context(self.nc.sbuf_tensor("arena", [128, self.AW], F32))
        self.arena_bf = self.arena.bitcast(BF16)
        self.off = 0
        self.marks = []
        self.phase_log = []
        self.banks = [self.stack.enter_context(self.nc.psum_tensor(f"bank{i}", [128, 512], F32)) for i in range(8)]
        self.bank_bufs = [Buf(f"bank{i}") for i in range(8)]

    def alloc(self, shape, dtype=F32):
        p = shape[0]
        n = int(np.prod(shape[1:]))
        words = n if dtype == F32 else (n + 1) // 2
        words = (words + 7) // 8 * 8
        off = self.off
        self.off += words
        assert self.off <= self.AW, f"SBUF arena overflow {self.off}"
        if dtype == F32:
            ap = self.arena[0:p, off:off + n]
        else:
            ap = self.arena_bf[0:p, 2 * off:2 * off + n]
        if len(shape) == 3:
            ap = ap.rearrange("p (a b) -> p a b", a=shape[1])
        elif len(shape) == 4:
            ap = ap.rearrange("p (a b c) -> p a b c", a=shape[1], b=shape[2])
        elif len(shape) == 5:
            ap = ap.rearrange("p (a b c d) -> p a b c d", a=shape[1], b=shape[2], c=shape[3])
        return ap

    def push(self):
        self.marks.append(self.off)

    def pop(self, label=None):
        self.barrier()
        self.off = self.marks.pop()
        if label:
            self.phase_log.append((label, {e: sum(1 for it in self.ops[e] if it[0] == "op" and it[2] == e) for e in COMPUTE}))

    def dram(self, name, shape, dtype=F32, kind="Internal"):
        return self.nc.dram_tensor(name, list(shape), dtype, kind=kind)

    def _need(self, eng, tok):
        if tok is None:
            return
        key, val = tok
        if key == eng and not self.same_engine_sync:
            return
        cur = self.waited[eng].get(key, 0)
        if cur >= val:
            return
        self.waited[eng][key] = val
        self.ops[eng].append(("wait", key, val))

    def _deps(self, eng, reads, writes, pe_accum=False, is_dma=False):
        for b in reads:
            self._need(eng, b.w)
        for b in writes:
            if b.w is not None and (is_dma or b.w[0] != eng) and not (pe_accum and b.w[0] == "pe"):
                self._need(eng, b.w)
            for t in b.r:
                if is_dma or t[0] != eng:
                    self._need(eng, t)

    def _mark(self, tok, reads, writes):
        for b in reads:
            b.r.append(tok)
            if len(b.r) > 16:
                best = {}
                for k, v in b.r:
                    if best.get(k, 0) < v:
                        best[k] = v
                b.r = list(best.items())
        for b in writes:
            b.w = tok
            b.r = []

    def op(self, eng, meth, *args, reads=(), writes=(), **kw):
        self._deps(eng, reads, writes)
        self.cnt[eng] += 1
        tok = (eng, self.cnt[eng])
        self.semkeys.add(eng)
        self.ops[eng].append(("op", (meth, args, kw), eng, 1))
        self._mark(tok, reads, writes)
        return tok

    def mm(self, calls, reads=(), writes=(), meth="matmul"):
        self._deps("pe", reads, writes, pe_accum=True)
        for (a, k) in calls[:-1]:
            self.ops["pe"].append(("op", (meth, a, k), None, 0))
        self.cnt["pe"] += 1
        tok = ("pe", self.cnt["pe"])
        self.semkeys.add("pe")
        a, k = calls[-1]
        self.ops["pe"].append(("op", (meth, a, k), "pe", 1))
        self._mark(tok, reads, writes)
        return tok

    def dma(self, q, out, in_, reads=(), writes=(), **kw):
        eng = q
        self._deps(eng, reads, writes, is_dma=True)
        i = self.dma_i[q]
        self.dma_i[q] += 1
        key = ("dma", q, i % RING)
        val = 16 * (i // RING + 1)
        if i >= RING:
            self._need(eng, (key, val - 16))
        self.semkeys.add(key)
        self.ops[eng].append(("op", ("dma_start", (out, in_), kw), key, 16))
        tok = (key, val)
        self._mark(tok, reads, writes)
        return tok

    def barrier(self):
        toks = [(e, self.cnt[e]) for e in COMPUTE if self.cnt[e] > 0]
        for q in QUEUES:
            n = self.dma_i[q]
            for j in range(max(0, n - RING), n):
                toks.append((("dma", q, j % RING), 16 * (j // RING + 1)))
        for e in self.ops:
            for t in toks:
                self._need(e, t)

    def finish(self):
        self.barrier()
        nc = self.nc
        sems = {}
        for key in sorted(self.semkeys, key=str):
            nm = key if isinstance(key, str) else f"d_{key[1]}_{key[2]}"
            sems[key] = self.stack.enter_context(nc.semaphore("s_" + nm))
        ops = self.ops

        def emit(e, lst):
            for it in lst:
                if it[0] == "wait":
                    e.wait_ge(sems[it[1]], it[2])
                else:
                    meth, a, k = it[1]
                    ins = getattr(e, meth)(*a, **k)
                    if it[2] is not None:
                        ins.then_inc(sems[it[2]], it[3])

        with nc.Block() as block:
            @block.sync
            def _(e):
                emit(e, ops["sp"])

            @block.tensor
            def _(e):
                emit(e, ops["pe"])

            @block.vector
            def _(e):
                emit(e, ops["dve"])

            @block.scalar
            def _(e):
                emit(e, ops["act"])

            @block.gpsimd
            def _(e):
                emit(e, ops["pool"])
        self.stack.close()
        return nc

    def stats(self):
        return ({e: sum(1 for it in l if it[0] == "op") for e, l in self.ops.items()},
                {e: sum(1 for it in l if it[0] == "wait") for e, l in self.ops.items()})


class Rot:
    def __init__(self, items):
        self.items = items
        self.i = 0

    def next(self):
        it = self.items[self.i % len(self.items)]
        self.i += 1
        return it


class K:
    def __init__(self, dbg=False, pT_in=False):
        self.P = P = Prog()
        self.dbg = dbg
        kin = "ExternalInput"
        sk = "ExternalOutput" if dbg else "Internal"
        self.xT_in = P.dram("xT", [D, T], F32, kin)
        self.cvec = P.dram("cvec", [128, KC, 2], F32, kin)
        self.adab = P.dram("adab", [DEPTH, 128, 72], F32, kin)
        self.norms = P.dram("norms", [DEPTH, 3, 128, KC], F32, kin)
        self.ada_w = P.dram("ada_w", [DEPTH, D, N_MOD * D], F32, kin)
        self.w_ffn = {}
        for nm in ("ffn1_gate", "ffn1_up", "ffn2_gate", "ffn2_up"):
            self.w_ffn[nm] = P.dram(nm, [DEPTH, D, D_FF], F32, kin)
        for nm in ("ffn1_down", "ffn2_down"):
            self.w_ffn[nm] = P.dram(nm, [DEPTH, D_FF, D], F32, kin)
        self.w_in = P.dram("w_in", [DEPTH, D, D_IN], F32, kin)
        self.w_out = P.dram("w_out", [DEPTH, D, D], F32, kin)
        self.rope_g = P.dram("rope_g", [2, 128, T], F32, kin)
        self.rope_m = P.dram("rope_m", [2, 96, T], F32, kin)
        self.c_bd64 = P.dram("c_bd64", [128, 128], F32, kin)
        self.c_rotm = P.dram("c_rotm", [128, 128], F32, kin)
        self.c_rot96 = P.dram("c_rot96", [96, 96], F32, kin)
        self.gqa_gain = P.dram("gqa_gain", [DEPTH, 128, 2], F32, kin)
        self.mla_gain = P.dram("mla_gain", [DEPTH, 128, 5], F32, kin)
        self.mla_w_uq = P.dram("mla_w_uq", [DEPTH, 256, 384], F32, kin)
        self.mla_w_ukv = P.dram("mla_w_ukv", [DEPTH, 128, 512], F32, kin)
        self.c_ident = P.dram("c_ident", [128, 128], F32, kin)
        self.hy_cw = P.dram("hy_cw", [DEPTH, 128, 6, 4], F32, kin)
        self.hy_biasT = P.dram("hy_biasT", [DEPTH, 128, 2], F32, kin)
        self.hy_fb = P.dram("hy_fb", [DEPTH, 64, 4], F32, kin)
        self.hy_f_w1 = P.dram("hy_f_w1", [DEPTH, 33, 64], F32, kin)
        self.hy_f_w2 = P.dram("hy_f_w2", [DEPTH, 64, 64], F32, kin)
        self.hy_f_w3 = P.dram("hy_f_w3", [DEPTH, 64, 64], F32, kin)
        self.hy_f_w4 = P.dram("hy_f_w4", [DEPTH, 64, 512], F32, kin)
        self.hy_ndelta = P.dram("hy_ndelta", [128, 512], F32, kin)
        self.hy_tabs = {}
        for L_ in (SEQ, CTX):
            ntt_ = L_ // 128
            self.hy_tabs[L_] = dict(
                z=P.dram(f"hy_z{L_}", [33, L_], F32, kin),
                t01=P.dram(f"hy_t01_{L_}", [128, ntt_], F32, kin),
                F=P.dram(f"dftF{L_}", [2, ntt_, 128, ntt_, 128], BF16, kin),
                I=P.dram(f"dftI{L_}", [2, L_ // 256, 128, ntt_, 256], BF16, kin))
        self.rw_muT = P.dram("rw_muT", [DEPTH, 128, 9], F32, kin)
        self.rw_w0a0 = P.dram("rw_w0a0", [DEPTH, 128, 2, 2, 2], F32, kin)
        self.rw_vecs = P.dram("rw_vecs", [DEPTH, 128, 2, 3], F32, kin)
        self.rw_w2 = P.dram("rw_w2", [DEPTH, 2, 64, 256], F32, kin)
        self.rw_a2 = P.dram("rw_a2", [DEPTH, 2, 64, 256], F32, kin)
        self.rw_g2 = P.dram("rw_g2", [DEPTH, 128, 256], F32, kin)
        self.rw_ln = P.dram("rw_ln", [DEPTH, 2, 256], F32, kin)
        self.rw_masks = P.dram("rw_masks", [128, 2, 2, 128], F32, kin)
        self.c_bdo2 = P.dram("c_bdo2", [128, 2], F32, kin)
        self.rwA = P.dram("rwA", [2, T // 64, 2, 128, 520], F32, sk)
        self.rw_y = P.dram("rw_y", [2, T, 256], F32, sk)
        self.rw_vtm = P.dram("rw_vtm", [T, 256], F32, sk)
        self.rw_gtm = P.dram("rw_gtm", [T, 256], F32, sk)
        self.rw_bon = P.dram("rw_bon", [T, 4], F32, sk)
        self.pT = P.dram("pT", [D_IN, T], F32, "ExternalInput" if pT_in else sk)
        self.vg = P.dram("vg", [T, 128], BF16, sk)
        self.ymixT = P.dram("ymixT", [D, T], BF16, sk)
        self.xs = P.dram("xs", [D, T], F32, sk)
        self.out = P.dram("outT", [D, SEQ], F32, "ExternalOutput")
        self.ones_bf = P.alloc([128, 128], BF16)
        self.b_const = Buf("const")
        P.op("pool", "memset", self.ones_bf, 1.0, writes=[self.b_const])
        self.der = [P.alloc([128, N_MOD, KC, 2], F32) for _ in range(DEPTH)]
        self.b_der = Buf("der")

    def dump(self, name, ap, shape, reads, dtype=F32):
        if not self.dbg:
            return
        d = self.P.dram(name, list(shape), dtype, "ExternalOutput")
        self.P.dma("sp", d.ap(), ap, reads=reads)

    def phase_mod(self):
        P = self.P
        P.push()
        cv = P.alloc([128, KC, 2], F32)
        sc = P.alloc([128, KC, 2], F32)
        b_cv = Buf()
        P.dma("sp", cv, self.cvec.ap(), writes=[b_cv])
        b_sc = Buf()
        P.op("act", "activation", sc, cv, AF.Silu, reads=[b_cv], writes=[b_sc])
        CB = 1152
        wrot = Rot([(P.alloc([128, KC, CB], F32), Buf()) for _ in range(2)])
        mod = P.alloc([128, N_MOD, KC, 2], F32)
        b_mod = Buf()
        adab = P.alloc([128, 72], F32)
        nrm = P.alloc([128, 3, KC], F32)
        b_ld = Buf()
        pm = self.P.banks[0]
        b_pm = self.P.bank_bufs[0]
        for l in range(DEPTH):
            P.dma("sp", adab, self.adab[l], writes=[b_ld])
            P.dma("sp", nrm, self.norms[l].rearrange("i p k -> p i k"), writes=[b_ld])
            for cb in range(N_MOD * D // CB):
                wt, b_wt = wrot.next()
                P.dma("sp", wt, self.ada_w[l].rearrange("(kc p) n -> p kc n", p=128)[:, :, cb * CB:(cb + 1) * CB], writes=[b_wt])
                for jj in range(CB // 128):
                    j = cb * (CB // 128) + jj
                    calls = [((pm[:, 2 * j:2 * j + 2], wt[:, kc, jj * 128:(jj + 1) * 128], sc[:, kc, :]),
                              dict(start=(kc == 0), stop=(kc == KC - 1))) for kc in range(KC)]
                    P.mm(calls, reads=[b_wt, b_sc], writes=[b_pm])
            pmv = pm[:, 0:144].rearrange("p (j s) -> p j s", s=2)
            modv = mod.rearrange("p n k s -> p (n k) s")
            for s in range(2):
                P.op("dve", "tensor_tensor", modv[:, :, s], pmv[:, :, s], adab, ALU.add,
                     reads=[b_pm, b_ld], writes=[b_mod])
            der = self.der[l]
            for i in range(3):
                for s in range(2):
                    P.op("dve", "tensor_copy", der[:, 3 * i, :, s], mod[:, 3 * i, :, s],
                         reads=[b_mod], writes=[self.b_der])
                    P.op("dve", "scalar_tensor_tensor", der[:, 3 * i + 1, :, s], mod[:, 3 * i + 1, :, s], 1.0,
                         nrm[:, i, :], ALU.add, ALU.mult, reads=[b_mod, b_ld], writes=[self.b_der])
                    f = 1.0 if i == 1 else 0.5
                    P.op("dve", "tensor_single_scalar", der[:, 3 * i + 2, :, s], mod[:, 3 * i + 2, :, s], f, ALU.mult,
                         reads=[b_mod], writes=[self.b_der])
            self.dump(f"dbg_der{l}", self.der[l].rearrange("p n k s -> p (n k s)"), [128, 144], [self.b_der])
            self.dump(f"dbg_mod{l}", mod.rearrange("p n k s -> p (n k s)"), [128, 144], [b_mod])
        P.pop("phase_mod")

    def emit_adaln(self, l, i, s, xt, b_xt, n, h, b_h, sq, b_sq, rstd, b_rstd, tmp_rot, ps, b_ps):
        P = self.P
        der = self.der[l]
        P.op("act", "activation", sq[:, :, :n], xt[:, :, :n], AF.Square, reads=[b_xt], writes=[b_sq])
        calls = [((ps[:, :n], self.ones_bf, sq[:, kc, :n]), dict(start=(kc == 0), stop=(kc == KC - 1))) for kc in range(KC)]
        P.mm(calls, reads=[b_sq, self.b_const], writes=[b_ps])
        P.op("act", "activation", rstd[:, :n], ps[:, :n], AF.Ln, bias=EPS, scale=1.0 / D, reads=[b_ps], writes=[b_rstd])
        P.op("act", "activation", rstd[:, :n], rstd[:, :n], AF.Exp, scale=-0.5, reads=[b_rstd], writes=[b_rstd])
        for kc in range(KC):
            tmp, b_tmp = tmp_rot.next()
            P.op("dve", "scalar_tensor_tensor", tmp[:, :n], xt[:, kc, :n], der[:, 3 * i + 1, kc, s:s + 1], rstd[:, :n],
                 ALU.mult, ALU.mult, reads=[b_xt, b_rstd, self.b_der], writes=[b_tmp])
            P.op("act", "activation", h[:, kc, :n], tmp[:, :n], AF.Identity, bias=der[:, 3 * i, kc, s:s + 1], scale=1.0,
                 reads=[b_tmp, self.b_der], writes=[b_h])

    def phase_ffn(self, l, which, src, dst, dst_lat=None, tiles=None):
        P = self.P
        i = 0 if which == 1 else 2
        P.push()
        wg = P.alloc([128, KC, D_FF], BF16)
        wu = P.alloc([128, KC, D_FF], BF16)
        wd = P.alloc([128, NF, D], BF16)
        b_wg, b_wu, b_wd = Buf(), Buf(), Buf()
        gsrc = self.w_ffn[f"ffn{which}_gate"][l].rearrange("(kc p) n -> p kc n", p=128)
        usrc = self.w_ffn[f"ffn{which}_up"][l].rearrange("(kc p) n -> p kc n", p=128)
        dsrc = self.w_ffn[f"ffn{which}_down"][l].rearrange("(f p) n -> p f n", p=128)
        for kc in range(KC):
            P.dma("pool", wg[:, kc, :], gsrc[:, kc, :], writes=[b_wg])
            P.dma("pool", wu[:, kc, :], usrc[:, kc, :], writes=[b_wu])
        for f0 in range(0, NF, 4):
            f1 = min(NF, f0 + 4)
            P.dma("pool", wd[:, f0:f1, :], dsrc[:, f0:f1, :], writes=[b_wd])
        xrot = Rot([(P.alloc([128, KC, NT], F32), Buf()) for _ in range(2)])
        hrot = Rot([(P.alloc([128, KC, NT], BF16), Buf()) for _ in range(2)])
        sq = P.alloc([128, KC, NT], BF16)
        b_sq = Buf()
        rstd = P.alloc([128, NT], F32)
        b_rstd = Buf()
        tmp_rot = Rot([(P.alloc([128, NT], F32), Buf()) for _ in range(2)])
        sg_rot = Rot([(P.alloc([128, NT], F32), Buf()) for _ in range(2)])
        a = P.alloc([128, NF, NT], BF16)
        b_a = [Buf() for _ in range(NF)]
        banks, bb = P.banks, P.bank_bufs
        pg_rot = Rot([(banks[1], bb[1]), (banks[2], bb[2])])
        pu_rot = Rot([(banks[3], bb[3]), (banks[4], bb[4])])
        pd_rot = Rot([(banks[5], bb[5]), (banks[6], bb[6])])
        srcv = src.rearrange("(kc p) t -> p kc t", p=128)
        gate = self.der[l]
        for (t0, n) in (tiles or TILES):
            s = 1 if t0 < CTX else 0
            xt, b_xt = xrot.next()
            h, b_h = hrot.next()
            P.dma("sp", xt[:, :, :n], srcv[:, :, t0:t0 + n], writes=[b_xt])
            self.emit_adaln(l, i, s, xt, b_xt, n, h, b_h, sq, b_sq, rstd, b_rstd, tmp_rot, banks[0], bb[0])
            if t0 == 0 and l == 0 and which == 1:
                self.dump("dbg_h", h.rearrange("p k n -> p (k n)"), [128, KC * NT], [b_h], BF16)
                self.dump("dbg_rstd", rstd, [128, NT], [b_rstd])
            for f in range(NF):
                pg, b_pg = pg_rot.next()
                pu, b_pu = pu_rot.next()
                P.mm([((pg[:, :n], wg[:, kc, f * 128:(f + 1) * 128], h[:, kc, :n]), dict(start=(kc == 0), stop=(kc == KC - 1)))
                      for kc in range(KC)], reads=[b_wg, b_h], writes=[b_pg])
                P.mm([((pu[:, :n], wu[:, kc, f * 128:(f + 1) * 128], h[:, kc, :n]), dict(start=(kc == 0), stop=(kc == KC - 1)))
                      for kc in range(KC)], reads=[b_wu, b_h], writes=[b_pu])
                sg, b_sg = sg_rot.next()
                P.op("act", "activation", sg[:, :n], pg[:, :n], AF.Silu, reads=[b_pg], writes=[b_sg])
                P.op("dve", "tensor_tensor", a[:, f, :n], sg[:, :n], pu[:, :n], ALU.mult, reads=[b_sg, b_pu], writes=[b_a[f]])
            if t0 == 0 and l == 0 and which == 1:
                self.dump("dbg_a", a.rearrange("p k n -> p (k n)"), [128, NF * NT], b_a, BF16)
            for dc in range(KC):
                pd, b_pd = pd_rot.next()
                P.mm([((pd[:, :n], wd[:, f, dc * 128:(dc + 1) * 128], a[:, f, :n]), dict(start=(f == 0), stop=(f == NF - 1)))
                      for f in range(NF)], reads=[b_wd] + b_a, writes=[b_pd])
                P.op("dve", "scalar_tensor_tensor", xt[:, dc, :n], pd[:, :n], gate[:, 3 * i + 2, dc, s:s + 1], xt[:, dc, :n],
                     ALU.mult, ALU.add, reads=[b_pd, b_xt, self.b_der], writes=[b_xt])
            if dst_lat is not None:
                if t0 >= CTX:
                    dv = dst_lat.rearrange("(kc p) t -> p kc t", p=128)
                    P.dma("pool", dv[:, :, t0 - CTX:t0 - CTX + n], xt[:, :, :n], reads=[b_xt])
            else:
                dv = dst.rearrange("(kc p) t -> p kc t", p=128)
                P.dma("pool", dv[:, :, t0:t0 + n], xt[:, :, :n], reads=[b_xt])
        P.pop("phase_ffn")


    def phase_proj(self, l, src):
        P = self.P
        P.push()
        win = P.alloc([128, KC, D_IN], BF16)
        b_win = Buf()
        wsrc = self.w_in[l].rearrange("(kc p) n -> p kc n", p=128)
        for kc in range(KC):
            P.dma("pool", win[:, kc, :], wsrc[:, kc, :], writes=[b_win])
        xrot = Rot([(P.alloc([128, KC, NT], F32), Buf()) for _ in range(2)])
        hrot = Rot([(P.alloc([128, KC, NT], BF16), Buf()) for _ in range(2)])
        sq = P.alloc([128, KC, NT], BF16)
        b_sq = Buf()
        rstd = P.alloc([128, NT], F32)
        b_rstd = Buf()
        tmp_rot = Rot([(P.alloc([128, NT], F32), Buf()) for _ in range(2)])
        NCH = 23
        st_rot = Rot([(P.alloc([128, NCH, NT], F32), Buf()) for _ in range(2)])
        vst_rot = Rot([(P.alloc([128, 2, 128], BF16), Buf()) for _ in range(2)])
        banks, bb = P.banks, P.bank_bufs
        pp_rot = Rot([(banks[i], bb[i]) for i in (1, 2, 3, 4)])
        pv_rot = Rot([(banks[i], bb[i]) for i in (5, 6)])
        srcv = src.rearrange("(kc p) t -> p kc t", p=128)
        pTv = self.pT[0:2816, :].rearrange("(c p) t -> p c t", p=128)
        VG0 = HY_COLS + 384
        for (t0, n) in TILES:
            s = 1 if t0 < CTX else 0
            xt, b_xt = xrot.next()
            h, b_h = hrot.next()
            P.dma("sp", xt[:, :, :n], srcv[:, :, t0:t0 + n], writes=[b_xt])
            self.emit_adaln(l, 1, s, xt, b_xt, n, h, b_h, sq, b_sq, rstd, b_rstd, tmp_rot, banks[0], bb[0])
            st, b_st = st_rot.next()
            for c in range(NCH):
                rows = 128 if c < 22 else 32
                pp, b_pp = pp_rot.next()
                P.mm([((pp[0:rows, :n], win[:, kc, c * 128:c * 128 + rows], h[:, kc, :n]), dict(start=(kc == 0), stop=(kc == KC - 1)))
                      for kc in range(KC)], reads=[b_win, b_h], writes=[b_pp])
                if c % 2 == 0:
                    P.op("act", "copy", st[0:rows, c, :n], pp[0:rows, :n], reads=[b_pp], writes=[b_st])
                else:
                    P.op("dve", "tensor_copy", st[0:rows, c, :n], pp[0:rows, :n], reads=[b_pp], writes=[b_st])
            P.dma("pool", pTv[:, :, t0:t0 + n], st[:, 0:22, :n], reads=[b_st])
            P.dma("pool", self.pT[2816:2848, t0:t0 + n], st[0:32, 22, :n], reads=[b_st])
            vst, b_vst = vst_rot.next()
            for sub in range(n // 128):
                pv, b_pv = pv_rot.next()
                P.mm([((pv[:, 0:128], h[:, kc, sub * 128:(sub + 1) * 128], win[:, kc, VG0:VG0 + 128]), dict(start=(kc == 0), stop=(kc == KC - 1)))
                      for kc in range(KC)], reads=[b_win, b_h], writes=[b_pv])
                P.op("dve", "tensor_copy", vst[:, sub, :], pv[:, 0:128], reads=[b_pv], writes=[b_vst])
            P.dma("pool", self.vg[t0:t0 + n, :].rearrange("(s p) c -> p s c", p=128), vst[:, 0:n // 128, :], reads=[b_vst])
        P.pop("phase_proj")

    def emit_headnorm(self, src, b_src, rows, n, ndim, ones_f, gain, cosv, sinv, rotm, out_bf, b_out, wk, psA, b_psA, psB, b_psB):
        P = self.P
        sq, b_sq, rstd, b_rstd, qn, b_qn, t1, b_t1 = wk
        P.op("act", "activation", sq[0:rows, :n], src, AF.Square, reads=[b_src], writes=[b_sq])
        P.mm([((psA[0:rows, :n], ones_f, sq[0:rows, :n]), dict(start=True, stop=True))], reads=[b_sq, self.b_const], writes=[b_psA])
        P.op("act", "activation", rstd[0:rows, :n], psA[0:rows, :n], AF.Ln, bias=EPS, scale=1.0 / ndim, reads=[b_psA], writes=[b_rstd])
        P.op("act", "activation", rstd[0:rows, :n], rstd[0:rows, :n], AF.Exp, scale=-0.5, reads=[b_rstd], writes=[b_rstd])
        P.op("dve", "scalar_tensor_tensor", qn[0:rows, :n], src, gain, rstd[0:rows, :n], ALU.mult, ALU.mult,
             reads=[b_src, b_rstd, self.b_const], writes=[b_qn])
        P.mm([((psB[0:rows, :n], rotm, qn[0:rows, :n]), dict(start=True, stop=True))], reads=[b_qn, self.b_const], writes=[b_psB])
        P.op("dve", "tensor_tensor", t1[0:rows, :n], qn[0:rows, :n], cosv, ALU.mult, reads=[b_qn, self.b_const], writes=[b_t1])
        P.op("dve", "tensor_tensor", qn[0:rows, :n], psB[0:rows, :n], sinv, ALU.mult, reads=[b_psB, self.b_const], writes=[b_qn])
        if isinstance(out_bf, list):
            for (sl, ap) in out_bf:
                P.op("dve", "tensor_tensor", ap, t1[sl, :n], qn[sl, :n], ALU.add, reads=[b_t1, b_qn], writes=[b_out])
        else:
            P.op("dve", "tensor_tensor", out_bf, t1[0:rows, :n], qn[0:rows, :n], ALU.add, reads=[b_t1, b_qn], writes=[b_out])

    def emit_attn(self, heads, scale, K_rows, n_kt, ebufs, b_stage_rot, ctx_out, co=None, co_every=8):
        P = self.P
        banks, bb = P.banks, P.bank_bufs
        st_rot = Rot([(banks[i], bb[i]) for i in (0, 1, 2, 3)])
        acc_rot = Rot([(banks[i], bb[i]) for i in (4, 5)])
        QN = 512
        LA = 3
        qtiles = [(CTX + i * QN, QN, n_kt) for i in range(SEQ // QN)]
        if ctx_out:
            qtiles = [(0, CTX, CTX // 128)] + qtiles
        for hd in heads:
            for (q0, qn_, nk) in qtiles:
                acc, b_acc = acc_rot.next()
                pend = []

                def pv(item, last):
                    kt_, eb_, b_eb_ = item
                    P.mm([((acc[:, :qn_], hd["v"](kt_), eb_[:, :qn_]), dict(start=(kt_ == 0), stop=last))],
                         reads=[hd["b_v"], b_eb_], writes=[b_acc])
                for kt in range(nk):
                    st, b_st = st_rot.next()
                    P.mm([((st[:, :qn_], hd["k"][:, kt * 128:(kt + 1) * 128], hd["q"][:, q0:q0 + qn_]), dict(start=True, stop=True))],
                         reads=[hd["b_k"], hd["b_q"]], writes=[b_st])
                    eb, b_eb = ebufs.next()
                    P.op("act", "activation", eb[:, :qn_], st[:, :qn_], AF.Exp, scale=scale, reads=[b_st], writes=[b_eb])
                    pend.append((kt, eb, b_eb))
                    if len(pend) > LA:
                        pv(pend.pop(0), False)
                    if co is not None and kt % co_every == co_every - 1:
                        if next(co, "end") == "end":
                            co = None
                while pend:
                    it_ = pend.pop(0)
                    pv(it_, len(pend) == 0)
                (rec, y), b_y = b_stage_rot.next()
                P.op("dve", "reciprocal", rec[0:64, :qn_], acc[64:128, :qn_], reads=[b_acc], writes=[b_y])
                P.op("dve", "tensor_tensor", y[0:64, :qn_], acc[0:64, :qn_], rec[0:64, :qn_], ALU.mult, reads=[b_acc, b_y], writes=[b_y])
                r0 = hd["out_row"]
                P.dma("pool", self.ymixT[r0:r0 + 64, q0:q0 + qn_], y[0:64, :qn_], reads=[b_y])
        if co is not None:
            for _ in co:
                pass

    def phase_gqa(self, l, ctx_out, co_rwkv=False):
        P = self.P
        P.push()
        cosg = P.alloc([128, T], F32)
        sing = P.alloc([128, T], F32)
        bd64 = P.alloc([128, 128], F32)
        rotm = P.alloc([128, 128], F32)
        gain = P.alloc([128, 2], F32)
        bc = self.b_const
        P.dma("sp", cosg, self.rope_g[0], writes=[bc])
        P.dma("sp", sing, self.rope_g[1], writes=[bc])
        P.dma("sp", bd64, self.c_bd64.ap(), writes=[bc])
        P.dma("sp", rotm, self.c_rotm.ap(), writes=[bc])
        P.dma("sp", gain, self.gqa_gain[l], writes=[bc])
        qr = P.alloc([128, 4, T], BF16)
        kr = P.alloc([128, T], BF16)
        b_qr, b_kr = Buf(), Buf()
        P.op("pool", "memset", qr, 0.0, writes=[b_qr])
        vaug = P.alloc([128, T // 128, 2, 128], BF16)
        b_v = Buf()
        P.op("pool", "memset", vaug, 1.0, writes=[b_v])
        vgv = self.vg.rearrange("(kt p) c -> p kt c", p=128)
        for g in range(2):
            P.dma("sp", vaug[:, :, g, 0:64], vgv[:, :, g * 64:(g + 1) * 64], writes=[b_v])
        QN = 512
        prot = Rot([(P.alloc([128, QN], F32), Buf()) for _ in range(2)])
        wk = (P.alloc([128, QN], F32), Buf(), P.alloc([128, QN], F32), Buf(), P.alloc([128, QN], F32), Buf(), P.alloc([128, QN], F32), Buf())
        banks, bb = P.banks, P.bank_bufs
        G0 = HY_COLS
        for r in range(3):
            for t0 in range(0, T, QN):
                n = min(QN, T - t0)
                pt, b_pt = prot.next()
                P.dma("sp", pt[:, :n], self.pT[G0 + r * 128:G0 + (r + 1) * 128, t0:t0 + n], writes=[b_pt])
                outap = [(slice(0, 64), qr[0:64, 2 * r, t0:t0 + n]), (slice(64, 128), qr[64:128, 2 * r + 1, t0:t0 + n])] if r < 2 else kr[:, t0:t0 + n]
                self.emit_headnorm(pt[:, :n], b_pt, 128, n, 64, bd64, gain[:, (0 if r < 2 else 1):(1 if r < 2 else 2)],
                                   cosg[:, t0:t0 + n], sing[:, t0:t0 + n], rotm, outap, (b_qr if r < 2 else b_kr), wk,
                                   banks[6], bb[6], banks[7], bb[7])
        ebufs = Rot([(P.alloc([128, QN], BF16), Buf()) for _ in range(4)])
        stage = Rot([((P.alloc([128, QN], F32), P.alloc([128, QN], BF16)), Buf()) for _ in range(2)])
        heads = []
        for r in range(2):
            for half in range(2):
                base = half * 64
                heads.append(dict(q=qr[:, 2 * r + half, :], b_q=b_qr, k=kr[:, :], b_k=b_kr,
                                  v=(lambda kt, half=half: vaug[:, kt, half, :]), b_v=b_v,
                                  out_row=256 + (half * 2 + r) * 64))
        co = self.rwkv_bc_gen(l, ctx_out, (6, 7)) if co_rwkv else None
        self.emit_attn(heads, 64 ** -0.5, 64, T // 128, ebufs, stage, ctx_out, co=co, co_every=8)
        P.pop("phase_gqa")

    def phase_mla(self, l, ctx_out):
        P = self.P
        P.push()
        bc = self.b_const
        cosm = P.alloc([96, T], F32)
        sinm = P.alloc([96, T], F32)
        ones_f = P.alloc([128, 128], F32)
        rot96 = P.alloc([96, 96], F32)
        gain = P.alloc([128, 5], F32)
        P.dma("sp", cosm, self.rope_m[0], writes=[bc])
        P.dma("sp", sinm, self.rope_m[1], writes=[bc])
        P.op("pool", "memset", ones_f, 1.0, writes=[bc])
        P.dma("sp", rot96, self.c_rot96.ap(), writes=[bc])
        P.dma("sp", gain, self.mla_gain[l], writes=[bc])
        wuq = P.alloc([128, 2, 384], BF16)
        wuk = P.alloc([128, 4, 64], BF16)
        wuv = P.alloc([128, 4, 64], BF16)
        P.dma("pool", wuq, self.mla_w_uq[l].rearrange("(c p) n -> p c n", p=128), writes=[bc])
        ukv = self.mla_w_ukv[l].rearrange("p (h two d) -> p h two d", h=4, two=2)
        P.dma("pool", wuk, ukv[:, :, 0, :], writes=[bc])
        P.dma("pool", wuv, ukv[:, :, 1, :], writes=[bc])
        qr = [P.alloc([96, T], BF16) for _ in range(4)]
        kr = [P.alloc([96, T], BF16) for _ in range(4)]
        b_qr = [Buf() for _ in range(4)]
        b_kr = [Buf() for _ in range(4)]
        vaug = P.alloc([128, T // 128, 4, 128], BF16)
        b_v = Buf()
        P.op("pool", "memset", vaug, 1.0, writes=[b_v])
        QN = 512
        M0 = HY_COLS + GQA_COLS
        cq_rot = Rot([(P.alloc([128, 3, QN], F32), Buf()) for _ in range(1)])
        sqc = P.alloc([128, 3, QN], F32)
        b_sqc = Buf()
        rs2 = P.alloc([128, 2, QN], F32)
        b_rs2 = Buf()
        cqn = P.alloc([128, 3, QN], BF16)
        b_cqn = Buf()
        qs_rot = Rot([(P.alloc([96, QN], F32), Buf()) for _ in range(2)])
        kf_rot = Rot([(P.alloc([96, QN], F32), Buf()) for _ in range(2)])
        wk = (P.alloc([128, QN], F32), Buf(), P.alloc([128, QN], F32), Buf(), P.alloc([128, QN], F32), Buf(), P.alloc([128, QN], F32), Buf())
        banks, bb = P.banks, P.bank_bufs
        pq_rot = Rot([(banks[i], bb[i]) for i in (1, 2)])
        pTc = self.pT[M0:M0 + 384, :].rearrange("(c p) t -> p c t", p=128)
        for t0 in range(0, T, QN):
            n = min(QN, T - t0)
            cq, b_cq = cq_rot.next()
            P.dma("sp", cq[:, :, :n], pTc[:, :, t0:t0 + n], writes=[b_cq])
            P.op("act", "activation", sqc[:, :, :n], cq[:, :, :n], AF.Square, reads=[b_cq], writes=[b_sqc])
            P.mm([((banks[6][:, :n], ones_f, sqc[:, c, :n]), dict(start=(c == 0), stop=(c == 1))) for c in range(2)],
                 reads=[b_sqc, bc], writes=[bb[6]])
            P.mm([((banks[7][:, :n], ones_f, sqc[:, 2, :n]), dict(start=True, stop=True))], reads=[b_sqc, bc], writes=[bb[7]])
            P.op("act", "activation", rs2[:, 0, :n], banks[6][:, :n], AF.Ln, bias=EPS, scale=1.0 / 256, reads=[bb[6]], writes=[b_rs2])
            P.op("act", "activation", rs2[:, 1, :n], banks[7][:, :n], AF.Ln, bias=EPS, scale=1.0 / 128, reads=[bb[7]], writes=[b_rs2])
            P.op("act", "activation", rs2[:, :, :n], rs2[:, :, :n], AF.Exp, scale=-0.5, reads=[b_rs2], writes=[b_rs2])
            for c in range(3):
                P.op("dve", "scalar_tensor_tensor", cqn[:, c, :n], cq[:, c, :n], gain[:, c:c + 1], rs2[:, (0 if c < 2 else 1), :n],
                     ALU.mult, ALU.mult, reads=[b_cq, b_rs2, bc], writes=[b_cqn])
            for h in range(4):
                pq, b_pq = pq_rot.next()
                P.mm([((pq[0:96, :n], wuq[:, c, h * 96:(h + 1) * 96], cqn[:, c, :n]), dict(start=(c == 0), stop=(c == 1))) for c in range(2)],
                     reads=[b_cqn, bc], writes=[b_pq])
                qs, b_qs = qs_rot.next()
                P.op("act", "copy", qs[:, :n], pq[0:96, :n], reads=[b_pq], writes=[b_qs])
                self.emit_headnorm(qs[:, :n], b_qs, 96, n, 96, ones_f[0:96, 0:96], gain[0:96, 3:4], cosm[:, t0:t0 + n], sinm[:, t0:t0 + n],
                                   rot96, qr[h][:, t0:t0 + n], b_qr[h], wk, banks[6], bb[6], banks[7], bb[7])
                pk, b_pk = pq_rot.next()
                P.mm([((pk[0:64, :n], wuk[:, h, :], cqn[:, 2, :n]), dict(start=True, stop=True))], reads=[b_cqn, bc], writes=[b_pk])
                kf, b_kf = kf_rot.next()
                P.dma("sp", kf[64:96, :n], self.pT[M0 + 384:M0 + 416, t0:t0 + n], writes=[b_kf])
                P.op("act", "copy", kf[0:64, :n], pk[0:64, :n], reads=[b_pk], writes=[b_kf])
                self.emit_headnorm(kf[:, :n], b_kf, 96, n, 96, ones_f[0:96, 0:96], gain[0:96, 4:5], cosm[:, t0:t0 + n], sinm[:, t0:t0 + n],
                                   rot96, kr[h][:, t0:t0 + n], b_kr[h], wk, banks[6], bb[6], banks[7], bb[7])
            for sub in range(n // 128):
                kt = t0 // 128 + sub
                pv, b_pv = pq_rot.next()
                P.mm([((pv[:, 0:256], cqn[:, 2, sub * 128:(sub + 1) * 128], wuv.rearrange("p h d -> p (h d)")), dict(start=True, stop=True))],
                     reads=[b_cqn, bc], writes=[b_pv])
                P.op("dve", "tensor_copy", vaug[:, kt, :, 0:64], pv[:, 0:256].rearrange("p (h d) -> p h d", h=4), reads=[b_pv], writes=[b_v])
        ebufs = Rot([(P.alloc([128, QN], BF16), Buf()) for _ in range(4)])
        stage = Rot([((P.alloc([128, QN], F32), P.alloc([128, QN], BF16)), Buf()) for _ in range(2)])
        heads = [dict(q=qr[h], b_q=b_qr[h], k=kr[h], b_k=b_kr[h], v=(lambda kt, h=h: vaug[:, kt, h, :]), b_v=b_v,
                      out_row=512 + h * 64) for h in range(4)]
        self.emit_attn(heads, 96 ** -0.5, 96, T // 128, ebufs, stage, ctx_out)
        P.pop("phase_mla")

    def emit_sin(self, ps, rows, n, bcol, fb, out, b_ps, b_out, wk):
        P = self.P
        pre, b_pre, r, b_r = wk
        MAGIC = 12582912.0
        TWO_PI = 2.0 * math.pi
        P.op("dve", "tensor_scalar", pre[0:rows, :n], ps, fb[0:rows, bcol:bcol + 1], fb[0:rows, 3:4], ALU.add, ALU.mult,
             reads=[b_ps, self.b_const], writes=[b_pre])
        P.op("dve", "tensor_scalar", r[0:rows, :n], pre[0:rows, :n], 1.0 / TWO_PI, MAGIC, ALU.mult, ALU.add, reads=[b_pre], writes=[b_r])
        P.op("dve", "tensor_scalar", r[0:rows, :n], r[0:rows, :n], MAGIC, -TWO_PI, ALU.subtract, ALU.mult, reads=[b_r], writes=[b_r])
        P.op("dve", "tensor_tensor", r[0:rows, :n], r[0:rows, :n], pre[0:rows, :n], ALU.add, reads=[b_r, b_pre], writes=[b_r])
        P.op("dve", "tensor_scalar", r[0:rows, :n], r[0:rows, :n], math.pi, -math.pi, ALU.min, ALU.max, reads=[b_r], writes=[b_r])
        P.op("act", "activation", out, r[0:rows, :n], AF.Sin, reads=[b_r], writes=[b_out])

    def phase_hyena(self, l, L, col0):
        P = self.P
        bc = self.b_const
        banks, bb = P.banks, P.bank_bufs
        ntt = L // 128
        tabs = self.hy_tabs[L]
        P.push()
        u = P.alloc([128, 2, L], F32)
        b_u = Buf()
        Akt = P.alloc([128, ntt, 256], BF16)
        Bkt = P.alloc([128, ntt, 256], BF16)
        b_AB = Buf()
        cw = P.alloc([128, 6, 4], F32)
        hbias = P.alloc([128, 2], F32)
        P.dma("sp", cw, self.hy_cw[l], writes=[bc])
        P.dma("sp", hbias, self.hy_biasT[l], writes=[bc])
        P.push()
        rhsC = P.alloc([128, ntt, 512], BF16)
        rhsS = P.alloc([128, ntt, 512], BF16)
        b_rhs = Buf()
        P.push()
        ident = P.alloc([128, 128], F32)
        P.dma("sp", ident, self.c_ident.ap(), writes=[bc])
        raw = P.alloc([128, L + 2], F32)
        b_raw = Buf()
        x1c = P.alloc([128, L], F32)
        b_x1c = Buf()
        vc = P.alloc([128, L], F32)
        b_vc = Buf()
        P.op("pool", "memset", raw[:, 0:1], 0.0, writes=[b_raw])
        P.op("pool", "memset", raw[:, L + 1:L + 2], 0.0, writes=[b_raw])

        def conv(j6, out, b_out):
            P.dma("sp", raw[:, 1:L + 1], self.pT[j6 * 128:(j6 + 1) * 128, col0:col0 + L], writes=[b_raw])
            P.op("dve", "tensor_scalar", out, raw[:, 1:L + 1], cw[:, j6, 1:2], cw[:, j6, 3:4], ALU.mult, ALU.add, reads=[b_raw, bc], writes=[b_out])
            P.op("dve", "scalar_tensor_tensor", out, raw[:, 0:L], cw[:, j6, 0:1], out, ALU.mult, ALU.add, reads=[b_raw, b_out, bc], writes=[b_out])
            P.op("dve", "scalar_tensor_tensor", out, raw[:, 2:L + 2], cw[:, j6, 2:3], out, ALU.mult, ALU.add, reads=[b_raw, b_out, bc], writes=[b_out])
        tr_rot = Rot([(banks[i], bb[i]) for i in (1, 2, 3, 4)])
        for j in range(2):
            conv(j, x1c, b_x1c)
            conv(4 + j, vc, b_vc)
            P.op("dve", "tensor_tensor", u[:, j, :], x1c, vc, ALU.mult, reads=[b_x1c, b_vc], writes=[b_u])
            for tt in range(ntt):
                pt, b_pt = tr_rot.next()
                P.mm([((pt[:, 0:128], u[:, j, tt * 128:(tt + 1) * 128], ident), {})], reads=[b_u, bc], writes=[b_pt], meth="transpose")
                P.op("act", "copy", rhsC[:, tt, j * 128:(j + 1) * 128], pt[:, 0:128], reads=[b_pt], writes=[b_rhs])
                P.op("dve", "tensor_copy", rhsS[:, tt, j * 128:(j + 1) * 128], pt[:, 0:128], reads=[b_pt], writes=[b_rhs])
        P.pop()
        P.push()
        z = P.alloc([33, L], F32)
        w1 = P.alloc([33, 64], F32)
        w2 = P.alloc([64, 64], F32)
        w3 = P.alloc([64, 64], F32)
        w4 = P.alloc([64, 512], F32)
        fb = P.alloc([64, 4], F32)
        t01 = P.alloc([128, ntt], F32)
        ndel = P.alloc([128, 512], F32)
        P.dma("sp", z, tabs["z"].ap(), writes=[bc])
        P.dma("sp", w1, self.hy_f_w1[l], writes=[bc])
        P.dma("sp", w2, self.hy_f_w2[l], writes=[bc])
        P.dma("sp", w3, self.hy_f_w3[l], writes=[bc])
        P.dma("sp", w4, self.hy_f_w4[l], writes=[bc])
        P.dma("sp", fb, self.hy_fb[l], writes=[bc])
        P.dma("sp", t01, tabs["t01"].ap(), writes=[bc])
        P.dma("sp", ndel, self.hy_ndelta.ap(), writes=[bc])
        hA = P.alloc([64, L], F32)
        hB = P.alloc([64, L], F32)
        b_hA, b_hB = Buf(), Buf()
        wk = (P.alloc([64, 512], F32), Buf(), P.alloc([64, 512], F32), Buf())
        CW = min(512, L)
        ps_rot = Rot([(banks[i], bb[i]) for i in (1, 2, 3, 4)])
        for (lhs, K_, src, b_src, dst, b_dst, bcol) in ((w1, 33, z, bc, hA, b_hA, 0), (w2, 64, hA, b_hA, hB, b_hB, 1), (w3, 64, hB, b_hB, hA, b_hA, 2)):
            for c0 in range(0, L, CW):
                ps, b_ps = ps_rot.next()
                P.mm([((ps[0:64, :CW], lhs, src[0:K_, c0:c0 + CW]), dict(start=True, stop=True))], reads=[b_src, bc], writes=[b_ps])
                self.emit_sin(ps[0:64, :CW], 64, CW, bcol, fb, dst[:, c0:c0 + CW], b_ps, b_dst, wk)
        wt_rot = Rot([(P.alloc([128, 512], F32), Buf()) for _ in range(2)])
        ft_rot = Rot([(P.alloc([128, 512], F32), Buf()) for _ in range(2)])
        for tt in range(ntt):
            ps, b_ps = ps_rot.next()
            P.mm([((ps[:, :], hA[:, tt * 128:(tt + 1) * 128], w4), dict(start=True, stop=True))], reads=[b_hA, bc], writes=[b_ps])
            wt, b_wt = wt_rot.next()
            P.op("act", "activation", wt, ndel, AF.Exp, scale=t01[:, tt:tt + 1], reads=[bc], writes=[b_wt])
            ft, b_ft = ft_rot.next()
            P.op("dve", "tensor_tensor", ft, ps[:, :], wt, ALU.mult, reads=[b_ps, b_wt], writes=[b_ft])
            P.op("dve", "tensor_tensor", rhsC[:, tt, 256:512], ft[:, 0:256], ft[:, 256:512], ALU.add, reads=[b_ft], writes=[b_rhs])
            P.op("dve", "tensor_tensor", rhsS[:, tt, 256:512], ft[:, 256:512], ft[:, 0:256], ALU.subtract, reads=[b_ft], writes=[b_rhs])
        P.pop()
        P.push()
        ck_rot = Rot([((P.alloc([128, ntt, 128], BF16), P.alloc([128, ntt, 128], BF16)), Buf()) for _ in range(2)])
        ec = P.alloc([128, 512], F32)
        es = P.alloc([128, 512], F32)
        b_ec, b_es = Buf(), Buf()
        t1 = P.alloc([128, 256], F32)
        t2 = P.alloc([128, 256], F32)
        b_t1, b_t2 = Buf(), Buf()
        pc_rot = Rot([(banks[i], bb[i]) for i in (1, 2)])
        psn_rot = Rot([(banks[i], bb[i]) for i in (3, 4)])
        for kt in range(ntt):
            (ck, sk), b_ck = ck_rot.next()
            P.dma("sp", ck, tabs["F"][0, kt], writes=[b_ck])
            P.dma("sp", sk, tabs["F"][1, kt], writes=[b_ck])
            pc, b_pc = pc_rot.next()
            psn, b_psn = psn_rot.next()
            P.mm([((pc[:, :], ck[:, tt, :], rhsC[:, tt, :]), dict(start=(tt == 0), stop=(tt == ntt - 1))) for tt in range(ntt)],
                 reads=[b_ck, b_rhs], writes=[b_pc])
            P.mm([((psn[:, :], sk[:, tt, :], rhsS[:, tt, :]), dict(start=(tt == 0), stop=(tt == ntt - 1))) for tt in range(ntt)],
                 reads=[b_ck, b_rhs], writes=[b_psn])
            P.op("act", "copy", ec, pc[:, :], reads=[b_pc], writes=[b_ec])
            P.op("act", "copy", es, psn[:, :], reads=[b_psn], writes=[b_es])
            Uc, Hre, Us, Him = ec[:, 0:256], ec[:, 256:512], es[:, 0:256], es[:, 256:512]
            P.op("dve", "tensor_tensor", t1, Hre, Uc, ALU.mult, reads=[b_ec], writes=[b_t1])
            P.op("dve", "tensor_tensor", t2, Him, Us, ALU.mult, reads=[b_es], writes=[b_t2])
            P.op("dve", "tensor_tensor", Akt[:, kt, :], t1, t2, ALU.add, reads=[b_t1, b_t2], writes=[b_AB])
            P.op("dve", "tensor_tensor", t1, Hre, Us, ALU.mult, reads=[b_ec, b_es], writes=[b_t1])
            P.op("dve", "tensor_tensor", t2, Him, Uc, ALU.mult, reads=[b_es, b_ec], writes=[b_t2])
            P.op("dve", "tensor_tensor", Bkt[:, kt, :], t1, t2, ALU.subtract, reads=[b_t1, b_t2], writes=[b_AB])
        P.pop()
        P.pop()
        P.push()
        TQ = 256
        ct_rot = Rot([((P.alloc([128, ntt, TQ], BF16), P.alloc([128, ntt, TQ], BF16)), Buf()) for _ in range(2)])
        rw_rot = Rot([(P.alloc([128, TQ + 2], F32), Buf()) for _ in range(2)])
        for (rw_, b_rw_) in rw_rot.items:
            P.op("pool", "memset", rw_, 0.0, writes=[b_rw_])
        x2c = P.alloc([128, TQ], F32)
        b_x2c = Buf()
        ta = P.alloc([128, TQ], F32)
        b_ta = Buf()
        yo_rot = Rot([(P.alloc([128, TQ], BF16), Buf()) for _ in range(2)])
        pd_rot = Rot([(banks[i], bb[i]) for i in (5, 6, 7)])
        for tq in range(L // TQ):
            (ct, stt), b_ct = ct_rot.next()
            P.dma("sp", ct, tabs["I"][0, tq], writes=[b_ct])
            P.dma("sp", stt, tabs["I"][1, tq], writes=[b_ct])
            t0 = tq * TQ
            for j in range(2):
                pd, b_pd = pd_rot.next()
                calls = []
                for kt in range(ntt):
                    calls.append(((pd[:, :TQ], Akt[:, kt, j * 128:(j + 1) * 128], ct[:, kt, :]), dict(start=(kt == 0), stop=False)))
                    calls.append(((pd[:, :TQ], Bkt[:, kt, j * 128:(j + 1) * 128], stt[:, kt, :]), dict(start=False, stop=(kt == ntt - 1))))
                P.mm(calls, reads=[b_AB, b_ct], writes=[b_pd])
                rw_, b_rw_ = rw_rot.next()
                lo = max(t0 - 1, 0)
                hi = min(t0 + TQ + 1, L)
                if lo == t0 or hi == t0 + TQ:
                    P.op("pool", "memset", rw_, 0.0, writes=[b_rw_])
                P.dma("sp", rw_[:, lo - (t0 - 1):hi - (t0 - 1)], self.pT[(2 + j) * 128:(3 + j) * 128, col0 + lo:col0 + hi], writes=[b_rw_])
                j6 = 2 + j
                P.op("dve", "tensor_scalar", x2c, rw_[:, 1:TQ + 1], cw[:, j6, 1:2], cw[:, j6, 3:4], ALU.mult, ALU.add, reads=[b_rw_, bc], writes=[b_x2c])
                P.op("dve", "scalar_tensor_tensor", x2c, rw_[:, 0:TQ], cw[:, j6, 0:1], x2c, ALU.mult, ALU.add, reads=[b_rw_, b_x2c, bc], writes=[b_x2c])
                P.op("dve", "scalar_tensor_tensor", x2c, rw_[:, 2:TQ + 2], cw[:, j6, 2:3], x2c, ALU.mult, ALU.add, reads=[b_rw_, b_x2c, bc], writes=[b_x2c])
                P.op("dve", "tensor_single_scalar", ta, u[:, j, t0:t0 + TQ], hbias[:, j:j + 1], ALU.mult, reads=[b_u, bc], writes=[b_ta])
                P.op("dve", "scalar_tensor_tensor", ta, pd[:, :TQ], 1.0 / L, ta, ALU.mult, ALU.add, reads=[b_pd, b_ta], writes=[b_ta])
                yo, b_yo = yo_rot.next()
                P.op("dve", "tensor_tensor", yo, ta, x2c, ALU.mult, reads=[b_ta, b_x2c], writes=[b_yo])
                P.dma("pool", self.ymixT[j * 128:(j + 1) * 128, col0 + t0:col0 + t0 + TQ], yo, reads=[b_yo])
        P.pop()
        P.pop("phase_hyena")

    RW0 = HY_COLS + GQA_COLS + MLA_COLS
    NCHUNK = T // 64

    def phase_rwkv_a(self, l):
        P = self.P
        bc = self.b_const
        banks, bb = P.banks, P.bank_bufs
        TB = 128
        F32R = mybir.dt.float32r

        def R(ap):
            return ap
        CDT = BF16 if RW_BF16 else F32
        DDT = BF16 if RW_DBL_BF16 else F32
        P.push()
        mu = P.alloc([128, 9], F32)
        omm = P.alloc([128, 9], F32)
        hmu = P.alloc([128, 9], F32)
        w0a0 = P.alloc([128, 2, 2, 2], F32)
        vecs = P.alloc([128, 2, 3], F32)
        omka = P.alloc([128, 2], F32)
        w2s = P.alloc([128, 256], F32)
        a2s = P.alloc([128, 256], F32)
        g2 = P.alloc([128, 256], F32)
        masks = P.alloc([128, 2, 2, 128], F32)
        ident = P.alloc([128, 128], F32)
        bd64 = P.alloc([128, 128], F32)
        bdo2 = P.alloc([128, 2], F32)
        ident_bf = P.alloc([128, 128], CDT)
        P.dma("pool", ident_bf, self.c_ident.ap(), writes=[bc])
        P.dma("sp", mu, self.rw_muT[l], writes=[bc])
        P.dma("sp", w0a0, self.rw_w0a0[l], writes=[bc])
        P.dma("sp", vecs, self.rw_vecs[l], writes=[bc])
        P.dma("sp", w2s, self.rw_w2[l].rearrange("d k c -> (d k) c"), writes=[bc])
        P.dma("sp", a2s, self.rw_a2[l].rearrange("d k c -> (d k) c"), writes=[bc])
        P.dma("sp", g2, self.rw_g2[l], writes=[bc])
        P.dma("sp", masks, self.rw_masks.ap(), writes=[bc])
        P.dma("sp", ident, self.c_ident.ap(), writes=[bc])
        P.dma("sp", bd64, self.c_bd64.ap(), writes=[bc])
        P.dma("sp", bdo2, self.c_bdo2.ap(), writes=[bc])
        P.op("dve", "tensor_scalar", omm, mu, -1.0, 1.0, ALU.mult, ALU.add, reads=[bc], writes=[bc])
        P.op("dve", "tensor_single_scalar", hmu, mu, 0.5, ALU.mult, reads=[bc], writes=[bc])
        P.op("dve", "tensor_scalar", omka, vecs[:, :, 1], -1.0, 1.0, ALU.mult, ALU.add, reads=[bc], writes=[bc])
        NBUF = 2
        blk_bufs = []
        for _ in range(NBUF):
            o = dict(raw=P.alloc([128, 9, TB + 2], F32), b_raw=Buf(), tsum=P.alloc([128, 9, TB], F32), b_tsum=Buf(),
                     sh=P.alloc([128, 9, TB], F32), b_sh=Buf(), kk=P.alloc([128, 2, TB], F32), nkk=P.alloc([128, 2, TB], F32), b_kk=Buf(),
                     act_t=P.alloc([128, 2, TB], F32), b_act=Buf(), vT=P.alloc([128, 2, 128], F32), b_vT=Buf(),
                     Vbd=P.alloc([128, 2, 2, 128], CDT), b_Vbd=Buf(), kd=P.alloc([128, 2, 2, TB], F32), b_kd=Buf(),
                     gt=P.alloc([128, 256], F32), b_gt=Buf(), bon=P.alloc([128, 4], F32), b_bon=Buf())
            P.op("pool", "memset", o["Vbd"], 0.0, writes=[o["b_Vbd"]])
            P.op("pool", "memset", o["raw"], 0.0, writes=[o["b_raw"]])
            blk_bufs.append(o)
        tmpA = Rot([(P.alloc([128, TB], F32), Buf()) for _ in range(8)])
        ctxs = {}
        for d in range(2):
            for pr in range(2):
                c = dict(lw=P.alloc([128, TB], F32), b_lw=Buf(), lwT=P.alloc([128, 128], F32), b_lwT=Buf(),
                         GE=P.alloc([128, 3, TB], F32), b_GE=Buf(), aa=P.alloc([128, TB], F32), b_aa=Buf(),
                         t1=P.alloc([128, TB], F32), b_t1=Buf(), be=P.alloc([128, TB], F32), b_be=Buf(),
                         QQ=P.alloc([128, 2, 256], CDT), b1=P.alloc([128, 2, 128], CDT), b2=P.alloc([128, 2, 128], CDT), bset=Buf(),
                         ATm=P.alloc([128, 2, 256], CDT), b_ATm=Buf(), BTm=P.alloc([128, 2, 256], CDT), b_BTm=Buf(),
                         Mt=[P.alloc([128, 2, 128], DDT) for _ in range(2)], b_Mt=[Buf(), Buf()],
                         Nt=[P.alloc([128, 2, 128], DDT) for _ in range(2)], b_Nt=[Buf(), Buf()],
                         Wt=[P.alloc([128, 2, 128], DDT) for _ in range(2)], b_Wt=[Buf(), Buf()],
                         M0b=P.alloc([128, 2, 128], DDT), b_M0b=Buf(), TTf=P.alloc([128, 2, 128], CDT), b_TTf=Buf(),
                         XQ=P.alloc([128, 2, 256], CDT), b_XQ=Buf(), TBVQ=P.alloc([128, 2, 256], CDT), b_TBVQ=Buf(),
                         b1tm=P.alloc([128, 2, 128], CDT), b_b1tm=Buf(), b2tm=P.alloc([128, 2, 128], CDT), b_b2tm=Buf(),
                         OUT=P.alloc([128, 2, 520], F32), b_OUT=Buf())
                for tl in (c["QQ"], c["b1"], c["b2"]):
                    P.op("pool", "memset", tl, 0.0, writes=[c["bset"]])
                ctxs[(d, pr)] = c
        ps_rot = Rot([(banks[i], bb[i]) for i in range(8)])
        R0 = self.RW0
        nblk = T // TB

        def ps():
            return ps_rot.next()

        def gen_dpr(d, pr, B, blk):
            c = ctxs[(d, pr)]
            kb = d * 64
            sh, b_sh, kk, nkk, b_kk, act_t, b_act = B["sh"], B["b_sh"], B["kk"], B["nkk"], B["b_kk"], B["act_t"], B["b_act"]
            Vbd, b_Vbd, kd, b_kd = B["Vbd"], B["b_Vbd"], B["kd"], B["b_kd"]
            lw, b_lw, lwT, b_lwT, GE, b_GE, aa, b_aa = c["lw"], c["b_lw"], c["lwT"], c["b_lwT"], c["GE"], c["b_GE"], c["aa"], c["b_aa"]
            QQ, b1, b2, bset = c["QQ"], c["b1"], c["b2"], c["bset"]
            ATm, b_ATm, BTm, b_BTm = c["ATm"], c["b_ATm"], c["BTm"], c["b_BTm"]
            Mt, b_Mt, Nt, b_Nt, Wt, b_Wt = c["Mt"], c["b_Mt"], c["Nt"], c["b_Nt"], c["Wt"], c["b_Wt"]
            XQ, b_XQ, TBVQ, b_TBVQ = c["XQ"], c["b_XQ"], c["TBVQ"], c["b_TBVQ"]
            b1tm, b_b1tm, b2tm, b_b2tm, OUT, b_OUT = c["b1tm"], c["b_b1tm"], c["b2tm"], c["b_b2tm"], c["OUT"], c["b_OUT"]
            pz, b_pz = ps()
            P.mm([((pz[:, 0:TB], w2s[kb:kb + 64, pr * 128:(pr + 1) * 128], act_t[kb:kb + 64, 0, :]), dict(start=True, stop=True))],
                 reads=[b_act, bc], writes=[b_pz])
            P.op("act", "activation", lw, pz[:, 0:TB], AF.Sigmoid, bias=w0a0[:, 0, d, pr:pr + 1], scale=1.0, reads=[b_pz, bc], writes=[b_lw])
            pz, b_pz = ps()
            P.mm([((pz[:, 0:TB], a2s[kb:kb + 64, pr * 128:(pr + 1) * 128], sh[kb:kb + 64, 7, :]), dict(start=True, stop=True))],
                 reads=[b_sh, bc], writes=[b_pz])
            P.op("act", "activation", aa, pz[:, 0:TB], AF.Sigmoid, bias=w0a0[:, 1, d, pr:pr + 1], scale=1.0, reads=[b_pz, bc], writes=[b_aa])
            yield
            pz, b_pz = ps()
            P.mm([((pz[:, 0:128], lw, ident), dict(start=True, stop=True))], reads=[b_lw, bc], writes=[b_pz])
            P.op("act", "activation", lwT, pz[:, 0:128], AF.Copy, scale=-math.exp(-0.5), reads=[b_pz], writes=[b_lwT])
            P.op("dve", "tensor_scalar", c["t1"], aa, vecs[:, pr, 1:2], omka[:, pr:pr + 1], ALU.mult, ALU.add, reads=[b_aa, bc], writes=[c["b_t1"]])
            P.op("dve", "tensor_tensor", kd[:, d, pr, :], sh[:, 2 + pr, :], c["t1"], ALU.mult, reads=[b_sh, c["b_t1"]], writes=[b_kd])
            P.op("dve", "tensor_tensor", c["be"], kk[:, pr, :], aa, ALU.mult, reads=[b_kk, b_aa], writes=[c["b_be"]])
            yield
            pz, b_pz = ps()
            P.mm([((pz[:, 0:128], lwT, masks[:, d, 1, :]), dict(start=True, stop=True))], reads=[b_lwT, bc], writes=[b_pz])
            P.mm([((pz[:, 128:256], lwT, masks[:, d, 0, :]), dict(start=True, stop=True))], reads=[b_lwT, bc], writes=[b_pz])
            P.op("act", "activation", GE[:, 0:2, :], pz[:, 0:256].rearrange("p (a b) -> p a b", a=2), AF.Exp, reads=[b_pz], writes=[b_GE])
            P.op("act", "activation", GE[:, 2, :], pz[:, 0:128], AF.Exp, scale=-1.0, reads=[b_pz], writes=[b_GE])
            yield
            for hh in range(2):
                sl = slice(hh * 64, (hh + 1) * 64)

                def v3(ap):
                    return ap[sl, :].rearrange("p (c t) -> p c t", c=2)
                eng = "dve"
                P.op(eng, "tensor_tensor", QQ[sl, :, hh * 64:(hh + 1) * 64], v3(nkk[:, pr, :]), v3(GE[:, 1, :]), ALU.mult,
                     reads=[b_kk, b_GE], writes=[bset])
                P.op(eng, "tensor_tensor", QQ[sl, :, 128 + hh * 64:128 + (hh + 1) * 64], v3(sh[:, pr, :]), v3(GE[:, 0, :]), ALU.mult,
                     reads=[b_sh, b_GE], writes=[bset])
                P.op(eng, "tensor_tensor", b1[sl, :, hh * 64:(hh + 1) * 64], v3(c["be"]), v3(GE[:, 2, :]), ALU.mult, reads=[c["b_be"], b_GE], writes=[bset])
                P.op(eng, "tensor_tensor", b2[sl, :, hh * 64:(hh + 1) * 64], v3(kd[:, d, pr, :]), v3(GE[:, 2, :]), ALU.mult,
                     reads=[b_kd, b_GE], writes=[bset])
            yield
            for c2 in range(2):
                pz, b_pz = ps()
                P.mm([((pz[:, 0:256], R(b1[:, c2, :]), R(QQ[:, c2, :])), dict(start=True, stop=True))], reads=[bset], writes=[b_pz])
                P.op("dve", "tensor_tensor", ATm[:, c2, :].rearrange("p (a b) -> p a b", a=2), pz[:, 0:256].rearrange("p (a b) -> p a b", a=2),
                     masks[:, d, :, :], ALU.mult, reads=[b_pz, bc], writes=[b_ATm])
                pz, b_pz = ps()
                P.mm([((pz[:, 0:256], R(b2[:, c2, :]), R(QQ[:, c2, :])), dict(start=True, stop=True))], reads=[bset], writes=[b_pz])
                P.op("dve", "tensor_tensor", BTm[:, c2, :].rearrange("p (a b) -> p a b", a=2), pz[:, 0:256].rearrange("p (a b) -> p a b", a=2),
                     masks[:, d, :, :], ALU.mult, reads=[b_pz, bc], writes=[b_BTm])
            pz, b_pz = ps()
            for c2 in range(2):
                P.mm([((pz[:, c2 * 128:(c2 + 1) * 128], R(QQ[:, c2, 0:128]), R(b1[:, c2, :])), dict(start=True, stop=True))], reads=[bset], writes=[b_pz])
            for c2 in range(2):
                P.op("dve", "tensor_tensor", Nt[0][:, c2, :], pz[:, c2 * 128:(c2 + 1) * 128], masks[:, 1 - d, 0, :], ALU.mult,
                     reads=[b_pz, bc], writes=[b_Nt[0]])
            for (src_fn, dst, b_dst) in ((lambda c2: QQ[:, c2, 0:128], None, None), (lambda c2: b1[:, c2, :], b1tm, b_b1tm), (lambda c2: b2[:, c2, :], b2tm, b_b2tm)):
                pz, b_pz = ps()
                for c2 in range(2):
                    P.mm([((pz[:, c2 * 128:(c2 + 1) * 128], src_fn(c2), ident_bf), dict(start=True, stop=True))], reads=[bset, bc], writes=[b_pz])
                if dst is None:
                    P.op("act", "copy", XQ[:, :, 128:256], pz[:, 0:256].rearrange("p (a b) -> p a b", a=2), reads=[b_pz], writes=[b_XQ])
                else:
                    P.op("act", "copy", dst, pz[:, 0:256].rearrange("p (a b) -> p a b", a=2), reads=[b_pz], writes=[b_dst])
            yield
            for c2 in range(2):
                P.op("dve", "tensor_tensor", Wt[1][:, c2, :], ATm[:, c2, 0:128], ident, ALU.add, reads=[b_ATm, bc], writes=[b_Wt[1]])
            P.op("pool", "tensor_copy", c["M0b"], ATm[:, :, 0:128], reads=[b_ATm], writes=[c["b_M0b"]])
            pz, b_pz = ps()
            for c2 in range(2):
                P.mm([((pz[:, c2 * 128:(c2 + 1) * 128], R(BTm[:, c2, 0:128]), R(Vbd[:, pr, c2, :])), dict(start=True, stop=True))],
                     reads=[b_BTm, b_Vbd], writes=[b_pz])
            P.op("act", "copy", XQ[:, :, 0:128], pz[:, 0:256].rearrange("p (a b) -> p a b", a=2), reads=[b_pz], writes=[b_XQ])

            def Mj(j, c2):
                return (c["M0b"][:, c2, :], c["b_M0b"]) if j == 0 else (Mt[j % 2][:, c2, :], b_Mt[j % 2])
            for jj in range(6):
                if jj >= 1:
                    pz, b_pz = ps()
                    for c2 in range(2):
                        P.mm([((pz[:, c2 * 128:(c2 + 1) * 128], R(Nt[jj % 2][:, c2, :]), R(Wt[jj % 2][:, c2, :])), dict(start=True, stop=True))],
                             reads=[b_Nt[jj % 2], b_Wt[jj % 2]], writes=[b_pz])
                    if jj == 5:
                        P.op("dve", "tensor_tensor", c["TTf"], pz[:, 0:256].rearrange("p (a b) -> p a b", a=2), Wt[jj % 2], ALU.add,
                             reads=[b_pz, b_Wt[jj % 2]], writes=[c["b_TTf"]])
                    else:
                        P.op("dve", "tensor_tensor", Wt[(jj + 1) % 2], pz[:, 0:256].rearrange("p (a b) -> p a b", a=2), Wt[jj % 2], ALU.add,
                             reads=[b_pz, b_Wt[jj % 2]], writes=[b_Wt[(jj + 1) % 2]])
                if jj < 4:
                    pz, b_pz = ps()
                    for c2 in range(2):
                        m_, b_m_ = Mj(jj, c2)
                        P.mm([((pz[:, c2 * 128:(c2 + 1) * 128], R(Nt[jj % 2][:, c2, :]), R(m_)), dict(start=True, stop=True))],
                             reads=[b_Nt[jj % 2], b_m_], writes=[b_pz])
                    P.op("act", "copy", Mt[(jj + 1) % 2], pz[:, 0:256].rearrange("p (a b) -> p a b", a=2), reads=[b_pz], writes=[b_Mt[(jj + 1) % 2]])
                if jj < 5:
                    pz, b_pz = ps()
                    for c2 in range(2):
                        m_, b_m_ = Mj(jj, c2)
                        P.mm([((pz[:, c2 * 128:(c2 + 1) * 128], R(m_), R(Nt[jj % 2][:, c2, :])), dict(start=True, stop=True))],
                             reads=[b_Nt[jj % 2], b_m_], writes=[b_pz])
                    P.op("act", "copy", Nt[(jj + 1) % 2], pz[:, 0:256].rearrange("p (a b) -> p a b", a=2), reads=[b_pz], writes=[b_Nt[(jj + 1) % 2]])
                yield
            TT, b_TT = c["TTf"], c["b_TTf"]
            for c2 in range(2):
                pz, b_pz = ps()
                P.mm([((pz[:, 0:256], R(TT[:, c2, :]), R(XQ[:, c2, :])), dict(start=True, stop=True))], reads=[b_TT, b_XQ], writes=[b_pz])
                P.op("act" if c2 == 0 else "dve", "copy" if c2 == 0 else "tensor_copy", TBVQ[:, c2, :], pz[:, 0:256], reads=[b_pz], writes=[b_TBVQ])
            yield
            pzs = [ps() for _ in range(4)]
            for c2 in range(2):
                TBV, TQT = R(TBVQ[:, c2, 0:128]), R(TBVQ[:, c2, 128:256])
                ApT, BpT = R(ATm[:, c2, 128:256]), R(BTm[:, c2, 128:256])
                V_ = R(Vbd[:, pr, c2, :])
                cs = slice(c2 * 128, (c2 + 1) * 128)
                P.mm([((pzs[0][0][:, cs], TQT, ApT), dict(start=True, stop=True))], reads=[b_TBVQ, b_ATm], writes=[pzs[0][1]])
                P.mm([((pzs[1][0][:, cs], ApT, TBV), dict(start=True, stop=False)), ((pzs[1][0][:, cs], BpT, V_), dict(start=False, stop=True))],
                     reads=[b_ATm, b_BTm, b_TBVQ, b_Vbd], writes=[pzs[1][1]])
                P.mm([((pzs[2][0][:, cs], TQT, R(b1tm[:, c2, :])), dict(start=True, stop=True))], reads=[b_TBVQ, b_b1tm], writes=[pzs[2][1]])
                P.mm([((pzs[3][0][:, cs], R(b1tm[:, c2, :]), TBV), dict(start=True, stop=False)),
                      ((pzs[3][0][:, cs], R(b2tm[:, c2, :]), V_), dict(start=False, stop=True))],
                     reads=[b_b1tm, b_b2tm, b_TBVQ, b_Vbd], writes=[pzs[3][1]])
            v2 = lambda ap: ap.rearrange("p (a b) -> p a b", a=2)
            P.op("dve", "tensor_tensor", OUT[:, :, 0:128], v2(pzs[0][0][:, 0:256]), QQ[:, :, 128:256], ALU.add, reads=[pzs[0][1], bset], writes=[b_OUT])
            P.op("act", "copy", OUT[:, :, 128:256], v2(pzs[1][0][:, 0:256]), reads=[pzs[1][1]], writes=[b_OUT])
            for c2 in range(2):
                cs = slice(c2 * 128, (c2 + 1) * 128)
                P.op("dve", "tensor_tensor", OUT[:, c2, 256:384], pzs[2][0][:, cs], ident, ALU.add, reads=[pzs[2][1], bc], writes=[b_OUT])
                gi = c2 * 64 + (63 if d == 0 else 0)
                gcol = GE[:, 0, gi:gi + 1]
                P.op("act", "activation", OUT[:, c2, 384:512], pzs[3][0][:, cs], AF.Copy, scale=gcol, reads=[pzs[3][1], b_GE], writes=[b_OUT])
                P.op("dve", "tensor_copy", OUT[:, c2, 512:513], gcol, reads=[b_GE], writes=[b_OUT])
            P.dma(STQ_RWA, self.rwA[d, blk * 2:blk * 2 + 2, pr].rearrange("c p w -> p c w"), OUT, reads=[b_OUT])
            yield

        nblk = int(os.environ.get("RWA_NBLK", nblk))
        def prep_blk(blk):
            B = blk_bufs[blk % NBUF]
            raw, b_raw, tsum, b_tsum, sh, b_sh = B["raw"], B["b_raw"], B["tsum"], B["b_tsum"], B["sh"], B["b_sh"]
            kk, nkk, b_kk, act_t, b_act, vT, b_vT, Vbd, b_Vbd = B["kk"], B["nkk"], B["b_kk"], B["act_t"], B["b_act"], B["vT"], B["b_vT"], B["Vbd"], B["b_Vbd"]
            t0 = blk * TB
            seg0, seg1 = (0, CTX) if t0 < CTX else (CTX, T)
            lo, hi = max(t0 - 1, seg0), min(t0 + TB + 1, seg1)
            if lo == t0 or hi == t0 + TB:
                P.op("pool", "memset", raw, 0.0, writes=[b_raw])
            P.dma("sp", raw[:, :, lo - (t0 - 1):hi - (t0 - 1)],
                  self.pT[R0:R0 + 1152, lo:hi].rearrange("(c p) t -> p c t", p=128), writes=[b_raw])
            yield
            P.op("dve", "tensor_tensor", tsum, raw[:, :, 0:TB], raw[:, :, 2:TB + 2], ALU.add, reads=[b_raw], writes=[b_tsum])
            for c in range(9):
                P.op("dve", "tensor_single_scalar", sh[:, c, :], raw[:, c, 1:TB + 1], omm[:, c:c + 1], ALU.mult, reads=[b_raw, bc], writes=[b_sh])
            yield
            for c in range(9):
                P.op("dve", "scalar_tensor_tensor", sh[:, c, :], tsum[:, c, :], hmu[:, c:c + 1], sh[:, c, :], ALU.mult, ALU.add,
                     reads=[b_tsum, b_sh, bc], writes=[b_sh])
            yield
            kqs = []
            for pr in range(2):
                kq, b_kq = tmpA.next()
                sq, b_sq = tmpA.next()
                kqs.append((kq, b_kq, sq, b_sq))
                P.op("dve", "tensor_single_scalar", kq, sh[:, 2 + pr, :], vecs[:, pr, 0:1], ALU.mult, reads=[b_sh, bc], writes=[b_kq])
            P.op("act", "activation", act_t[:, 0, :], sh[:, 6, :], AF.Tanh, reads=[b_sh], writes=[b_act])
            P.op("act", "activation", act_t[:, 1, :], sh[:, 8, :], AF.Sigmoid, reads=[b_sh], writes=[b_act])
            yield
            pzv, b_pzv = ps()
            for pr in range(2):
                P.mm([((pzv[:, pr * 128:(pr + 1) * 128], sh[:, 4 + pr, :], ident), dict(start=True, stop=True))], reads=[b_sh, bc], writes=[b_pzv])
            P.op("act", "copy", vT, pzv[:, 0:256].rearrange("p (a b) -> p a b", a=2), reads=[b_pzv], writes=[b_vT])
            for pr in range(2):
                kq, b_kq, sq, b_sq = kqs[pr]
                P.op("act", "activation", sq, kq, AF.Square, reads=[b_kq], writes=[b_sq])
            yield
            pzg, b_pzg = ps()
            P.mm([((pzg[:, 0:256], act_t[:, 1, :], g2), dict(start=True, stop=True))], reads=[b_act, bc], writes=[b_pzg])
            P.op("act", "copy", B["gt"], pzg[:, 0:256], reads=[b_pzg], writes=[B["b_gt"]])
            for pr in range(2):
                kq, b_kq, sq, b_sq = kqs[pr]
                pz, b_pz = ps()
                P.mm([((pz[:, 0:TB], bd64, sq), dict(start=True, stop=True))], reads=[b_sq, bc], writes=[b_pz])
                P.op("act", "activation", sq, pz[:, 0:TB], AF.Ln, bias=1e-24, scale=1.0, reads=[b_pz], writes=[b_sq])
            P.dma(STQ_RWA, self.rw_gtm[t0:t0 + TB, :], B["gt"], reads=[B["b_gt"]])
            P.dma(STQ_RWA, self.rw_vtm[t0:t0 + TB, :], vT.rearrange("p a b -> p (a b)"), reads=[b_vT])
            yield
            for pr in range(2):
                kq, b_kq, sq, b_sq = kqs[pr]
                P.op("act", "activation", sq, sq, AF.Exp, scale=-0.5, reads=[b_sq], writes=[b_sq])
            for c2 in range(2):
                for hh in range(2):
                    P.op("dve", "tensor_copy", Vbd[hh * 64:(hh + 1) * 64, :, c2, hh * 64:(hh + 1) * 64],
                         vT[c2 * 64:(c2 + 1) * 64, :, hh * 64:(hh + 1) * 64], reads=[b_vT], writes=[b_Vbd])
            yield
            for pr in range(2):
                kq, b_kq, sq, b_sq = kqs[pr]
                P.op("dve", "tensor_tensor", kk[:, pr, :], kq, sq, ALU.mult, reads=[b_kq, b_sq], writes=[b_kk])
                P.op("dve", "scalar_tensor_tensor", nkk[:, pr, :], kq, -1.0, sq, ALU.mult, ALU.mult, reads=[b_kq, b_sq, b_kk], writes=[b_kk])
            yield

        def chains_blk(blk):
            B = blk_bufs[blk % NBUF]
            sh, b_sh = B["sh"], B["b_sh"]
            t0 = blk * TB
            gens = [gen_dpr(d, pr, B, blk) for d in range(2) for pr in range(2)]
            nxt = prep_blk(blk + 1) if blk + 1 < nblk else None
            while gens:
                for g_ in list(gens):
                    try:
                        next(g_)
                    except StopIteration:
                        gens.remove(g_)
                if nxt is not None:
                    try:
                        next(nxt)
                    except StopIteration:
                        nxt = None
            if nxt is not None:
                for _ in nxt:
                    pass
            pz, b_pz = ps()
            for pr in range(2):
                ks, b_ks = tmpA.next()
                P.op("dve", "tensor_tensor", ks, B["kd"][:, 0, pr, :], B["kd"][:, 1, pr, :], ALU.add, reads=[B["b_kd"]], writes=[b_ks])
                P.op("dve", "scalar_tensor_tensor", ks, sh[:, pr, :], vecs[:, pr, 2:3], ks, ALU.mult, ALU.mult, reads=[b_sh, b_ks, bc], writes=[b_ks])
                P.mm([((pz[:, 2 * pr:2 * pr + 2], ks, bdo2), dict(start=True, stop=True))], reads=[b_ks, bc], writes=[b_pz])
            P.op("dve", "tensor_copy", B["bon"], pz[:, 0:4], reads=[b_pz], writes=[B["b_bon"]])
            P.dma(STQ_RWA, self.rw_bon[t0:t0 + TB, :], B["bon"], reads=[B["b_bon"]])

        for _ in prep_blk(0):
            pass
        for blk in range(nblk):
            chains_blk(blk)
        P.pop("phase_rwkv_a")

    def rwkv_bc_gen(self, l, ctx_out, bank_ids):
        P = self.P
        banks, bb = P.banks, P.bank_bufs
        OUTW = 520
        NB = 8
        in_rot = Rot([(P.alloc([128, OUTW], F32), Buf()) for _ in range(NB)])
        st = {}
        for d in range(2):
            for pr in range(2):
                tl = [(P.alloc([128, 128], F32), Buf()) for _ in range(2)]
                P.op("pool", "memset", tl[0][0], 0.0, writes=[tl[0][1]])
                st[(d, pr)] = [tl, 0]
        yo_rot = Rot([(P.alloc([128, 128], F32), Buf()) for _ in range(4)])
        ps_rot = Rot([(banks[i], bb[i]) for i in bank_ids])
        order = {0: list(range(self.NCHUNK)), 1: [3, 2, 1, 0] + list(range(self.NCHUNK - 1, 3, -1))}
        its = [(s_, d, pr, order[d][s_]) for s_ in range(self.NCHUNK) for d in range(2) for pr in range(2)]
        LA = NB - 2
        loaded = []

        def load(i):
            (_, d, pr, j) = its[i]
            it, b_it = in_rot.next()
            P.dma("sp", it, self.rwA[d, j, pr], writes=[b_it])
            loaded.append((it, b_it))
        for i in range(min(LA, len(its))):
            load(i)
        for i, (s_, d, pr, j) in enumerate(its):
            if i + LA < len(its):
                load(i + LA)
            it, b_it = loaded[i]
            tl, cur = st[(d, pr)]
            Pc, b_Pc = tl[cur]
            Pn, b_Pn = tl[1 - cur]
            pz, b_pz = ps_rot.next()
            P.mm([((pz[:, 0:128], it[:, 0:128], Pc), dict(start=True, stop=True))], reads=[b_it, b_Pc], writes=[b_pz])
            yo, b_yo = yo_rot.next()
            P.op("dve", "tensor_tensor", yo, pz[:, 0:128], it[:, 128:256], ALU.add, reads=[b_pz, b_it], writes=[b_yo])
            for hh in range(2):
                hcol = (pr * 2 + hh) * 64
                P.dma("pool", self.rw_y[d, j * 64:(j + 1) * 64, hcol:hcol + 64], yo[hh * 64:(hh + 1) * 64, hh * 64:(hh + 1) * 64], reads=[b_yo])
            pz, b_pz = ps_rot.next()
            P.mm([((pz[:, 0:128], it[:, 256:384], Pc), dict(start=True, stop=True))], reads=[b_it, b_Pc], writes=[b_pz])
            P.op("dve", "scalar_tensor_tensor", Pn, pz[:, 0:128], it[:, 512:513], it[:, 384:512], ALU.mult, ALU.add,
                 reads=[b_pz, b_it], writes=[b_Pn])
            st[(d, pr)][1] = 1 - cur
            if i % 4 == 3:
                yield
        P.barrier()
        bc = self.b_const
        lnw = P.alloc([128, 2, 256], F32)
        ident = P.alloc([128, 128], F32)
        P.dma("sp", lnw, self.rw_ln[l:l + 1].rearrange("o a c -> o (a c)").partition_broadcast(128), writes=[bc])
        P.dma("sp", ident, self.c_ident.ap(), writes=[bc])
        TB = 128
        rot = lambda shape, n=2, dt=F32: Rot([(P.alloc(shape, dt), Buf()) for _ in range(n)])
        yf_r, yb_r, v_r, g_r, bo_r = rot([128, 256], 3), rot([128, 256], 3), rot([128, 256], 3), rot([128, 256], 3), rot([128, 4], 3)
        y_r, sq_r, st_r, o_r = rot([128, 256]), rot([128, 256]), rot([128, 4, 4]), rot([128, 256], 2, BF16)
        ps_rot = Rot([(banks[i], bb[i]) for i in bank_ids])
        blk0 = 0 if ctx_out else CTX // TB
        ldd = {}

        def loadc(blk):
            t0 = blk * TB
            yf, b_yf = yf_r.next(); yb, b_yb = yb_r.next(); vv, b_vv = v_r.next(); gg, b_gg = g_r.next(); bo, b_bo = bo_r.next()
            P.dma("sp", yf, self.rw_y[0, t0:t0 + TB, :], writes=[b_yf])
            P.dma("sp", yb, self.rw_y[1, t0:t0 + TB, :], writes=[b_yb])
            P.dma("sp", vv, self.rw_vtm[t0:t0 + TB, :], writes=[b_vv])
            P.dma("sp", gg, self.rw_gtm[t0:t0 + TB, :], writes=[b_gg])
            P.dma("sp", bo, self.rw_bon[t0:t0 + TB, :], writes=[b_bo])
            ldd[blk] = (yf, b_yf, yb, b_yb, vv, b_vv, gg, b_gg, bo, b_bo)
        loadc(blk0)
        for blk in range(blk0, T // TB):
            t0 = blk * TB
            if blk + 1 < T // TB:
                loadc(blk + 1)
            yf, b_yf, yb, b_yb, vv, b_vv, gg, b_gg, bo, b_bo = ldd.pop(blk)
            y, b_y = y_r.next(); sq, b_sq = sq_r.next(); stt, b_stt = st_r.next()
            P.op("dve", "tensor_tensor", y, yf, yb, ALU.add, reads=[b_yf, b_yb], writes=[b_y])
            y3 = y.rearrange("p (h n) -> p h n", h=4)
            P.op("act", "activation", sq, y, AF.Square, reads=[b_y], writes=[b_sq])
            P.op("dve", "tensor_reduce", stt[:, 0, :], y3, AX.X, ALU.add, reads=[b_y], writes=[b_stt])
            P.op("dve", "tensor_reduce", stt[:, 1, :], sq.rearrange("p (h n) -> p h n", h=4), AX.X, ALU.add, reads=[b_sq, b_stt], writes=[b_stt])
            P.op("dve", "tensor_single_scalar", stt[:, 0, :], stt[:, 0, :], 1.0 / 64, ALU.mult, reads=[b_stt], writes=[b_stt])
            P.op("dve", "tensor_tensor", stt[:, 2, :], stt[:, 0, :], stt[:, 0, :], ALU.mult, reads=[b_stt], writes=[b_stt])
            P.op("dve", "scalar_tensor_tensor", stt[:, 1, :], stt[:, 1, :], 1.0 / 64, stt[:, 2, :], ALU.mult, ALU.subtract, reads=[b_stt], writes=[b_stt])
            P.op("act", "activation", stt[:, 3, :], stt[:, 1, :], AF.Ln, bias=64e-5, scale=1.0, reads=[b_stt], writes=[b_stt])
            P.op("act", "activation", stt[:, 3, :], stt[:, 3, :], AF.Exp, scale=-0.5, reads=[b_stt], writes=[b_stt])
            for h in range(4):
                P.op("dve", "tensor_scalar", y3[:, h, :], y3[:, h, :], stt[:, 0, h:h + 1], stt[:, 3, h:h + 1], ALU.subtract, ALU.mult,
                     reads=[b_y, b_stt], writes=[b_y])
            P.op("dve", "tensor_tensor", y, y, lnw[:, 0, :], ALU.mult, reads=[b_y, bc], writes=[b_y])
            P.op("dve", "tensor_tensor", y, y, lnw[:, 1, :], ALU.add, reads=[b_y, bc], writes=[b_y])
            v3 = vv.rearrange("p (h n) -> p h n", h=4)
            for h in range(4):
                P.op("dve", "scalar_tensor_tensor", y3[:, h, :], v3[:, h, :], bo[:, h:h + 1], y3[:, h, :], ALU.mult, ALU.add,
                     reads=[b_vv, b_bo, b_y], writes=[b_y])
            P.op("dve", "tensor_tensor", y, y, gg, ALU.mult, reads=[b_y, b_gg], writes=[b_y])
            o, b_o = o_r.next()
            for pr in range(2):
                pz, b_pz = ps_rot.next()
                P.mm([((pz[:, 0:128], y[:, pr * 128:(pr + 1) * 128], ident), {})], reads=[b_y, bc], writes=[b_pz], meth="transpose")
                P.op("act", "copy", o[:, pr * 128:(pr + 1) * 128], pz[:, 0:128], reads=[b_pz], writes=[b_o])
                P.dma("pool", self.ymixT[768 + pr * 128:768 + (pr + 1) * 128, t0:t0 + TB], o[:, pr * 128:(pr + 1) * 128], reads=[b_o])
            yield

    def phase_rwkv_bc(self, l, ctx_out):
        self.P.push()
        for _ in self.rwkv_bc_gen(l, ctx_out, tuple(range(8))):
            pass
        self.P.pop("phase_rwkv_bc")

    def phase_out(self, l, src, dst, tiles=None):
        P = self.P
        P.push()
        wo = P.alloc([128, KC, D], BF16)
        b_wo = Buf()
        wsrc = self.w_out[l].rearrange("(kc p) n -> p kc n", p=128)
        for kc in range(KC):
            P.dma("pool", wo[:, kc, :], wsrc[:, kc, :], writes=[b_wo])
        xrot = Rot([(P.alloc([128, KC, NT], F32), Buf()) for _ in range(2)])
        yrot = Rot([(P.alloc([128, KC, NT], BF16), Buf()) for _ in range(2)])
        banks, bb = P.banks, P.bank_bufs
        pd_rot = Rot([(banks[i], bb[i]) for i in (1, 2, 3, 4)])
        srcv = src.rearrange("(kc p) t -> p kc t", p=128)
        dstv = dst.rearrange("(kc p) t -> p kc t", p=128)
        ymv = self.ymixT.rearrange("(kc p) t -> p kc t", p=128)
        gate = self.der[l]
        for (t0, n) in (tiles or TILES):
            s = 1 if t0 < CTX else 0
            xt, b_xt = xrot.next()
            yt, b_yt = yrot.next()
            P.dma("sp", xt[:, :, :n], srcv[:, :, t0:t0 + n], writes=[b_xt])
            P.dma("sp", yt[:, :, :n], ymv[:, :, t0:t0 + n], writes=[b_yt])
            for dc in range(KC):
                pd, b_pd = pd_rot.next()
                P.mm([((pd[:, :n], wo[:, kc, dc * 128:(dc + 1) * 128], yt[:, kc, :n]), dict(start=(kc == 0), stop=(kc == KC - 1)))
                      for kc in range(KC)], reads=[b_wo, b_yt], writes=[b_pd])
                P.op("dve", "scalar_tensor_tensor", xt[:, dc, :n], pd[:, :n], gate[:, 5, dc, s:s + 1], xt[:, dc, :n],
                     ALU.mult, ALU.add, reads=[b_pd, b_xt, self.b_der], writes=[b_xt])
            P.dma("pool", dstv[:, :, t0:t0 + n], xt[:, :, :n], reads=[b_xt])
        P.pop("phase_out")


def host_layout(inputs, b):
    m = {}
    x = inputs["x"][b]
    ctx = inputs["ctx"][b]
    m["xT"] = np.ascontiguousarray(np.concatenate([ctx, x], axis=0).T)
    cv = np.stack([inputs["c"][b], inputs["c_ctx"]], axis=-1)
    m["cvec"] = np.ascontiguousarray(cv.reshape(KC, 128, 2).transpose(1, 0, 2))
    return m


def rope_consts():
    tl = np.arange(SEQ)
    row = (tl // 64).astype(np.float64)
    col = (tl % 64).astype(np.float64)

    def table(n_freq, dims, lead):
        inv = 10000.0 ** (-np.arange(n_freq, dtype=np.float64) / n_freq)
        cos = np.ones((lead + dims, T), np.float64)
        sin = np.zeros((lead + dims, T), np.float64)
        for d in range(dims):
            m = d // 2
            ang = row * inv[m] if m < n_freq else col * inv[m - n_freq]
            cos[lead + d, CTX:] = np.cos(ang)
            sin[lead + d, CTX:] = np.sin(ang)
        return cos, sin
    cg, sg = table(16, 64, 0)
    rope_g = np.stack([np.tile(cg, (2, 1)), np.tile(sg, (2, 1))]).astype(np.float32)
    cm, sm = table(8, 32, 64)
    rope_m = np.stack([cm, sm]).astype(np.float32)
    rotm = np.zeros((128, 128), np.float32)
    for m in range(64):
        rotm[2 * m + 1, 2 * m] = -1.0
        rotm[2 * m, 2 * m + 1] = 1.0
    rot96 = np.zeros((96, 96), np.float32)
    for m in range(16):
        j0 = 64 + 2 * m
        rot96[j0 + 1, j0] = -1.0
        rot96[j0, j0 + 1] = 1.0
    bd64 = np.zeros((128, 128), np.float32)
    bd64[:64, :64] = 1.0
    bd64[64:, 64:] = 1.0
    return dict(rope_g=rope_g, rope_m=rope_m, c_rotm=rotm, c_rot96=rot96, c_bd64=bd64)


_HY_CACHE = {}


def hyena_consts():
    if _HY_CACHE:
        return _HY_CACHE
    m = {}
    m["c_ident"] = np.eye(128, dtype=np.float32)
    max_decay = math.log(1e-2) / 0.3
    min_decay = math.log(1e-2) / 1.5
    deltas = np.abs(np.linspace(min_decay, max_decay, HY_CH, dtype=np.float32))
    m["hy_ndelta"] = np.ascontiguousarray(np.broadcast_to(-np.tile(deltas, 2)[None, :], (128, 512))).astype(np.float32)
    for L in (SEQ, CTX):
        ntt = L // 128
        t01 = np.linspace(0.0, 1.0, L, dtype=np.float32)
        bands = 16
        w_ang = (np.float32(2.0 * math.pi) * np.arange(L, dtype=np.float32) / np.float32(L)).astype(np.float32)
        f = np.linspace(1e-4, bands - 1, bands, dtype=np.float32)
        arg = (f[None, :] * w_ang[:, None]).astype(np.float32)
        z = np.concatenate([t01[:, None], np.cos(arg), -np.sin(arg)], axis=-1).astype(np.float32)
        m[f"hy_z{L}"] = np.ascontiguousarray(z.T)
        m[f"hy_t01_{L}"] = np.ascontiguousarray(t01.reshape(ntt, 128).T)
        N = 2 * L
        t = np.arange(L, dtype=np.int64)
        kk = np.arange(L, dtype=np.int64)
        ph = ((2 * kk[None, :] + 1) * t[:, None]) % (2 * N)
        ang = ph.astype(np.float64) * (math.pi / N)
        mats = [np.cos(ang).astype(ml_dtypes.bfloat16), np.sin(ang).astype(ml_dtypes.bfloat16)]
        del ang, ph
        F = np.stack([M.reshape(ntt, 128, ntt, 128).transpose(2, 1, 0, 3) for M in mats])
        m[f"dftF{L}"] = np.ascontiguousarray(F)
        I = np.stack([M.reshape(L // 256, 256, ntt, 128).transpose(0, 3, 2, 1) for M in mats])
        m[f"dftI{L}"] = np.ascontiguousarray(I)
    _HY_CACHE.update(m)
    return _HY_CACHE


def host_shared(inputs):
    m = {}
    m.update(hyena_consts())
    cwv = np.concatenate([inputs["hy_conv_w"], inputs["hy_conv_b"][:, None, :]], axis=1)
    m["hy_cw"] = np.ascontiguousarray(cwv.reshape(DEPTH, 4, 6, 128).transpose(0, 3, 2, 1))
    m["hy_biasT"] = np.ascontiguousarray(inputs["hy_bias"].reshape(DEPTH, 2, 128).transpose(0, 2, 1))
    m["hy_fb"] = np.ascontiguousarray(np.stack([inputs["hy_f_b1"], inputs["hy_f_b2"], inputs["hy_f_b3"], inputs["hy_f_freq"]], axis=-1))
    for nm in ("hy_f_w1", "hy_f_w2", "hy_f_w3", "hy_f_w4", "rw_w2", "rw_a2", "rw_g2"):
        m[nm] = inputs[nm]
    m["rw_muT"] = np.ascontiguousarray(inputs["rw_mu"].reshape(DEPTH, 9, 128).transpose(0, 2, 1))
    wa = np.stack([inputs["rw_w0"], inputs["rw_a0"]], axis=1)
    m["rw_w0a0"] = np.ascontiguousarray(wa.reshape(DEPTH, 2, 2, 2, 128).transpose(0, 4, 1, 2, 3))
    vv = np.stack([inputs["rw_k_k"], inputs["rw_k_a"], inputs["rw_r_k"].reshape(DEPTH, 256)], axis=-1)
    m["rw_vecs"] = np.ascontiguousarray(vv.reshape(DEPTH, 2, 128, 3).transpose(0, 2, 1, 3))
    m["rw_ln"] = np.ascontiguousarray(np.stack([inputs["rw_ln_w"], inputs["rw_ln_b"]], axis=1))
    i_ = np.arange(64)
    S_f = (i_[:, None] < i_[None, :]).astype(np.float32)
    I_f = (i_[:, None] <= i_[None, :]).astype(np.float32)
    mk = np.zeros((128, 2, 2, 128), np.float32)
    for dd, (S_, I_) in enumerate(((S_f, I_f), (S_f.T, I_f.T))):
        for hb in range(2):
            mk[hb * 64:(hb + 1) * 64, dd, 0, hb * 64:(hb + 1) * 64] = S_
            mk[hb * 64:(hb + 1) * 64, dd, 1, hb * 64:(hb + 1) * 64] = I_
    m["rw_masks"] = mk
    bo2 = np.zeros((128, 2), np.float32)
    bo2[:64, 0] = 1.0
    bo2[64:, 1] = 1.0
    m["c_bdo2"] = bo2
    m["adab"] = np.ascontiguousarray(inputs["ada_b"].reshape(DEPTH, 72, 128).transpose(0, 2, 1))
    nr = np.stack([inputs["norm_ffn1"], inputs["norm_mix"], inputs["norm_ffn2"]], axis=1)
    m["norms"] = np.ascontiguousarray(nr.reshape(DEPTH, 3, KC, 128).transpose(0, 1, 3, 2))
    m["ada_w"] = inputs["ada_w"]
    for nm in ("ffn1_gate", "ffn1_up", "ffn1_down", "ffn2_gate", "ffn2_up", "ffn2_down", "w_out", "mla_w_uq", "mla_w_ukv"):
        m[nm] = inputs[nm]
    w_in = inputs["w_in"].copy()
    q0 = HY_COLS
    qc = inputs["w_in"][:, :, q0:q0 + 256].reshape(DEPTH, D, 4, 64)
    w_in[:, :, q0:q0 + 256] = qc[:, :, [0, 2, 1, 3], :].reshape(DEPTH, D, 256)
    m["w_in"] = w_in
    m.update(rope_consts())
    m["gqa_gain"] = np.ascontiguousarray(np.stack([np.tile(inputs["gqa_q_norm"], (1, 2)), np.tile(inputs["gqa_k_norm"], (1, 2))], axis=-1))
    mg = np.zeros((DEPTH, 128, 5), np.float32)
    mg[:, :, 0] = inputs["mla_cq_norm"][:, 0:128]
    mg[:, :, 1] = inputs["mla_cq_norm"][:, 128:256]
    mg[:, :, 2] = inputs["mla_ckv_norm"]
    mg[:, 0:96, 3] = inputs["mla_q_norm"]
    mg[:, 0:96, 4] = inputs["mla_k_norm"]
    m["mla_gain"] = mg
    return m


def build(dbg=False, stop=None):
    if stop == "rwa_only":
        k = K(dbg=dbg, pT_in=True)
        k.phase_rwkv_a(0)
        return k, k.P.finish()
    k = K(dbg=dbg)
    k.phase_mod()
    if stop in ("proj", "rw", "hy", "attn"):
        k.phase_ffn(0, 1, k.xT_in, k.xs)
        k.phase_proj(0, k.xs)
        if stop == "rw":
            k.phase_rwkv_a(0)
            if not os.environ.get("RWA_ONLY"):
                k.phase_rwkv_bc(0, True)
        if stop == "hy":
            k.phase_hyena(0, SEQ, CTX)
            k.phase_hyena(0, CTX, 0)
        if stop == "attn":
            k.phase_gqa(0, True)
            k.phase_mla(0, True)
        return k, k.P.finish()
    lat_tiles = [tl for tl in TILES if tl[0] >= CTX]
    for l in range(DEPTH):
        last = (l == DEPTH - 1)
        ctx_out = not last
        k.phase_ffn(l, 1, k.xT_in if l == 0 else k.xs, k.xs)
        k.phase_proj(l, k.xs)
        k.phase_hyena(l, SEQ, CTX)
        if ctx_out:
            k.phase_hyena(l, CTX, 0)
        k.phase_rwkv_a(l)
        k.phase_gqa(l, ctx_out, co_rwkv=True)
        k.phase_mla(l, ctx_out)
        k.phase_out(l, k.xs, k.xs, tiles=None if ctx_out else lat_tiles)
        if last:
            k.phase_ffn(l, 2, k.xs, k.xs, dst_lat=k.out, tiles=lat_tiles)
        else:
            k.phase_ffn(l, 2, k.xs, k.xs)
    nc = k.P.finish()
    return k, nc


def kernel(**inputs):
    inputs = {k_: np.asarray(v) for k_, v in inputs.items()}
    k, nc = build()
    shared = host_shared(inputs)
    in_maps = []
    for b in range(8):
        m = dict(shared)
        m.update(host_layout(inputs, b))
        in_maps.append(m)
    res = run_bass_kernel_spmd(nc, in_maps, core_ids=list(range(8)))
    out = np.stack([np.ascontiguousarray(r["outT"].T) for r in res.results], axis=0)
    return out.astype(np.float32)
```
